# Optimizing a Trainium2 kernel written in Bass

```python
import functools
import jax, jax.numpy as jnp
from jax import lax
import numpy as np

D_MODEL = 1024
BATCH = 32
SEQ = 256
DEPTH = 4
DEC_BATCH = 8
DEC_SEQ = 4096
PAST_LEN = 512

GRID_W = 64
N_MIXERS = 4
HEAD_DIM = 64
N_HEADS = D_MODEL // HEAD_DIM
N_KV_HEADS = N_HEADS // 4
GQA_GROUP = N_HEADS // N_KV_HEADS
Q_WIDTH = N_HEADS * HEAD_DIM
KV_WIDTH = N_KV_HEADS * HEAD_DIM
BLOCK = 128
SWA_WINDOW = 128
NA_KH = 8
NA_KW = 16
NA_QCB = 16
NA_SLAB = 32
RET_HEADS = 4
RET_DK = D_MODEL // RET_HEADS
RET_DV = 2 * D_MODEL // RET_HEADS
RET_CHUNK = 128
D_FF = 4 * D_MODEL
ROPE_BASE = 10000.0
LN_EPS = 1e-5
RMS_EPS = 1e-6
NEG_INF = -1e30
DEEPNORM_ALPHA = (2 * DEPTH) ** 0.25
DEEPNORM_BETA = (8 * DEPTH) ** -0.25
N_SWA = len(range(0, DEPTH, N_MIXERS))
N_NA = len(range(1, DEPTH, N_MIXERS))
N_RET = len(range(2, DEPTH, N_MIXERS))
N_GQA = len(range(3, DEPTH, N_MIXERS))
F32 = jnp.float32

kernel_name = 'hybrid_diffusion_prefix_trunk_step'


def layer_norm(x, g, b):
    xf = x.astype(F32)
    mu = jnp.mean(xf, -1, keepdims=True)
    var = jnp.mean(jnp.square(xf - mu), -1, keepdims=True)
    return ((xf - mu) * lax.rsqrt(var + LN_EPS)).astype(x.dtype) * g + b


def rms_norm(x, g):
    xf = x.astype(F32)
    return (xf * lax.rsqrt(jnp.mean(jnp.square(xf), -1, keepdims=True) + RMS_EPS)).astype(x.dtype) * g


def modulate(x, shift, scale):
    return x * (1 + scale) + shift


def sq_relu_mlp(h, w1, w2):
    return jnp.square(jax.nn.relu(h @ w1)) @ w2


def rope_1d(x, pos):
    half = x.shape[-1] // 2
    inv = ROPE_BASE ** (-jnp.arange(half, dtype=F32) / half)
    ang = pos.astype(F32)[:, None] * inv[None, :]
    bshape = (pos.shape[0],) + (1,) * (x.ndim - 3) + (half,)
    cos = jnp.cos(ang).reshape(bshape).astype(x.dtype)
    sin = jnp.sin(ang).reshape(bshape).astype(x.dtype)
    x1, x2 = x[..., :half], x[..., half:]
    return jnp.concatenate([x1 * cos - x2 * sin, x2 * cos + x1 * sin], axis=-1)


def rope_2d(x):
    t = jnp.arange(x.shape[1])
    half = x.shape[-1] // 2
    return jnp.concatenate([rope_1d(x[..., :half], t // GRID_W), rope_1d(x[..., half:], t % GRID_W)], axis=-1)


def joint_softmax(scores, sink=None):
    m = functools.reduce(jnp.maximum, [jnp.max(s, -1, keepdims=True) for s in scores])
    if sink is not None:
        m = jnp.maximum(m, sink)
    ps = [jnp.exp(s - m) for s in scores]
    den = functools.reduce(jnp.add, [jnp.sum(p, -1, keepdims=True) for p in ps])
    if sink is not None:
        den = den + jnp.exp(sink - m)
    return [p / den for p in ps]


def split_gqa(proj):
    B, L, _ = proj.shape
    q, k, v = jnp.split(proj, [Q_WIDTH, Q_WIDTH + KV_WIDTH], axis=-1)
    return (q.reshape(B, L, N_KV_HEADS, GQA_GROUP, HEAD_DIM),
            k.reshape(B, L, N_KV_HEADS, HEAD_DIM),
            v.reshape(B, L, N_KV_HEADS, HEAD_DIM))


def split_mha(proj):
    B, L, _ = proj.shape
    q, k, v = jnp.split(proj, 3, axis=-1)
    return (q.reshape(B, L, N_HEADS, HEAD_DIM), k.reshape(B, L, N_HEADS, HEAD_DIM),
            v.reshape(B, L, N_HEADS, HEAD_DIM))


def dense_attention(q, k, v, ctx_k=None, ctx_v=None, sink=None):
    B, L = q.shape[:2]
    nb = L // BLOCK
    scale = q.shape[-1] ** -0.5
    qb = q.reshape(B, nb, BLOCK, *q.shape[2:]).swapaxes(0, 1)

    def block(qi):
        scores = [jnp.einsum('bqkgd,blkd->bkgql', qi, k).astype(F32) * scale]
        if ctx_k is not None:
            scores.append(jnp.einsum('bqkgd,blkd->bkgql', qi, ctx_k).astype(F32) * scale)
        probs = joint_softmax(scores, sink)
        o = jnp.einsum('bkgql,blkd->bqkgd', probs[0].astype(v.dtype), v)
        if ctx_k is not None:
            o = o + jnp.einsum('bkgql,blkd->bqkgd', probs[1].astype(ctx_v.dtype), ctx_v)
        return o

    o = lax.map(block, qb)
    return o.swapaxes(0, 1).reshape(B, L, -1)


def swa_latent_attention(q, k, v, ctx_k, ctx_v, sink):
    B, L = q.shape[:2]
    nb = L // BLOCK
    scale = q.shape[-1] ** -0.5
    pad = ((0, 0), (BLOCK, BLOCK), (0, 0), (0, 0))
    kp, vp = jnp.pad(k, pad), jnp.pad(v, pad)
    qb = q.reshape(B, nb, BLOCK, *q.shape[2:]).swapaxes(0, 1)

    def band_block(args):
        i, qi = args
        ki = lax.dynamic_slice_in_dim(kp, i * BLOCK, 3 * BLOCK, axis=1)
        vi = lax.dynamic_slice_in_dim(vp, i * BLOCK, 3 * BLOCK, axis=1)
        qpos = i * BLOCK + jnp.arange(BLOCK)
        kpos = (i - 1) * BLOCK + jnp.arange(3 * BLOCK)
        ok = ((jnp.abs(qpos[:, None] - kpos[None, :]) <= SWA_WINDOW)
              & (kpos >= 0)[None, :] & (kpos < L)[None, :])
        s_loc = jnp.where(ok, jnp.einsum('bqkgd,blkd->bkgql', qi, ki).astype(F32) * scale, NEG_INF)
        s_ctx = jnp.einsum('bqkgd,blkd->bkgql', qi, ctx_k).astype(F32) * scale
        p_loc, p_ctx = joint_softmax([s_loc, s_ctx], sink)
        return (jnp.einsum('bkgql,blkd->bqkgd', p_loc.astype(v.dtype), vi)
                + jnp.einsum('bkgql,blkd->bqkgd', p_ctx.astype(ctx_v.dtype), ctx_v))

    o = lax.map(band_block, (jnp.arange(nb), qb))
    return o.swapaxes(0, 1).reshape(B, L, -1)


def na_latent_attention(q, k, v, ctx_k, ctx_v, rpb):
    B, L, H, d = q.shape
    rows = L // GRID_W
    kh = min(NA_KH, rows)
    ncb = GRID_W // NA_QCB
    scale = d ** -0.5
    qcol = np.arange(GRID_W).reshape(ncb, NA_QCB)
    slab0 = np.clip(np.arange(ncb) * NA_QCB - NA_KW // 2, 0, GRID_W - NA_SLAB)
    kcol = slab0[:, None] + np.arange(NA_SLAB)[None, :]
    cstart = np.clip(qcol - NA_KW // 2, 0, GRID_W - NA_KW)
    col_ok = (kcol[:, None, :] >= cstart[:, :, None]) & (kcol[:, None, :] < cstart[:, :, None] + NA_KW)
    col_ok = np.broadcast_to(col_ok[:, :, None, :], (ncb, NA_QCB, kh, NA_SLAB)).reshape(ncb, NA_QCB, kh * NA_SLAB)
    dc = np.clip(kcol[:, None, :] - qcol[:, :, None] + NA_KW - 1, 0, 2 * NA_KW - 2)
    bias_col = rpb.astype(F32)[:, :, dc]
    kg = k.reshape(B, rows, GRID_W, H, d)
    vg = v.reshape(B, rows, GRID_W, H, d)
    qg = q.reshape(B, rows, ncb, NA_QCB, H, d).swapaxes(0, 1)

    def row_block(args):
        r, qr = args
        r0 = jnp.clip(r - kh // 2, 0, rows - kh)

        def gather(t):
            t = lax.dynamic_slice_in_dim(t, r0, kh, axis=1)[:, :, kcol]
            return t.transpose(0, 2, 1, 3, 4, 5).reshape(B, ncb, kh * NA_SLAB, H, d)

        kr, vr = gather(kg), gather(vg)
        dr = r0 + jnp.arange(kh) - r + NA_KH - 1
        bias = bias_col[:, dr].transpose(0, 2, 3, 1, 4).reshape(H, ncb, NA_QCB, kh * NA_SLAB)
        s_loc = jnp.einsum('bjqhd,bjnhd->bhjqn', qr, kr).astype(F32) * scale + bias[None]
        s_loc = jnp.where(col_ok[None, None], s_loc, NEG_INF)
        s_ctx = jnp.einsum('bjqhd,blhd->bhjql', qr, ctx_k).astype(F32) * scale
        p_loc, p_ctx = joint_softmax([s_loc, s_ctx])
        return (jnp.einsum('bhjqn,bjnhd->bjqhd', p_loc.astype(v.dtype), vr)
                + jnp.einsum('bhjql,blhd->bjqhd', p_ctx.astype(ctx_v.dtype), ctx_v))

    o = lax.map(row_block, (jnp.arange(rows), qg))
    return o.swapaxes(0, 1).reshape(B, L, H * d)


def retention_scan(q, k, v, log_gamma, s0):
    B, L, H, dk = q.shape
    dv = v.shape[-1]
    n = L // RET_CHUNK
    idx = jnp.arange(RET_CHUNK, dtype=F32)
    diff = idx[:, None] - idx[None, :]
    intra = jnp.where(diff >= 0, jnp.exp(jnp.maximum(diff, 0.0)[None] * log_gamma[:, None, None]), 0.0)
    q_dec = jnp.exp((idx + 1.0)[:, None] * log_gamma[None, :])[None, :, :, None]
    k_dec = jnp.exp((RET_CHUNK - 1.0 - idx)[:, None] * log_gamma[None, :])[None, :, :, None]
    chunk_dec = jnp.exp(RET_CHUNK * log_gamma)[None, :, None, None]

    def step(S, xs):
        qc, kc, vc = xs
        s = jnp.einsum('bihd,bjhd->bhij', qc, kc) * intra[None]
        o = jnp.einsum('bhij,bjhe->bihe', s, vc) + jnp.einsum('bihd,bhde->bihe', qc, S) * q_dec
        S = S * chunk_dec + jnp.einsum('bjhd,bjhe->bhde', kc * k_dec, vc)
        return S, o

    xs = tuple(t.reshape(B, n, RET_CHUNK, H, t.shape[-1]).swapaxes(0, 1) for t in (q, k, v))
    s_final, o = lax.scan(step, s0, xs)
    return o.swapaxes(0, 1).reshape(B, L, H, dv), s_final


def head_norm(o):
    mu = jnp.mean(o, -1, keepdims=True)
    var = jnp.mean(jnp.square(o - mu), -1, keepdims=True)
    o = (o - mu) * lax.rsqrt(var + LN_EPS)
    return o.reshape(o.shape[0], o.shape[1], -1)


def retention_mixer(h, w, decay_logit, gn_g, s0, rotate):
    B, L, _ = h.shape
    hk, hv = RET_HEADS * RET_DK, RET_HEADS * RET_DV
    q, k, v, g_f, g_b = jnp.split(h @ w, [hk, 2 * hk, 2 * hk + hv, 2 * hk + 2 * hv], axis=-1)
    q = q.reshape(B, L, RET_HEADS, RET_DK)
    k = k.reshape(B, L, RET_HEADS, RET_DK) * (RET_DK ** -0.5)
    v = v.reshape(B, L, RET_HEADS, RET_DV)
    if rotate:
        q, k = rope_2d(q), rope_2d(k)
    q, k, v = q.astype(F32), k.astype(F32), v.astype(F32)
    log_gamma = jax.nn.log_sigmoid(decay_logit.astype(F32))
    s0 = s0.astype(F32)
    o_f, s_f = retention_scan(q, k, v, log_gamma[0], s0[:, 0])
    o_b, s_b = retention_scan(q[:, ::-1], k[:, ::-1], v[:, ::-1], log_gamma[1], s0[:, 1])
    gn_g = gn_g.astype(F32)
    y = (head_norm(o_f) * gn_g[0] * jax.nn.silu(g_f.astype(F32))
         + head_norm(o_b[:, ::-1]) * gn_g[1] * jax.nn.silu(g_b.astype(F32)))
    return y.astype(h.dtype), jnp.stack([s_f, s_b], axis=1)


def swa_context(h, wqkv, wo, sink):
    q, k, v = split_gqa(h @ wqkv)
    return dense_attention(q, k, v, sink=sink) @ wo, k, v


def swa_latent(h, wqkv, wo, sink, ck, cv):
    q, k, v = split_gqa(h @ wqkv)
    return swa_latent_attention(rope_2d(q), rope_2d(k), v, ck, cv, sink) @ wo


def na_context(h, wqkv, wo):
    q, k, v = split_mha(h @ wqkv)
    return dense_attention(q[:, :, :, None], k, v) @ wo, k, v


def na_latent(h, wqkv, wo, rpb, ck, cv):
    q, k, v = split_mha(h @ wqkv)
    return na_latent_attention(q, k, v, ck, cv, rpb) @ wo


def ret_context(h, w, wo, decay_logit, gn_g):
    s0 = jnp.zeros((h.shape[0], 2, RET_HEADS, RET_DK, RET_DV), F32)
    y, s = retention_mixer(h, w, decay_logit, gn_g, s0, rotate=False)
    return y @ wo, s


def ret_latent(h, w, wo, decay_logit, gn_g, s0):
    y, _ = retention_mixer(h, w, decay_logit, gn_g, s0, rotate=True)
    return y @ wo


def gqa_qkv(h, wqkv, qn, kn):
    q, k, v = split_gqa(h @ wqkv)
    return rms_norm(q, qn), rms_norm(k, kn), v


def gqa_context(h, wqkv, wo, qn, kn):
    q, k, v = gqa_qkv(h, wqkv, qn, kn)
    return dense_attention(q, k, v) @ wo, k, v


def gqa_latent(h, wqkv, wo, qn, kn, ck, cv):
    q, k, v = gqa_qkv(h, wqkv, qn, kn)
    return dense_attention(rope_2d(q), rope_2d(k), v, ck, cv) @ wo


def setup_inputs(seed: int = 0) -> dict:
    key = jax.random.key(seed)
    ks = iter(jax.random.split(key, 64))

    def nrm(shape, scale):
        return jax.random.normal(next(ks), shape, F32) * scale

    D = D_MODEL
    din = D ** -0.5
    beta = DEEPNORM_BETA
    base = 1.0 - 2.0 ** (-5.0 - np.arange(RET_HEADS))
    decay_logit = jnp.asarray(np.log(base / (1.0 - base)).astype(np.float32))
    ret_cols = 2 * RET_HEADS * RET_DK + 3 * RET_HEADS * RET_DV
    return {
        'x_prompt': nrm((BATCH, SEQ, D), 1.0),
        'x_sample': nrm((DEC_BATCH, DEC_SEQ, D), 1.0),
        'cache_swa_k': nrm((DEC_BATCH, N_SWA, PAST_LEN, N_KV_HEADS, HEAD_DIM), 1.0),
        'cache_swa_v': nrm((DEC_BATCH, N_SWA, PAST_LEN, N_KV_HEADS, HEAD_DIM), 1.0),
        'cache_na_k': nrm((DEC_BATCH, N_NA, PAST_LEN, N_HEADS, HEAD_DIM), 1.0),
        'cache_na_v': nrm((DEC_BATCH, N_NA, PAST_LEN, N_HEADS, HEAD_DIM), 1.0),
        'state_ret': nrm((DEC_BATCH, N_RET, 2, RET_HEADS, RET_DK, RET_DV), 0.3),
        'cache_gqa_k': nrm((DEC_BATCH, N_GQA, PAST_LEN, N_KV_HEADS, HEAD_DIM), 1.0),
        'cache_gqa_v': nrm((DEC_BATCH, N_GQA, PAST_LEN, N_KV_HEADS, HEAD_DIM), 1.0),
        'c': nrm((DEC_BATCH, D), 1.0),
        'c_ctx': nrm((D,), 1.0),
        'mod_w': nrm((DEPTH, D, 6 * D), 0.5 * din),
        'mod_b': nrm((DEPTH, 6 * D), 0.02),
        'ln_g': 1.0 + nrm((DEPTH, 2, D), 0.02),
        'ln_b': nrm((DEPTH, 2, D), 0.02),
        'mlp_w1': nrm((DEPTH, D, D_FF), din),
        'mlp_w2': nrm((DEPTH, D_FF, D), beta * D_FF ** -0.5),
        'swa_wqkv': nrm((N_SWA, D, Q_WIDTH + 2 * KV_WIDTH), din),
        'swa_wo': nrm((N_SWA, Q_WIDTH, D), beta * Q_WIDTH ** -0.5),
        'swa_sink': nrm((N_SWA, N_HEADS), 1.0),
        'na_wqkv': nrm((N_NA, D, 3 * Q_WIDTH), din),
        'na_wo': nrm((N_NA, Q_WIDTH, D), beta * Q_WIDTH ** -0.5),
        'na_rpb': nrm((N_NA, N_HEADS, 2 * NA_KH - 1, 2 * NA_KW - 1), 0.1),
        'ret_wqkvg': nrm((N_RET, D, ret_cols), din),
        'ret_wo': nrm((N_RET, RET_HEADS * RET_DV, D), beta * (RET_HEADS * RET_DV) ** -0.5),
        'ret_decay': jnp.broadcast_to(decay_logit, (N_RET, 2, RET_HEADS)) + nrm((N_RET, 2, RET_HEADS), 0.05),
        'ret_gn_g': 1.0 + nrm((N_RET, 2, RET_HEADS * RET_DV), 0.02),
        'gqa_wqkv': nrm((N_GQA, D, Q_WIDTH + 2 * KV_WIDTH), din),
        'gqa_q_norm': 1.0 + nrm((N_GQA, HEAD_DIM), 0.02),
        'gqa_k_norm': 1.0 + nrm((N_GQA, HEAD_DIM), 0.02),
        'gqa_wo': nrm((N_GQA, Q_WIDTH, D), beta * Q_WIDTH ** -0.5),
    }


def reference(x_prompt, x_sample, cache_swa_k, cache_swa_v, cache_na_k, cache_na_v, state_ret,
              cache_gqa_k, cache_gqa_v, c, c_ctx, mod_w, mod_b, ln_g, ln_b, mlp_w1, mlp_w2,
              swa_wqkv, swa_wo, swa_sink, na_wqkv, na_wo, na_rpb,
              ret_wqkvg, ret_wo, ret_decay, ret_gn_g,
              gqa_wqkv, gqa_q_norm, gqa_k_norm, gqa_wo):
    alpha = DEEPNORM_ALPHA
    mod_p = jnp.einsum('d,lde->le', jax.nn.silu(c_ctx), mod_w) + mod_b
    mod_s = jnp.einsum('bd,lde->lbe', jax.nn.silu(c), mod_w) + mod_b[:, None, :]
    xp, xs = x_prompt, x_sample
    swa_k, swa_v, na_k, na_v, ret_s, gqa_k, gqa_v = [], [], [], [], [], [], []
    for i in range(DEPTH):
        m, j = i % N_MIXERS, i // N_MIXERS
        p_sh1, p_sc1, p_g1, p_sh2, p_sc2, p_g2 = jnp.split(mod_p[i], 6, axis=-1)
        s_sh1, s_sc1, s_g1, s_sh2, s_sc2, s_g2 = jnp.split(mod_s[i][:, None, :], 6, axis=-1)
        hp, hs = modulate(xp, p_sh1, p_sc1), modulate(xs, s_sh1, s_sc1)
        if m == 0:
            sink = swa_sink[j].astype(F32).reshape(1, N_KV_HEADS, GQA_GROUP, 1, 1)
            yp, kc, vc = swa_context(hp, swa_wqkv[j], swa_wo[j], sink)
            ys = swa_latent(hs, swa_wqkv[j], swa_wo[j], sink, cache_swa_k[:, j], cache_swa_v[:, j])
            swa_k.append(kc)
            swa_v.append(vc)
        elif m == 1:
            yp, kc, vc = na_context(hp, na_wqkv[j], na_wo[j])
            ys = na_latent(hs, na_wqkv[j], na_wo[j], na_rpb[j], cache_na_k[:, j], cache_na_v[:, j])
            na_k.append(kc)
            na_v.append(vc)
        elif m == 2:
            yp, sc = ret_context(hp, ret_wqkvg[j], ret_wo[j], ret_decay[j], ret_gn_g[j])
            ys = ret_latent(hs, ret_wqkvg[j], ret_wo[j], ret_decay[j], ret_gn_g[j], state_ret[:, j])
            ret_s.append(sc)
        else:
            yp, kc, vc = gqa_context(hp, gqa_wqkv[j], gqa_wo[j], gqa_q_norm[j], gqa_k_norm[j])
            ys = gqa_latent(hs, gqa_wqkv[j], gqa_wo[j], gqa_q_norm[j], gqa_k_norm[j],
                            cache_gqa_k[:, j], cache_gqa_v[:, j])
            gqa_k.append(kc)
            gqa_v.append(vc)
        xp = layer_norm(alpha * xp + p_g1 * yp, ln_g[i, 0], ln_b[i, 0])
        xs = layer_norm(alpha * xs + s_g1 * ys, ln_g[i, 0], ln_b[i, 0])
        xp = layer_norm(alpha * xp + p_g2 * sq_relu_mlp(modulate(xp, p_sh2, p_sc2), mlp_w1[i], mlp_w2[i]),
                        ln_g[i, 1], ln_b[i, 1])
        xs = layer_norm(alpha * xs + s_g2 * sq_relu_mlp(modulate(xs, s_sh2, s_sc2), mlp_w1[i], mlp_w2[i]),
                        ln_g[i, 1], ln_b[i, 1])
    return (xp, xs, jnp.stack(swa_k, axis=1), jnp.stack(swa_v, axis=1), jnp.stack(na_k, axis=1),
            jnp.stack(na_v, axis=1), jnp.stack(ret_s, axis=1), jnp.stack(gqa_k, axis=1), jnp.stack(gqa_v, axis=1))
```

```python
import os
import numpy as np
from contextlib import ExitStack
import concourse.bass as bass
import concourse.mybir as mybir
from concourse.bass_utils import run_bass_kernel_spmd

F32 = mybir.dt.float32
BF16 = mybir.dt.bfloat16
ALU = mybir.AluOpType
AF = mybir.ActivationFunctionType

ENGS = ("pe", "act", "dve", "pool", "sp")
NDSEM = 24
ALPHA = 8.0 ** 0.25
LN_EPS = 1e-5
RMS_EPS = 1e-6
G = 256
NPG = 4
NSG = 16
NTOK = 5120


class Reg:
    __slots__ = ("name", "w", "rs", "drs")

    def __init__(self, name=""):
        self.name = name
        self.w = None
        self.rs = {}
        self.drs = []


class Ins:
    __slots__ = ("eng", "fn", "deps", "dma", "marked", "semval", "dsem", "dval", "emitted")

    def __init__(self, eng, fn, dma):
        self.eng = eng
        self.fn = fn
        self.dma = dma
        self.deps = []
        self.marked = False
        self.semval = 0
        self.dsem = None
        self.dval = 0
        self.emitted = False


class Prog:
    def __init__(self, nc, stack):
        self.nc = nc
        self.lists = {e: [] for e in ENGS}
        self.sem = {e: stack.enter_context(nc.semaphore("s_" + e)) for e in ENGS if e != "sp"}
        self.count = {e: 0 for e in ENGS}
        self.dsems = {q: [stack.enter_context(nc.semaphore("d_%s%d" % (q, i))) for i in range(NDSEM)]
                      for q in ("sp", "pool")}
        self.dcount = {"sp": 0, "pool": 0}
        self.dhist = {"sp": [], "pool": []}
        self.waited = {e: {} for e in ENGS}
        self.last_marked = {e: None for e in ENGS}
        self.n_ins = 0
        self.n_wait = 0

    def _add(self, eng, fn, reads, writes, dma):
        ins = Ins(eng, fn, dma)
        deps = {}

        def dep(d):
            if d is None or d is ins:
                return
            if (not d.dma) and (not dma) and d.eng == eng and eng == "pe":
                return
            if (not d.dma) and d.emitted and not d.marked:
                d = self.last_marked[d.eng]
                if d is None:
                    return
            deps[id(d)] = d

        for r in reads:
            dep(r.w)
        for w in writes:
            dep(w.w)
            for d in w.rs.values():
                dep(d)
            for d in w.drs:
                dep(d)
        if dma:
            q = eng
            j = self.dcount[q]
            self.dcount[q] += 1
            ins.dsem = self.dsems[q][j % NDSEM]
            ins.dval = 16 * (j // NDSEM + 1)
            if j >= NDSEM:
                dep(self.dhist[q][j - NDSEM])
            self.dhist[q].append(ins)
        for d in deps.values():
            if not d.dma:
                d.marked = True
        ins.deps = list(deps.values())
        for r in reads:
            if dma:
                r.drs.append(ins)
            else:
                r.rs[eng] = ins
        for w in writes:
            w.w = ins
            w.rs = {}
            w.drs = []
        self.lists[eng].append(ins)
        self.n_ins += 1
        return ins

    def op(self, eng, fn, reads=(), writes=()):
        return self._add(eng, fn, reads, writes, False)

    def dma(self, q, fn, reads=(), writes=()):
        return self._add(q, fn, reads, writes, True)

    def flush(self, final_waits=()):
        nc = self.nc
        for e in ENGS:
            lst = [i for i in self.lists[e] if not i.dma]
            if lst:
                lst[-1].marked = True
        for e in ENGS:
            for i in self.lists[e]:
                if i.marked and not i.dma:
                    self.count[e] += 1
                    i.semval = self.count[e]
        lists = self.lists
        self.lists = {e: [] for e in ENGS}
        handles = {"pe": "tensor", "act": "scalar", "dve": "vector", "pool": "gpsimd", "sp": "sync"}
        with nc.Block() as block:
            for e in ENGS:
                def body(eng, e=e):
                    wt = self.waited[e]
                    for ins in lists[e]:
                        for d in ins.deps:
                            if d.dma:
                                key = id(d.dsem)
                                if wt.get(key, 0) < d.dval:
                                    eng.wait_ge(d.dsem, d.dval)
                                    wt[key] = d.dval
                                    self.n_wait += 1
                            else:
                                if wt.get(d.eng, 0) < d.semval:
                                    eng.wait_ge(self.sem[d.eng], d.semval)
                                    wt[d.eng] = d.semval
                                    self.n_wait += 1
                        bi = ins.fn(eng)
                        if ins.dma:
                            bi.then_inc(ins.dsem, 16)
                        elif ins.marked:
                            bi.then_inc(self.sem[e], 1)
                            self.last_marked[e] = ins
                        ins.emitted = True
                    if e == "sp":
                        for q in ("sp", "pool"):
                            for d in self.dhist[q][-NDSEM:]:
                                key = id(d.dsem)
                                if wt.get(key, 0) < d.dval:
                                    eng.wait_ge(d.dsem, d.dval)
                                    wt[key] = d.dval
                        for d in final_waits:
                            key = id(d.dsem)
                            if wt.get(key, 0) < d.dval:
                                eng.wait_ge(d.dsem, d.dval)
                                wt[key] = d.dval
                getattr(block, handles[e])(body)


def _host_consts():
    c = {}
    t = np.arange(4096)
    row, col = (t // 64).astype(np.float64), (t % 64).astype(np.float64)
    p = np.arange(128) % 64
    inv16 = 10000.0 ** (-(np.arange(16)) / 16.0)
    pos = np.where((p < 32)[:, None], row[None, :], col[None, :])
    ang = pos * inv16[p % 16][:, None]
    c["rope64"] = np.stack([np.cos(ang), np.sin(ang)]).astype(np.float32)
    inv64 = 10000.0 ** (-(np.arange(64)) / 64.0)
    p2 = np.arange(128) % 64
    a0 = row[None, :] * inv64[p2][:, None]
    a1 = col[None, :] * inv64[p2][:, None]
    c["rope256"] = np.stack([np.stack([np.cos(a0), np.cos(a1)]), np.stack([np.sin(a0), np.sin(a1)])]).astype(np.float32)
    j = np.arange(128)[:, None].astype(np.float64)
    i = np.arange(128)[None, :].astype(np.float64)
    ret = np.zeros((128, 5, 128), np.float32)
    ret[:, 0] = i - j
    ret[:, 1] = (i >= j) / 16.0
    ret[:, 2] = (j >= i) / 16.0
    ret[:, 3] = np.broadcast_to(i + 1.0, (128, 128))
    ret[:, 4] = np.broadcast_to(128.0 - i, (128, 128))
    c["retc"] = ret.reshape(128, 640)
    rc = np.zeros((128, 2), np.float32)
    rc[:, 0] = 127.0 - np.arange(128)
    rc[:, 1] = np.arange(128)
    c["retcol"] = rc
    kk = np.arange(128)[:, None]
    qq = np.arange(128)[None, :]
    mp = np.where(kk >= qq, 0.0, -1e30)
    mn = np.where(kk <= qq, 0.0, -1e30)
    c["swamask"] = np.stack([np.tile(mp, (1, 4)), np.tile(mn, (1, 4))]).astype(np.float32)
    E = np.zeros((31, 64, 2, 64), np.float32)
    for qc in range(64):
        for kc in range(64):
            jj = kc - qc + 15
            if 0 <= jj <= 30:
                E[jj, qc, :, kc] = 1.0
    c["naE"] = E.reshape(31, 8192)
    cs = np.clip(np.arange(64) - 8, 0, 48)
    kc = np.arange(64)[:, None]
    ok = (kc >= cs[None, :]) & (kc < cs[None, :] + 16)
    c["namask"] = np.tile(np.where(ok, 0.0, -1e30), (2, 1)).astype(np.float32)
    return c


class KB:
    def __init__(self, nsub=8):
        self.nsub = nsub
        self.nc = bass.Bass("TRN2", target_bir_lowering=False)
        self.top = ExitStack()
        self.cur = self.top
        self.P = Prog(self.nc, self.top)
        self.uid = 0
        self.D = {}
        self.fin = []
        self.in_names = []
        self.out_names = []

    def din(self, n, shape, dt=F32):
        self.D[n] = self.nc.dram_tensor(n, list(shape), dt, kind="ExternalInput").ap()
        self.in_names.append(n)
        return self.D[n]

    def dout(self, n, shape):
        self.D[n] = self.nc.dram_tensor(n, list(shape), F32, kind="ExternalOutput").ap()
        self.out_names.append(n)
        return self.D[n]

    def dscr(self, n, shape, dt=F32):
        self.D[n] = self.nc.dram_tensor(n, list(shape), dt, kind="Internal").ap()
        return self.D[n]

    def sb(self, shape, dt=F32, name="t"):
        self.uid += 1
        return self.cur.enter_context(self.nc.sbuf_tensor("%s_%d" % (name, self.uid), list(shape), dt))

    def phase(self):
        kb = self

        class _Ph:
            def __enter__(s):
                s.st = ExitStack()
                s.prev = kb.cur
                kb.cur = s.st
                return s

            def __exit__(s, et, ev, tb):
                if et is None:
                    kb.P.flush()
                kb.cur = s.prev
                s.st.close()
                return False
        return _Ph()

    def init_psum(self):
        self.banks = [self.top.enter_context(self.nc.psum_tensor("psb%d" % i, [128, 512], F32)) for i in range(8)]
        self.bregs = [[Reg("b%d" % i)] for i in range(8)]
        self.full_ctr = 0
        self.ntrans = 7
        self.half_ctr = 0

    def ps_full(self):
        b = self.full_ctr % self.ntrans
        self.full_ctr += 1
        return self.banks[b], self.bregs[b]

    def ps_half(self):
        s = self.half_ctr % (2 * self.ntrans)
        self.half_ctr += 1
        b, h = s % self.ntrans, s // self.ntrans
        return self.banks[b][:, h * 256:(h + 1) * 256], self.bregs[b]

    def xgrp(self, g):
        return g * G

    def load_x(self, g, dst, reg):
        xT = self.D["xT_d"]
        t0 = g * G
        return self.P.dma("sp", lambda e: e.dma_start(out=dst, in_=xT[:, :, t0:t0 + G].rearrange("c p t -> p c t")), [self.xreg[g]], [reg])

    def store_x(self, g, src, reg):
        xT = self.D["xT_d"]
        t0 = g * G
        return self.P.dma("sp", lambda e: e.dma_start(out=xT[:, :, t0:t0 + G].rearrange("c p t -> p c t"), in_=src), [reg], [self.xreg[g]])

    def modv(self, g, l, j, c):
        grp = 0 if g < NPG else 1
        return self.modT[:, grp, l, j * 8 + c:j * 8 + c + 1]

    def modp1(self, g, l, which, c):
        grp = 0 if g < NPG else 1
        return self.modP1[:, grp, l, which, c:c + 1]

    def modulate(self, g, l, which, xt, rx, hT, rh):
        P = self.P
        for c in range(8):
            sc = self.modp1(g, l, which, c)
            sh = self.modv(g, l, 3 * which, c)
            P.op("dve", lambda e, c=c, sc=sc, sh=sh: e.tensor_scalar(hT[:, c, :], xt[:, c, :], sc, sh, ALU.mult, ALU.add),
                 [rx, self.rmod], [rh])

    def ln_store(self, g, l, i, zz, rz0, rz1):
        P = self.P
        nc = self.nc
        st = self.lnst
        self.ln_p1(zz, rz0, rz1)
        self.ln_p2(g, l, i, zz, rz0, rz1)

    def ln_p1(self, zz, rz0, rz1):
        P = self.P
        red, rred = self.lnst["red"], self.lnst["rred"]
        P.op("act", lambda e: e.activation(zz[:, 1], zz[:, 0], AF.Square), [rz0], [rz1])
        P.op("dve", lambda e: e.tensor_reduce(red[:], zz[:].rearrange("p a c t -> p a t c"), mybir.AxisListType.X, ALU.add), [rz0, rz1], [rred])

    def ln_p2(self, g, l, i, zz, rz0, rz1):
        P = self.P
        st = self.lnst
        bank, br = self.ps_full()
        red, rred = st["red"], st["rred"]
        P.op("pe", lambda e: e.matmul(bank[:, :].rearrange("p (a t) -> p a t", a=2), self.ones32[:, :], red[:], start=True, stop=True), [rred, self.rconst], br)
        m, msq, var, rstd = st["m"], st["msq"], st["var"], st["rstd"]
        rm, rv, rr = st["rm"], st["rv"], st["rr"]
        P.op("act", lambda e: e.mul(m[:], bank[:, 0:256], 1.0 / 1024.0), [], [rm] + br)
        P.op("dve", lambda e: e.tensor_tensor(msq[:], m[:], m[:], ALU.mult), [rm], [rv])
        P.op("dve", lambda e: e.scalar_tensor_tensor(var[:], bank[:, 256:512], 1.0 / 1024.0, msq[:], ALU.mult, ALU.subtract), [rv], [rv] + br)
        P.op("dve", lambda e: e.tensor_scalar(var[:], var[:], LN_EPS, None, ALU.add), [rv], [rv])
        P.op("dve", lambda e: e.reciprocal(var[:], var[:]), [rv], [rv])
        P.op("act", lambda e: e.activation(rstd[:], var[:], AF.Sqrt), [rv], [rr])
        mb = m[:, None, :].to_broadcast([128, 8, G])
        rb = rstd[:, None, :].to_broadcast([128, 8, G])
        P.op("dve", lambda e: e.tensor_tensor(zz[:, 0], zz[:, 0], mb, ALU.subtract), [rz0, rm], [rz0])
        P.op("dve", lambda e: e.tensor_tensor(zz[:, 0], zz[:, 0], rb, ALU.mult), [rz0, rr], [rz0])
        gb = self.lngT[:, l * 16 + i * 8:l * 16 + i * 8 + 8][:, :, None].to_broadcast([128, 8, G])
        bb = self.lnbT[:, l * 16 + i * 8:l * 16 + i * 8 + 8][:, :, None].to_broadcast([128, 8, G])
        P.op("pool", lambda e: e.tensor_tensor(zz[:, 0], zz[:, 0], gb, ALU.mult), [rz0, self.rmod], [rz0])
        P.op("pool", lambda e: e.tensor_tensor(zz[:, 1], zz[:, 0], bb, ALU.add), [rz0, self.rmod], [rz1])
        self.store_x(g, zz[:, 1], rz1)

    def alloc_ln(self):
        st = {}
        for n in ("m", "msq", "var", "rstd"):
            st[n] = self.sb([128, G], F32, "ln" + n)
        st["rm"], st["rv"], st["rr"] = Reg(), Reg(), Reg()
        st["red"] = self.sb([128, 2, G], F32, "lnred")
        st["rred"] = Reg()
        self.lnst = st

    def wload(self, dst, src, reg, reads=()):
        return self.P.dma("pool", lambda e: e.dma_start(out=dst, in_=src), list(reads), [reg])

    def declare(self):
        d = self.din
        d("xp", [1024, 1024]); d("xs", [4096, 1024])
        d("cswak", [512, 256]); d("cswav", [512, 256]); d("cnak", [512, 1024]); d("cnav", [512, 1024])
        d("sret", [2048, 512]); d("cgqak", [512, 256]); d("cgqav", [512, 256])
        d("cT_h", [128, 16]); d("lngT_h", [128, 64]); d("lnbT_h", [128, 64]); d("modbT_h", [128, 192]); d("gqn", [128, 4])
        d("mod_w", [4, 1024, 6144]); d("mlp_w1", [4, 1024, 4096]); d("mlp_w2", [4, 4096, 1024])
        d("swa_wqkv_p", [1024, 1536]); d("swa_wo", [1024, 1024]); d("swa_sink", [1, 16])
        d("na_wqkv", [1024, 3072]); d("na_wo", [1024, 1024]); d("na_rpbT", [31, 240])
        d("ret_w", [1024, 8192]); d("ret_wo", [2048, 1024]); d("ret_decay", [1, 8]); d("ret_gn", [2, 2048])
        d("gqa_wqkv_p", [1024, 1536]); d("gqa_wo", [1024, 1024])
        d("rope64", [2, 128, 4096]); d("rope256", [2, 2, 128, 4096]); d("retc", [128, 640]); d("retcol", [128, 2])
        d("swamask", [2, 128, 512]); d("naE", [31, 8192]); d("namask", [128, 64])
        o = self.dout
        o("y_p", [1024, 1024]); o("y_s", [4096, 1024])
        o("o_swak", [1024, 256]); o("o_swav", [1024, 256]); o("o_nak", [1024, 1024]); o("o_nav", [1024, 1024])
        o("o_ret", [8192, 512]); o("o_gqak", [1024, 256]); o("o_gqav", [1024, 256])
        s = self.dscr
        s("xT_d", [8, 128, NTOK]); s("mixA_d", [64, 16, NTOK], BF16); s("mixR_d", [128, 16, NTOK], BF16)
        s("Yd", [2, NTOK, 2048]); s("natab_d", [128, 14336], BF16)
        self.xreg = [Reg("xg%d" % g) for g in range(NPG + NSG)]
        self.mixreg = [Reg("mx%d" % g) for g in range(NPG + NSG)]
        self.ident = self.sb([128, 128], F32, "ident")
        self.identb = self.sb([128, 128], BF16, "identb")
        self.ones32 = self.sb([128, 128], F32, "ones32")
        self.blk32 = self.sb([128, 128], F32, "blk32")
        self.sel64 = self.sb([128, 128], F32, "sel64")
        self.cT = self.sb([128, 16], F32, "cT")
        self.lngT = self.sb([128, 64], F32, "lngT")
        self.lnbT = self.sb([128, 64], F32, "lnbT")
        self.modbT = self.sb([128, 192], F32, "modbT")
        self.gqn = self.sb([128, 4], F32, "gqn")
        self.modT = self.sb([128, 2, 4, 48], F32, "modT")
        self.modP1 = self.sb([128, 2, 4, 2, 8], F32, "modP1")
        self.rconst = Reg("const")
        self.rmod = Reg("mod")
        self.init_psum()

    def prep(self):
        P, nc, D = self.P, self.nc, self.D
        rc, rmod = self.rconst, self.rmod
        with self.phase():
            ident, identb, ones32, blk32 = self.ident, self.identb, self.ones32, self.blk32
            P.op("pool", lambda e: e.memset(ident[:], 0.0), [], [rc])
            P.op("pool", lambda e: e.affine_select(out=ident[:], in_=ident[:], pattern=[[-1, 128]], compare_op=ALU.not_equal,
                                                   fill=1.0, base=0, channel_multiplier=1), [rc], [rc])
            P.op("pool", lambda e: e.memset(ones32[:], 1.0), [], [rc])
            P.op("pool", lambda e: e.memset(blk32[:], 0.0), [], [rc])
            P.op("pool", lambda e: e.memset(blk32[0:64, 0:64], 1.0), [rc], [rc])
            P.op("pool", lambda e: e.memset(blk32[64:128, 64:128], 1.0), [rc], [rc])
            P.op("pool", lambda e: e.memset(self.sel64[:], 0.0), [], [rc])
            P.op("pool", lambda e: e.memset(self.sel64[64:65, :], 1.0), [rc], [rc])
            P.op("dve", lambda e: e.tensor_copy(identb[:], ident[:]), [rc], [rc])
            for dst, src in ((self.cT, "cT_h"), (self.lngT, "lngT_h"), (self.lnbT, "lnbT_h"), (self.modbT, "modbT_h"), (self.gqn, "gqn")):
                P.dma("sp", lambda e, dst=dst, src=src: e.dma_start(out=dst[:], in_=D[src][:, :]), [], [rmod])
            sil = self.sb([128, 8, 2], F32, "sil")
            rs = Reg()
            P.op("act", lambda e: e.activation(sil[:, :, 0], self.cT[:, 0:8], AF.Silu), [rmod], [rs])
            P.op("act", lambda e: e.activation(sil[:, :, 1], self.cT[:, 8:16], AF.Silu), [rmod, rs], [rs])
            mw = [self.sb([128, 8, 768], F32, "mw") for _ in range(2)]
            rmw = [Reg(), Reg()]
            pm, prm = self.banks[7], self.bregs[7]
            it = 0
            for l in range(4):
                for nb in range(8):
                    s = it % 2
                    it += 1
                    P.dma("sp", lambda e, l=l, nb=nb, s=s: e.dma_start(
                        out=mw[s][:], in_=D["mod_w"][l, :, nb * 768:(nb + 1) * 768].rearrange("(k p) n -> p k n", p=128)), [], [rmw[s]])
                    for n6 in range(6):
                        n = nb * 6 + n6
                        for kc in range(8):
                            P.op("pe", lambda e, s=s, n6=n6, kc=kc, l=l, n=n: e.matmul(
                                pm[:, l * 96 + n * 2:l * 96 + n * 2 + 2], mw[s][:, kc, n6 * 128:(n6 + 1) * 128], sil[:, kc, :],
                                start=(kc == 0), stop=(kc == 7)), [rmw[s], rs], prm)
            for grp in range(2):
                P.op("dve", lambda e, grp=grp: e.tensor_tensor(
                    self.modT[:, grp].rearrange("p l n -> p (l n)"),
                    pm[:, 0:384].rearrange("p (x g) -> p x g", g=2)[:, :, grp], self.modbT[:, :], ALU.add), [rmod], [rmod] + prm)
            for grp in range(2):
                for which, j in ((0, 1), (1, 4)):
                    P.op("dve", lambda e, grp=grp, which=which, j=j: e.tensor_scalar(
                        self.modP1[:, grp, :, which, :], self.modT[:, grp, :, j * 8:(j + 1) * 8], 1.0, None, ALU.add), [rmod], [rmod])
            xin = [self.sb([128, 2, 1024], F32, "xin") for _ in range(2)]
            rxin = [Reg(), Reg()]
            xt = [self.sb([128, 8, G], F32, "xt0") for _ in range(2)]
            rxt = [Reg(), Reg()]
            for g in range(NPG + NSG):
                s = g % 2
                src = D["xp"][g * G:(g + 1) * G, :] if g < NPG else D["xs"][(g - NPG) * G:(g - NPG + 1) * G, :]
                P.dma("sp", lambda e, s=s, src=src: e.dma_start(out=xin[s][:], in_=src.rearrange("(b p) f -> p b f", p=128)), [], [rxin[s]])
                for c2 in range(4):
                    bank, br = self.ps_full()
                    for cc in range(2):
                        c = c2 * 2 + cc
                        for b in range(2):
                            P.op("pe", lambda e, s=s, c=c, cc=cc, b=b, bank=bank: e.transpose(
                                bank[:, cc * 256 + b * 128:cc * 256 + b * 128 + 128], xin[s][:, b, c * 128:(c + 1) * 128], ident[:]),
                                [rxin[s], rc], br)
                    eng = "act" if c2 % 2 == 0 else "dve"
                    if eng == "act":
                        P.op("act", lambda e, s=s, c2=c2, bank=bank: e.copy(xt[s][:, 2 * c2:2 * c2 + 2, :], bank[:, :].rearrange("p (a t) -> p a t", a=2)), [], [rxt[s]] + br)
                    else:
                        P.op("dve", lambda e, s=s, c2=c2, bank=bank: e.tensor_copy(xt[s][:, 2 * c2:2 * c2 + 2, :], bank[:, :].rearrange("p (a t) -> p a t", a=2)), [], [rxt[s]] + br)
                self.store_x(g, xt[s][:], rxt[s])

    def final(self):
        P, D = self.P, self.D
        with self.phase():
            xt = [self.sb([128, 8, G], F32, "xtf") for _ in range(2)]
            rxt = [Reg(), Reg()]
            yo = [self.sb([128, 2, 1024], F32, "yo") for _ in range(2)]
            ryo = [Reg(), Reg()]
            for g in range(NPG + NSG):
                s = g % 2
                self.load_x(g, xt[s][:], rxt[s])
                for b in range(2):
                    for c4 in range(2):
                        bank, br = self.ps_full()
                        for cc in range(4):
                            c = c4 * 4 + cc
                            P.op("pe", lambda e, s=s, c=c, cc=cc, b=b, bank=bank: e.transpose(
                                bank[:, cc * 128:(cc + 1) * 128], xt[s][:, c, b * 128:(b + 1) * 128], self.ident[:]), [rxt[s], self.rconst], br)
                        if c4 == 0:
                            P.op("act", lambda e, s=s, b=b, c4=c4, bank=bank: e.copy(yo[s][:, b, c4 * 512:(c4 + 1) * 512], bank[:, :]), [], [ryo[s]] + br)
                        else:
                            P.op("dve", lambda e, s=s, b=b, c4=c4, bank=bank: e.tensor_copy(yo[s][:, b, c4 * 512:(c4 + 1) * 512], bank[:, :]), [], [ryo[s]] + br)
                dst = D["y_p"][g * G:(g + 1) * G, :] if g < NPG else D["y_s"][(g - NPG) * G:(g - NPG + 1) * G, :]
                self.fin.append(P.dma("sp", lambda e, s=s, dst=dst: e.dma_start(out=dst.rearrange("(b p) f -> p b f", p=128), in_=yo[s][:]), [ryo[s]], []))

    def mlp_prefetch(self, l):
        D = self.D
        W1 = self.sb([128, 8, 4096], BF16, "W1")
        W2a = self.sb([128, 16, 1024], BF16, "W2a")
        rW1 = [[Reg() for _ in range(2)] for _ in range(8)]
        rW2 = [Reg() for _ in range(8)]
        for h in range(2):
            for kc in range(8):
                self.wload(W1[:, kc, h * 2048:(h + 1) * 2048], D["mlp_w1"][l, kc * 128:(kc + 1) * 128, h * 2048:(h + 1) * 2048], rW1[kc][h])
        for f4 in range(4):
            self.wload(W2a[:, f4 * 4:(f4 + 1) * 4, :], D["mlp_w2"][l, f4 * 512:(f4 + 1) * 512, :].rearrange("(f p) n -> p f n", p=128), rW2[f4])
        return W1, W2a, rW1, rW2

    def mlp(self, l, pre):
        P, D = self.P, self.D
        W1, W2a, rW1, rW2 = pre
        with self.phase():
            W2b = self.sb([128, 16, 1024], BF16, "W2b")
            for f4 in range(4, 8):
                self.wload(W2b[:, (f4 - 4) * 4:(f4 - 3) * 4, :], D["mlp_w2"][l, f4 * 512:(f4 + 1) * 512, :].rearrange("(f p) n -> p f n", p=128), rW2[f4])

            def w2ap(f, n):
                return (W2a if f < 16 else W2b)[:, f % 16, n * 128:(n + 1) * 128]
            xt = [self.sb([128, 8, G], F32, "xt") for _ in range(2)]
            rxt = [Reg(), Reg()]
            hT = self.sb([128, 8, G], BF16, "hT")
            rh = Reg()
            hid = self.sb([128, 32, G], BF16, "hid")
            rhid = [Reg() for _ in range(32)]
            zz = self.sb([128, 2, 8, G], F32, "zz")
            rz0, rz1 = Reg(), Reg()
            rt = [self.sb([128, G], F32, "rt") for _ in range(3)]
            rrt = [Reg() for _ in range(3)]
            self.alloc_ln()
            ng = NPG + NSG
            self.load_x(0, xt[0][:], rxt[0])
            self.modulate(0, l, 1, xt[0], rxt[0], hT, rh)

            def step3(g, f):
                ps, pr = self.ps_half()
                for kc in range(8):
                    P.op("pe", lambda e, f=f, kc=kc, ps=ps: e.matmul(ps, W1[:, kc, f * 128:(f + 1) * 128], hT[:, kc, :],
                                                             start=(kc == 0), stop=(kc == 7)), [rW1[kc][f // 16], rh], pr)
                k = f % 3
                P.op("act", lambda e, k=k, ps=ps: e.activation(rt[k][:], ps, AF.Relu), [], [rrt[k]] + pr)
                eng = "dve" if f % 2 == 0 else "pool"
                P.op(eng, lambda e, k=k, f=f: e.tensor_tensor(hid[:, f, :], rt[k][:], rt[k][:], ALU.mult), [rrt[k]], [rhid[f]])

            for g in range(ng):
                s = g % 2
                if g + 1 < ng:
                    self.load_x(g + 1, xt[1 - s][:], rxt[1 - s])
                for f in range(8):
                    step3(g, f)
                if g > 0:
                    self.ln_p2(g - 1, l, 1, zz, rz0, rz1)
                for f in range(8, 32):
                    step3(g, f)
                P.op("act", lambda e, s=s: e.mul(zz[:, 0], xt[s][:], ALPHA), [rxt[s]], [rz0])
                for n in range(8):
                    ps, pr = self.ps_half()
                    for f in range(32):
                        P.op("pe", lambda e, f=f, n=n, ps=ps: e.matmul(ps, w2ap(f, n), hid[:, f, :],
                                                                start=(f == 0), stop=(f == 31)), [rW2[f // 4], rhid[f]], pr)
                    g2 = self.modv(g, l, 5, n)
                    P.op("dve", lambda e, n=n, ps=ps, g2=g2: e.scalar_tensor_tensor(zz[:, 0, n, :], ps, g2, zz[:, 0, n, :], ALU.mult, ALU.add),
                         [rz0, self.rmod], [rz0] + pr)
                self.ln_p1(zz, rz0, rz1)
                if g + 1 < ng:
                    self.modulate(g + 1, l, 1, xt[1 - s], rxt[1 - s], hT, rh)
            self.ln_p2(ng - 1, l, 1, zz, rz0, rz1)

    def mixer(self, l, with_mlp=False):
        m = l % 4
        kind = ("swa", "na", "ret", "gqa")[m]
        if m == 2:
            self.ret_layer(l)
        else:
            self.attn_layer(l, kind)
        if not with_mlp:
            self.outproj(l, kind)
            return
        st = ExitStack()
        prev = self.cur
        self.cur = st
        try:
            pre = self.mlp_prefetch(l)
            self.outproj(l, kind)
            self.mlp(l, pre)
        finally:
            self.cur = prev
            st.close()

    def outproj(self, l, kind):
        P, D = self.P, self.D
        with self.phase():
            if kind == "ret":
                kp, mixd, wsrc = 128, D["mixR_d"], D["ret_wo"].rearrange("(c p) n -> p c n", p=128)
            else:
                wn = {"swa": "swa_wo", "na": "na_wo", "gqa": "gqa_wo"}[kind]
                kp, mixd, wsrc = 64, D["mixA_d"], D[wn].rearrange("(h d) n -> d h n", d=64)
            Wo = self.sb([128, 16, 1024], BF16, "Wo")
            rWo = [Reg() for _ in range(4)]
            for q in range(4):
                self.wload(Wo[0:kp, q * 4:(q + 1) * 4, :], wsrc[:, q * 4:(q + 1) * 4, :], rWo[q])
            xt = [self.sb([128, 8, G], F32, "xt") for _ in range(2)]
            rxt = [Reg(), Reg()]
            mx = [self.sb([128, 16, G], BF16, "mx") for _ in range(2)]
            rmx = [Reg(), Reg()]
            zzs = [self.sb([128, 2, 8, G], F32, "zz") for _ in range(2)]
            rzs = [(Reg(), Reg()) for _ in range(2)]
            self.alloc_ln()
            ng = NPG + NSG

            def loads(g):
                s = g % 2
                self.load_x(g, xt[s][:], rxt[s])
                t0 = g * G
                P.dma("sp", lambda e: e.dma_start(out=mx[s][0:kp], in_=mixd[:, :, t0:t0 + G]), [self.mixreg[g]], [rmx[s]])

            def mm(g, n):
                s = g % 2
                zz, (rz0, rz1) = zzs[s], rzs[s]
                ps, pr = self.ps_half()
                for j in range(16):
                    P.op("pe", lambda e, j=j, n=n, ps=ps, s=s: e.matmul(ps, Wo[0:kp, j, n * 128:(n + 1) * 128], mx[s][0:kp, j, :],
                                                                   start=(j == 0), stop=(j == 15)), [rWo[j // 4], rmx[s]], pr)
                g1 = self.modv(g, l, 2, n)
                P.op("dve", lambda e, n=n, ps=ps, g1=g1, zz=zz: e.scalar_tensor_tensor(zz[:, 0, n, :], ps, g1, zz[:, 0, n, :], ALU.mult, ALU.add),
                     [rz0, self.rmod], [rz0] + pr)
            loads(0)
            for g in range(ng):
                s = g % 2
                zz, (rz0, rz1) = zzs[s], rzs[s]
                if g + 1 < ng:
                    loads(g + 1)
                P.op("act", lambda e, s=s, zz=zz: e.mul(zz[:, 0], xt[s][:], ALPHA), [rxt[s]], [rz0])
                for n in range(4):
                    mm(g, n)
                if g > 0:
                    self.ln_p2(g - 1, l, 0, zzs[1 - s], rzs[1 - s][0], rzs[1 - s][1])
                for n in range(4, 8):
                    mm(g, n)
                self.ln_p1(zz, rz0, rz1)
            sl = (ng - 1) % 2
            self.ln_p2(ng - 1, l, 0, zzs[sl], rzs[sl][0], rzs[sl][1])

    def attn_layer(self, l, kind):
        P, D, nc = self.P, self.D, self.nc
        cfg = {"swa": dict(nk=2, vh=4, rope=True, rms=False, w="swa_wqkv_p", ck="cswak", cv="cswav", ok="o_swak", ov="o_swav", ns=16, roll=True),
               "na": dict(nk=8, vh=16, rope=False, rms=False, w="na_wqkv", ck="cnak", cv="cnav", ok="o_nak", ov="o_nav", ns=4, roll=True),
               "gqa": dict(nk=2, vh=4, rope=True, rms=True, w="gqa_wqkv_p", ck="cgqak", cv="cgqav", ok="o_gqak", ov="o_gqav", ns=16, roll=False)}[kind]
        nk, vh, ns = cfg["nk"], cfg["vh"], cfg["ns"]
        kvw = nk * 128
        nqk = 8 + nk
        self.ntrans = 4
        with self.phase():
            rc = self.rconst
            Wqk = self.sb([128, 8, nqk * 128], BF16, "Wqk")
            Wv = self.sb([128, 8, kvw], BF16, "Wv")
            rW = [Reg() for _ in range(8)]
            for kc in range(8):
                self.wload(Wqk[:, kc, :], D[cfg["w"]][kc * 128:(kc + 1) * 128, 0:nqk * 128], rW[kc])
            rWv = [Reg() for _ in range(8)]
            for kc in range(8):
                self.wload(Wv[:, kc, :], D[cfg["w"]][kc * 128:(kc + 1) * 128, nqk * 128:nqk * 128 + kvw], rWv[kc])
            Wrot, rWrot = None, Reg()
            if cfg["rope"]:
                Wrot = self.sb([128, 8, nqk * 128], BF16, "Wrot")
                src = Wqk[:].rearrange("p k (b t i) -> p k b t i", t=2, i=16)
                dst = Wrot[:].rearrange("p k (b t i) -> p k b t i", t=2, i=16)
                P.op("pool", lambda e: e.tensor_scalar(dst[:, :, :, 0, :], src[:, :, :, 1, :], -1.0, None, ALU.mult), rW, [rWrot])
                P.op("pool", lambda e: e.tensor_copy(dst[:, :, :, 1, :], src[:, :, :, 0, :]), rW + [rWrot], [rWrot])
            hoist = kind != "na"
            nxt = 3 if hoist else 1
            nht = 2 if hoist else 1
            xt = [self.sb([128, 8, G], F32, "xt") for _ in range(nxt)]
            rxt = [Reg() for _ in range(nxt)]
            hTs = [self.sb([128, 8, G], BF16, "hT") for _ in range(nht)]
            rhs_ = [Reg() for _ in range(nht)]
            QTe = [self.sb([128, 8, G], BF16, "QTe") for _ in range(2)]
            QTo = [self.sb([128, 8, G], BF16, "QTo") for _ in range(2)]
            rQ = [Reg() for _ in range(2)]
            for qi in range(2):
                P.op("pool", lambda e, qi=qi: e.memset(QTe[qi][:], 0.0), [], [rQ[qi]])
                P.op("pool", lambda e, qi=qi: e.memset(QTo[qi][:], 0.0), [], [rQ[qi]])
            KTp = self.sb([128, nk, G], BF16, "KTp")
            Vp = self.sb([128, 2, vh, 65], BF16, "Vp")
            rKp, rVp = Reg(), Reg()
            KTs = self.sb([128, nk, ns * G], BF16, "KTs")
            Vs = self.sb([128, ns * 2, vh, 65], BF16, "Vs")
            rK = [Reg() for _ in range(ns)]
            rV = [Reg() for _ in range(ns)]
            cKT = self.sb([128, nk, 512], BF16, "cKT")
            cV = self.sb([128, 4, vh, 65], BF16, "cV")
            rcK, rcV = Reg(), Reg()
            stg = self.sb([128, 4, 1024], F32, "stg")
            rstg = Reg()
            NPT = 6
            self.PT = [self.sb([128, 256 if kind == "na" else 512], BF16, "PT") for _ in range(NPT)]
            self.rPT = [Reg() for _ in range(NPT)]
            self.pt_ctr = 0
            OT = self.sb([64, 16, G], BF16, "OT")
            rOT = Reg()
            NW = 3
            NMAX = 256 if kind == "na" else 512
            dns = [self.sb([128, NMAX], F32, "dn") for _ in range(NW)]
            rdns = [Reg() for _ in range(NW)]
            for wi in range(NW):
                P.op("pool", lambda e, wi=wi: e.memset(dns[wi][:], 0.0), [], [rdns[wi]])
            bcss = [self.sb([128, NMAX], F32, "bcs") for _ in range(NW)]
            rbcss = [Reg() for _ in range(NW)]
            tmp = {n: [self.sb([128, G], F32, n) for _ in range(2)] for n in (("t1", "t2", "sq", "rstd") if cfg["rope"] else ())}
            rtmp = {n: [Reg(), Reg()] for n in tmp}
            tctr = [0]
            cs = [self.sb([128, 2, G], F32, "cs") for _ in range(3 if cfg["rope"] else 0)]
            rcs = [Reg(), Reg(), Reg()]
            P.op("pool", lambda e: e.memset(Vp[:, :, :, 64:65], 1.0), [], [rVp])
            P.op("pool", lambda e: e.memset(Vs[:, :, :, 64:65], 1.0), [], rV)
            P.op("pool", lambda e: e.memset(cV[:, :, :, 64:65], 1.0), [], [rcV])
            exps, rsink = None, Reg()
            if kind == "swa":
                exps = self.sb([128, 16], F32, "exps")
                P.dma("sp", lambda e: e.dma_start(out=exps[:], in_=D["swa_sink"].partition_broadcast(128)), [], [rsink])
                P.op("act", lambda e: e.activation(exps[:], exps[:], AF.Exp), [rsink], [rsink])
                mk = self.sb([128, 2, 512], BF16, "mk")
                rmk = Reg()
                self.wload(mk[:], D["swamask"].rearrange("m p n -> p m n"), rmk)
            if kind == "na":
                tab = self.sb([128, 14336], BF16, "tab")
                rtab = Reg()
                P.dma("sp", lambda e: e.dma_start(out=tab[:], in_=D["natab_d"][:, :]), [self.rnatab], [rtab])
            P.dma("sp", lambda e: e.dma_start(out=stg[:, :, 0:kvw], in_=D[cfg["ck"]].rearrange("(b p) f -> p b f", p=128)), [], [rstg])
            for j in range(nk):
                bank, br = self.ps_full()
                for tb in range(4):
                    P.op("pe", lambda e, j=j, tb=tb, bank=bank: e.transpose(bank[:, tb * 128:(tb + 1) * 128], stg[:, tb, j * 128:(j + 1) * 128], self.ident[:]),
                         [rstg, rc], br)
                P.op("act" if j % 2 == 0 else "dve",
                     (lambda e, j=j, bank=bank: e.copy(cKT[:, j, :], bank[:, :])) if j % 2 == 0 else (lambda e, j=j, bank=bank: e.tensor_copy(cKT[:, j, :], bank[:, :])),
                     [], [rcK] + br)
            P.dma("sp", lambda e: e.dma_start(out=stg[:, :, 0:kvw], in_=D[cfg["cv"]].rearrange("(b p) f -> p b f", p=128)), [], [rstg])
            P.op("act", lambda e: e.copy(cV[:, :, :, 0:64], stg[:, :, 0:kvw].rearrange("p b (h d) -> p b h d", d=64)), [rstg], [rcV])

            def qk_post(sample, is_k, psa, pra, psb, prb, dests, rdest, csg, rcsg, k32, rk32):
                i = tctr[0] % 2
                tctr[0] += 1
                if not cfg["rms"]:
                    if not (cfg["rope"] and sample):
                        if k32 is not None:
                            P.op("dve", lambda e: e.tensor_copy(k32, psa), [], [rk32] + pra)
                            P.op("act", lambda e: e.copy(dests[0][0], k32), [rk32], [rdest])
                        else:
                            for di, (dst, lo, hi) in enumerate(dests):
                                if (i + di) % 2 == 0:
                                    P.op("act", lambda e, dst=dst, lo=lo, hi=hi: e.copy(dst, psa[lo:hi]), [], [rdest] + pra)
                                else:
                                    P.op("dve", lambda e, dst=dst, lo=lo, hi=hi: e.tensor_copy(dst, psa[lo:hi]), [], [rdest] + pra)
                        return
                    t1, t2 = tmp["t1"][i], tmp["t2"][i]
                    r1, r2 = rtmp["t1"][i], rtmp["t2"][i]
                    P.op("dve", lambda e: e.tensor_tensor(t1[:], psa, csg[:, 0, :], ALU.mult), [rcsg], [r1] + pra)
                    P.op("dve", lambda e: e.tensor_tensor(t2[:], psb, csg[:, 1, :], ALU.mult), [rcsg], [r2] + prb)
                    for dst, lo, hi in dests:
                        P.op("pool", lambda e, dst=dst, lo=lo, hi=hi: e.tensor_tensor(dst, t1[lo:hi], t2[lo:hi], ALU.add), [r1, r2], [rdest])
                    return
                sq, rstd = tmp["sq"][i], tmp["rstd"][i]
                rsq, rrs = rtmp["sq"][i], rtmp["rstd"][i]
                gc = self.gqn[:, 2:3] if is_k else self.gqn[:, 0:1]
                gcp = self.gqn[:, 3:4] if is_k else self.gqn[:, 1:2]
                P.op("act", lambda e: e.activation(sq[:], psa, AF.Square), [], [rsq] + pra)
                pss, prs = self.ps_half()
                P.op("pe", lambda e: e.matmul(pss, self.blk32[:, :], sq[:], start=True, stop=True), [rsq, rc], prs)
                P.op("dve", lambda e: e.tensor_scalar(rstd[:], pss, 1.0 / 64.0, RMS_EPS, ALU.mult, ALU.add), [], [rrs] + prs)
                P.op("dve", lambda e: e.reciprocal(rstd[:], rstd[:]), [rrs], [rrs])
                P.op("act", lambda e: e.activation(rstd[:], rstd[:], AF.Sqrt), [rrs], [rrs])
                if not sample:
                    if k32 is not None:
                        P.op("dve", lambda e: e.scalar_tensor_tensor(k32, psa, gc, rstd[:], ALU.mult, ALU.mult), [rrs, self.rmod], [rk32] + pra)
                        P.op("act", lambda e: e.copy(dests[0][0], k32), [rk32], [rdest])
                    else:
                        for dst, lo, hi in dests:
                            P.op("dve", lambda e, dst=dst, lo=lo, hi=hi: e.scalar_tensor_tensor(dst, psa[lo:hi], gc[lo:hi], rstd[lo:hi], ALU.mult, ALU.mult), [rrs, self.rmod], [rdest] + pra)
                    return
                t1, t2 = tmp["t1"][i], tmp["t2"][i]
                r1, r2 = rtmp["t1"][i], rtmp["t2"][i]
                P.op("dve", lambda e: e.scalar_tensor_tensor(t1[:], psa, gc, csg[:, 0, :], ALU.mult, ALU.mult), [rcsg, self.rmod], [r1] + pra)
                P.op("dve", lambda e: e.scalar_tensor_tensor(t2[:], psb, gcp, csg[:, 1, :], ALU.mult, ALU.mult), [rcsg, self.rmod], [r2] + prb)
                P.op("pool", lambda e: e.tensor_tensor(t1[:], t1[:], t2[:], ALU.add), [r1, r2], [r1])
                for dst, lo, hi in dests:
                    P.op("pool", lambda e, dst=dst, lo=lo, hi=hi: e.tensor_tensor(dst, t1[lo:hi], rstd[lo:hi], ALU.mult), [r1, rrs], [rdest])

            k32T = self.sb([128, nk, G], F32, "k32T")
            rk32 = Reg()
            ktok = stg[:, 2:4, 0:kvw]
            rktok = rstg
            v32 = stg[:, 0:2, 0:kvw]
            rv32 = rstg
            xs_ctr = [0]

            plist = [(g, True, True) for g in range(NPG)]
            if cfg["roll"]:
                plist += [(NPG + gi, True, True) for gi in range(NSG)]
            else:
                plist += [(NPG + gi, False, True) for gi in range(NSG)] + [(NPG + gi, True, False) for gi in range(NSG)]
            pn = [0]

            def prefetch(n):
                g = plist[n][0]
                self.load_x(g, xt[n % nxt][:], rxt[n % nxt])
                if cfg["rope"] and g >= NPG:
                    gi = g - NPG
                    P.dma("sp", lambda e: e.dma_start(out=cs[n % 3][:], in_=D["rope64"][:, :, gi * G:(gi + 1) * G].rearrange("c p t -> p c t")), [], [rcs[n % 3]])

            def do_mod(n):
                g = plist[n][0]
                self.modulate(g, l, 0, xt[n % nxt], rxt[n % nxt], hTs[n % nht], rhs_[n % nht])
            if hoist:
                prefetch(0)
                prefetch(1)
                do_mod(0)

            def proj(g, do_q, do_kv):
                n = pn[0]
                pn[0] += 1
                assert plist[n] == (g, do_q, do_kv), (n, plist[n], g, do_q, do_kv)
                sample = g >= NPG
                gi = g - NPG
                if hoist:
                    if n + 2 < len(plist):
                        prefetch(n + 2)
                    if n + 1 < len(plist):
                        do_mod(n + 1)
                else:
                    prefetch(n)
                    do_mod(n)
                hT, rh = hTs[n % nht], rhs_[n % nht]
                roped = cfg["rope"] and sample
                csg, rcsg = None, None
                if roped:
                    csg, rcsg = cs[n % 3], rcs[n % 3]
                js = (list(range(8)) if do_q else []) + (list(range(8, 8 + nk)) if do_kv else [])
                qs = g % 2
                for j in js:
                    psa, pra = self.ps_half()
                    for kc in range(8):
                        P.op("pe", lambda e, j=j, kc=kc, psa=psa: e.matmul(psa, Wqk[:, kc, j * 128:(j + 1) * 128], hT[:, kc, :], start=(kc == 0), stop=(kc == 7)),
                             [rW[kc], rh], pra)
                    psb, prb = None, None
                    if roped:
                        psb, prb = self.ps_half()
                        for kc in range(8):
                            P.op("pe", lambda e, j=j, kc=kc, psb=psb: e.matmul(psb, Wrot[:, kc, j * 128:(j + 1) * 128], hT[:, kc, :], start=(kc == 0), stop=(kc == 7)),
                                 [rWrot, rh], prb)
                    if j < 8:
                        qk_post(sample, False, psa, pra, psb, prb, [(QTe[qs][0:64, j, :], 0, 64), (QTo[qs][64:128, j, :], 64, 128)], rQ[qs], csg, rcsg, None, None)
                    elif sample:
                        sl = gi % ns
                        qk_post(True, True, psa, pra, psb, prb, [(KTs[:, j - 8, sl * G:(sl + 1) * G], 0, 128)], rK[sl], csg, rcsg, None, None)
                    else:
                        qk_post(False, True, psa, pra, psb, prb, [(KTp[:, j - 8, :], 0, 128)], rKp, csg, rcsg, k32T[:, j - 8, :], rk32)
                if not do_kv:
                    return
                for b in range(2):
                    for cb in range((kvw + 511) // 512):
                        w = min(512, kvw - cb * 512)
                        bank, br = self.ps_full()
                        for kc in range(8):
                            P.op("pe", lambda e, b=b, cb=cb, w=w, kc=kc, bank=bank: e.matmul(bank[:, 0:w], hT[:, kc, b * 128:(b + 1) * 128], Wv[:, kc, cb * 512:cb * 512 + w],
                                                                                     start=(kc == 0), stop=(kc == 7)), [rWv[kc], rh], br)
                        nh = w // 64
                        h0 = cb * 8
                        if sample:
                            sl = gi % ns
                            P.op("act", lambda e, b=b, sl=sl, h0=h0, nh=nh, w=w, bank=bank: e.copy(Vs[:, sl * 2 + b, h0:h0 + nh, 0:64], bank[:, 0:w].rearrange("p (h d) -> p h d", d=64)),
                                 [], [rV[sl]] + br)
                        else:
                            P.op("act", lambda e, b=b, h0=h0, nh=nh, w=w, bank=bank: e.copy(Vp[:, b, h0:h0 + nh, 0:64], bank[:, 0:w].rearrange("p (h d) -> p h d", d=64)),
                                 [], [rVp] + br)
                            P.op("dve", lambda e, b=b, cb=cb, w=w, bank=bank: e.tensor_copy(v32[:, b, cb * 512:cb * 512 + w], bank[:, 0:w]), [], [rv32] + br)
                if not sample:
                    self.fin.append(P.dma("sp", lambda e: e.dma_start(out=D[cfg["ov"]][g * G:(g + 1) * G, :].rearrange("(b p) f -> p b f", p=128), in_=v32), [rv32], []))
                    for b in range(2):
                        for j4 in range((nk + 3) // 4):
                            nj = min(4, nk - j4 * 4)
                            bank, br = self.ps_full()
                            for jj in range(nj):
                                j = j4 * 4 + jj
                                P.op("pe", lambda e, b=b, j=j, jj=jj, bank=bank: e.transpose(bank[:, jj * 128:(jj + 1) * 128], k32T[:, j, b * 128:(b + 1) * 128], self.ident[:]),
                                     [rk32, rc], br)
                            P.op("dve", lambda e, b=b, j4=j4, nj=nj, bank=bank: e.tensor_copy(ktok[:, b, j4 * 512:j4 * 512 + nj * 128], bank[:, 0:nj * 128]), [], [rktok] + br)
                    self.fin.append(P.dma("sp", lambda e: e.dma_start(out=D[cfg["ok"]][g * G:(g + 1) * G, :].rearrange("(b p) f -> p b f", p=128), in_=ktok), [rktok], []))

            acc_ctr = [0]

            def core(ai, qap, rq, N, a3, chunks, sinkap, dest):
                acc, racc = self.banks[4 + ai], self.bregs[4 + ai]
                dn, rdn, bcs, rbcs = dns[ai], rdns[ai], bcss[ai], rbcss[ai]
                n = len(chunks)
                pts = []
                for i in range(n + 1):
                    if i < n:
                        ch = chunks[i]
                        bank, br = self.ps_full()
                        pb, kp = ch["pb"], ch["kp"]
                        ni = ch.get("n", N)
                        qa = ch.get("q", qap)
                        sv = bank[pb:pb + kp, 0:ni]
                        sv3 = sv.rearrange("p (a t) -> p a t", a=a3) if a3 > 1 else sv
                        hasb = ch.get("bias") is not None
                        P.op("pe", lambda e, ch=ch, sv3=sv3, hasb=hasb, qa=qa: e.matmul(sv3, ch["kt"], qa, start=True, stop=not hasb), [ch["rk"], rq], br)
                        if hasb:
                            P.op("pe", lambda e, ch=ch, sv=sv: e.matmul(sv, ch["bl"], ch["bias"], start=False, stop=True), [ch["rb"], rc], br)
                        k = self.pt_ctr % len(self.PT)
                        self.pt_ctr += 1
                        pt, rpt = self.PT[k], self.rPT[k]
                        P.op("act", lambda e, pt=pt, pb=pb, kp=kp, sv=sv, ni=ni: e.activation(pt[pb:pb + kp, 0:ni], sv, AF.Exp, scale=0.125), [], [rpt] + br)
                        if ch.get("zero") is not None:
                            lo, hi = ch["zero"]
                            P.op("pool", lambda e, pt=pt, lo=lo, hi=hi, ni=ni: e.memset(pt[lo:hi, 0:ni], 0.0), [], [rpt])
                        pts.append((pt, rpt))
                    if i >= 1:
                        ch = chunks[i - 1]
                        pt, rpt = pts[i - 1]
                        pb, kp = ch["pb"], ch["kp"]
                        ni = ch.get("n", N)
                        c0 = ch.get("c0", 0)
                        P.op("pe", lambda e, ch=ch, pt=pt, pb=pb, kp=kp, i=i, ni=ni, c0=c0: e.matmul(acc[0:65, c0:c0 + ni], ch["v"], pt[pb:pb + kp, 0:ni], start=(i == 1), stop=(i == n)),
                             [ch["rv"], rpt], racc)
                    yield
                v3 = (lambda ap: ap.rearrange("p (a t) -> p a t", a=a3)) if a3 > 1 else (lambda ap: ap)
                if sinkap is not None:
                    P.op("dve", lambda e: e.tensor_tensor(v3(dn[64:65, 0:N]), v3(acc[64:65, 0:N]), sinkap, ALU.add), [rsink], [rdn] + racc)
                else:
                    P.op("dve", lambda e: e.tensor_copy(dn[64:65, 0:N], acc[64:65, 0:N]), [], [rdn] + racc)
                P.op("dve", lambda e: e.reciprocal(dn[64:65, 0:N], dn[64:65, 0:N]), [rdn], [rdn])
                yield
                yield
                bcb, rbc = self.banks[7], self.bregs[7]
                P.op("pe", lambda e: e.matmul(bcb[:, 0:N], self.sel64[:, :], dn[:, 0:N], start=True, stop=True), [rdn, rc], rbc)
                P.op("act", lambda e: e.copy(bcs[0:64, 0:N], bcb[0:64, 0:N]), [], [rbcs] + rbc)
                yield
                P.op("dve", lambda e: e.tensor_tensor(dest, v3(acc[0:64, 0:N]), v3(bcs[0:64, 0:N]), ALU.mult), [rbcs], [rOT] + racc)

            def run_units(gens):
                active = {}
                pending = list(gens)
                while active or pending:
                    for slot in range(NW):
                        if slot not in active and pending:
                            active[slot] = pending.pop(0)(slot)
                    for slot in list(active):
                        try:
                            next(active[slot])
                        except StopIteration:
                            del active[slot]

            def attend(g):
                sample = g >= NPG
                gi = g - NPG
                qs = g % 2
                rq = rQ[qs]
                units = []
                if kind in ("swa", "gqa"):
                    for kvh in range(4):
                        b64 = (kvh % 2) * 64
                        c0 = 4 * (kvh // 2)
                        for qb in range(2):
                            qap = (QTe if kvh % 2 == 0 else QTo)[qs][:, c0:c0 + 4, qb * 128:(qb + 1) * 128]
                            chunks = []
                            if not sample:
                                for kb in range(2):
                                    chunks.append(dict(kt=KTp[:, kvh // 2, kb * 128:(kb + 1) * 128], rk=rKp, v=Vp[:, kb, kvh, :], rv=rVp, pb=0, kp=128))
                            else:
                                i = 2 * gi + qb
                                blks = [i - 1, i, i + 1] if kind == "swa" else list(range(32))
                                for bi in blks:
                                    if bi < 0 or bi > 31:
                                        continue
                                    ch = dict(kt=KTs[:, kvh // 2, bi * 128:(bi + 1) * 128], rk=rK[bi // 2], v=Vs[:, bi, kvh, :], rv=rV[bi // 2], pb=0, kp=128)
                                    if kind == "swa" and bi != i:
                                        ch["bias"] = mk[:, 0 if bi < i else 1, :]
                                        ch["bl"] = self.identb[:, :]
                                        ch["rb"] = rmk
                                    chunks.append(ch)
                                for tb in range(4):
                                    chunks.append(dict(kt=cKT[:, kvh // 2, tb * 128:(tb + 1) * 128], rk=rcK, v=cV[:, tb, kvh, :], rv=rcV, pb=0, kp=128))
                            sinkap = None
                            if kind == "swa":
                                sinkap = exps[64:65, 4 * kvh:4 * kvh + 4][:, :, None].to_broadcast([1, 4, 128])
                            units.append(lambda ai, qap=qap, chunks=chunks, sinkap=sinkap, kvh=kvh, qb=qb: core(ai, qap, rq, 512, 4, chunks, sinkap, OT[0:64, 4 * kvh:4 * kvh + 4, qb * 128:(qb + 1) * 128]))
                else:
                    for h in range(16):
                        b64 = (h % 2) * 64
                        if not sample:
                            qap = (QTe if h % 2 == 0 else QTo)[qs][:, h // 2, :]
                            chunks = [dict(kt=KTp[:, h // 2, kb * 128:(kb + 1) * 128], rk=rKp, v=Vp[:, kb, h, :], rv=rVp, pb=0, kp=128) for kb in range(2)]
                            units.append(lambda ai, qap=qap, chunks=chunks, h=h: core(ai, qap, rq, 256, 1, chunks, None, OT[0:64, h, :]))
                        else:
                            qsel = (QTe if h % 2 == 0 else QTo)[qs]
                            chunks = []
                            for tb in range(4):
                                chunks.append(dict(kt=cKT[:, h // 2, tb * 128:(tb + 1) * 128], rk=rcK, v=cV[:, tb, h, :], rv=rcV, pb=0, kp=128,
                                                   q=qsel[:, h // 2, :], c0=0, n=256))
                            for rl in range(4):
                                r = 4 * gi + rl
                                r0 = min(max(r - 4, 0), 56)
                                for m in range(r0 // 2, (r0 + 7) // 2 + 1):
                                    a_in = r0 <= 2 * m <= r0 + 7
                                    b_in = r0 <= 2 * m + 1 <= r0 + 7
                                    ee = 2 * m - r + 7
                                    assert 0 <= ee <= 13, (r, m, ee)
                                    sl = (m // 2) % ns
                                    lb = m % 2
                                    ch = dict(kt=KTs[:, h // 2, sl * G + lb * 128:sl * G + lb * 128 + 128], rk=rK[sl],
                                              v=Vs[:, sl * 2 + lb, h, :], rv=rV[sl], pb=0, kp=128,
                                              bias=tab[:, (h * 14 + ee) * 64:(h * 14 + ee + 1) * 64], bl=self.identb[:, :], rb=rtab,
                                              q=qsel[:, h // 2, rl * 64:(rl + 1) * 64], c0=rl * 64, n=64)
                                    if not a_in:
                                        ch["zero"] = (0, 64)
                                    if not b_in:
                                        ch["zero"] = (64, 128)
                                    chunks.append(ch)
                            units.append(lambda ai, chunks=chunks, h=h: core(ai, None, rq, 256, 1, chunks, None, OT[0:64, h, :]))
                run_units(units)
                t0 = g * G
                P.dma("sp", lambda e: e.dma_start(out=D["mixA_d"][:, :, t0:t0 + G], in_=OT[:]), [rOT], [self.mixreg[g]])

            for g in range(NPG):
                proj(g, True, True)
                attend(g)
            if cfg["roll"]:
                proj(NPG, True, True)
                for gi in range(NSG):
                    if gi + 1 < NSG:
                        proj(NPG + gi + 1, True, True)
                    attend(NPG + gi)
            else:
                for gi in range(NSG):
                    proj(NPG + gi, False, True)
                for gi in range(NSG):
                    proj(NPG + gi, True, False)
                    attend(NPG + gi)
        self.ntrans = 7

    def na_table(self):
        P, D = self.P, self.D
        self.rnatab = Reg("natab")
        with self.phase():
            E = self.sb([31, 8192], F32, "naE")
            rpbT = self.sb([31, 240], F32, "rpbT")
            msk = self.sb([128, 64], F32, "namsk")
            T = self.sb([128, 14336], BF16, "naT")
            rE, rT = Reg(), Reg()
            P.dma("sp", lambda e: e.dma_start(out=E[:], in_=D["naE"][:, :]), [], [rE])
            P.dma("sp", lambda e: e.dma_start(out=rpbT[:], in_=D["na_rpbT"][:, :]), [], [rE])
            P.dma("sp", lambda e: e.dma_start(out=msk[:], in_=D["namask"][:, :]), [], [rE])
            T4 = T[:].rearrange("p (h x q) -> p h x q", x=14, q=64)
            for qc in range(64):
                bank, br = self.ps_full()
                P.op("pe", lambda e, qc=qc, bank=bank: e.matmul(bank[:, 0:240], E[:, qc * 128:(qc + 1) * 128], rpbT[:, :], start=True, stop=True), [rE], br)
                for half in range(2):
                    lo = half * 64
                    P.op("dve", lambda e, qc=qc, bank=bank, lo=lo, half=half: e.tensor_scalar(
                        T4[lo:lo + 64, :, :, qc], bank[lo:lo + 64, 0:240].rearrange("p (h d) -> p h d", d=15)[:, :, half:half + 14],
                        8.0, msk[lo:lo + 64, qc:qc + 1], ALU.mult, ALU.add), [rE], [rT] + br)
            P.dma("sp", lambda e: e.dma_start(out=D["natab_d"][:, :], in_=T[:]), [rT], [self.rnatab])

    def ret_layer(self, l):
        P, D = self.P, self.D
        rc = self.rconst
        with self.phase():
            dec = self.sb([128, 8], F32, "dec")
            negl = self.sb([128, 8], F32, "negl")
            lg = self.sb([128, 8], F32, "lg")
            retc = self.sb([128, 5, 128], F32, "retc")
            retcol = self.sb([128, 2], F32, "retcol")
            intra = self.sb([128, 8, 128], F32, "intra")
            qdec = self.sb([128, 8, 128], F32, "qdec")
            kdec = self.sb([128, 8], F32, "kdec")
            cdec = self.sb([128, 8], F32, "cdec")
            rdec = Reg()
            P.dma("sp", lambda e: e.dma_start(out=dec[:], in_=D["ret_decay"].partition_broadcast(128)), [], [rdec])
            P.dma("sp", lambda e: e.dma_start(out=retc[:], in_=D["retc"].rearrange("p (a i) -> p a i", a=5)), [], [rdec])
            P.dma("sp", lambda e: e.dma_start(out=retcol[:], in_=D["retcol"][:, :]), [], [rdec])
            P.op("act", lambda e: e.activation(negl[:], dec[:], AF.Exp, scale=-1.0), [rdec], [rdec])
            P.op("act", lambda e: e.activation(negl[:], negl[:], AF.Ln, bias=1.0), [rdec], [rdec])
            P.op("act", lambda e: e.mul(lg[:], negl[:], -1.0), [rdec], [rdec])
            for dh in range(8):
                d = dh // 4
                sc = lg[:, dh:dh + 1] if d == 0 else negl[:, dh:dh + 1]
                P.op("act", lambda e, dh=dh, sc=sc: e.activation(intra[:, dh, :], retc[:, 0, :], AF.Exp, scale=sc), [rdec], [rdec])
                P.op("dve", lambda e, dh=dh, d=d: e.tensor_tensor(intra[:, dh, :], intra[:, dh, :], retc[:, 1 + d, :], ALU.mult), [rdec], [rdec])
                P.op("act", lambda e, dh=dh, d=d: e.activation(qdec[:, dh, :], retc[:, 3 + d, :], AF.Exp, scale=lg[:, dh:dh + 1]), [rdec], [rdec])
                P.op("act", lambda e, dh=dh, d=d: e.activation(kdec[:, dh:dh + 1], retcol[:, d:d + 1], AF.Exp, scale=lg[:, dh:dh + 1]), [rdec], [rdec])
                P.op("act", lambda e, dh=dh: e.activation(cdec[:, dh:dh + 1], lg[:, dh:dh + 1], AF.Exp, scale=128.0), [rdec], [rdec])
            P.op("dve", lambda e: e.tensor_scalar(kdec[:], kdec[:], 1.0 / 16.0, None, ALU.mult), [rdec], [rdec])
            Wr = [dict(q=self.sb([128, 8, 256], BF16, "Wq"), k=self.sb([128, 8, 256], BF16, "Wk"), v=self.sb([128, 8, 512], BF16, "Wv"),
                       g=self.sb([128, 8, 512], BF16, "Wg"), rot=self.sb([128, 8, 512], BF16, "Wrot"), gn=self.sb([128, 512], F32, "gn"),
                       r=Reg(), rr=Reg()) for _ in range(2)]
            xt = [self.sb([128, 8, G], F32, "xt") for _ in range(3)]
            rxt = [Reg() for _ in range(3)]
            hTs = [self.sb([128, 8, G], BF16, "hT") for _ in range(2)]
            rhs_ = [Reg(), Reg()]
            csr = [self.sb([128, 2, 2, G], F32, "csr") for _ in range(3)]
            rcsr = [Reg() for _ in range(3)]
            qT = self.sb([128, 2, G], BF16, "qT"); kT = self.sb([128, 2, G], BF16, "kT")
            qdT = [self.sb([128, 2, G], BF16, "qdT") for _ in range(2)]
            rqT, rkT, rqd = Reg(), Reg(), [Reg(), Reg()]
            t1 = [self.sb([128, G], F32, "t1") for _ in range(2)]
            t2 = [self.sb([128, G], F32, "t2") for _ in range(2)]
            rt1, rt2 = [Reg(), Reg()], [Reg(), Reg()]
            kd = [self.sb([128, 256], BF16, "kd") for _ in range(4)]
            vv = [self.sb([128, 512], BF16, "vv") for _ in range(4)]
            sg = [self.sb([128, 512], F32, "sg") for _ in range(4)]
            sm = [self.sb([128, 128], BF16, "sm") for _ in range(4)]
            Usb = [self.sb([128, 2, 512], F32, "Usb") for _ in range(4)]
            yy = [self.sb([128, 512], F32, "yy") for _ in range(2)]
            rkd, rvv, rsg, rsm, rU = ([Reg() for _ in range(4)] for _ in range(5))
            ryy = [Reg(), Reg()]
            cctr4 = [0]
            S = self.sb([128, 2, 512], F32, "S")
            Sbf = self.sb([128, 2, 512], BF16, "Sbf")
            rS = [Reg(), Reg()]
            rSb = [Reg(), Reg()]
            stats = [self.sb([128, 6], F32, "bst") for _ in range(2)]
            mv = [self.sb([128, 2], F32, "mv") for _ in range(2)]
            rmv = [Reg(), Reg()]
            cctr = [0]
            tctr = [0]
            passes = [(d, h) for d in (1, 0) for h in range(4)]

            def wl(pi):
                d, h = passes[pi]
                w = Wr[pi % 2]
                src = D["ret_w"]
                P.dma("pool", lambda e: e.dma_start(out=w["q"][:], in_=src[:, h * 256:(h + 1) * 256].rearrange("(k p) n -> p k n", p=128)), [], [w["r"]])
                P.dma("pool", lambda e: e.dma_start(out=w["k"][:], in_=src[:, 1024 + h * 256:1024 + (h + 1) * 256].rearrange("(k p) n -> p k n", p=128)), [], [w["r"]])
                P.dma("pool", lambda e: e.dma_start(out=w["v"][:], in_=src[:, 2048 + h * 512:2048 + (h + 1) * 512].rearrange("(k p) n -> p k n", p=128)), [], [w["r"]])
                c0 = 4096 + d * 2048 + h * 512
                P.dma("pool", lambda e: e.dma_start(out=w["g"][:], in_=src[:, c0:c0 + 512].rearrange("(k p) n -> p k n", p=128)), [], [w["r"]])
                P.dma("sp", lambda e: e.dma_start(out=w["gn"][:], in_=D["ret_gn"][d:d + 1, h * 512:(h + 1) * 512].partition_broadcast(128)), [], [w["r"]])
                for wi, nm in enumerate(("q", "k")):
                    sv = w[nm][:].rearrange("p k (b t i) -> p k b t i", t=2, i=64)
                    dv = w["rot"][:, :, wi * 256:(wi + 1) * 256].rearrange("p k (b t i) -> p k b t i", t=2, i=64)
                    P.op("pool", lambda e, sv=sv, dv=dv: e.tensor_scalar(dv[:, :, :, 0, :], sv[:, :, :, 1, :], -1.0, None, ALU.mult), [w["r"]], [w["rr"]])
                    P.op("pool", lambda e, sv=sv, dv=dv: e.tensor_copy(dv[:, :, :, 1, :], sv[:, :, :, 0, :]), [w["r"], w["rr"]], [w["rr"]])

            def prefetch(g, s):
                self.load_x(g, xt[s][:], rxt[s])
                if g >= NPG:
                    gi = g - NPG
                    P.dma("sp", lambda e: e.dma_start(out=csr[s][:], in_=D["rope256"][:, :, :, gi * G:(gi + 1) * G].rearrange("c d p t -> p c d t")), [], [rcsr[s]])

            def group(pi, g, first, last, n2):
                d, h = passes[pi]
                dh = d * 4 + h
                w = Wr[pi % 2]
                sample = g >= NPG
                gi = g - NPG
                s3 = n2 % 3
                s = n2 % 2
                hT, rh = hTs[s], rhs_[s]
                if n2 + 2 < len(allitems):
                    prefetch(allitems[n2 + 2][1], (n2 + 2) % 3)
                if n2 + 1 < len(allitems):
                    g1 = allitems[n2 + 1][1]
                    self.modulate(g1, l, 0, xt[(n2 + 1) % 3], rxt[(n2 + 1) % 3], hTs[1 - s], rhs_[1 - s])
                qd, rqdx = qdT[s], rqd[s]
                for wi, (nm, dst, rdst) in enumerate((("q", qT, rqT), ("k", kT, rkT))):
                    for dc in range(2):
                        psa, pra = self.ps_half()
                        for kc in range(8):
                            P.op("pe", lambda e, nm=nm, dc=dc, kc=kc, psa=psa: e.matmul(psa, w[nm][:, kc, dc * 128:(dc + 1) * 128], hT[:, kc, :], start=(kc == 0), stop=(kc == 7)),
                                 [w["r"], rh], pra)
                        if not sample:
                            P.op("act", lambda e, dst=dst, dc=dc, psa=psa: e.copy(dst[:, dc, :], psa), [], [rdst] + pra)
                            continue
                        psb, prb = self.ps_half()
                        for kc in range(8):
                            P.op("pe", lambda e, wi=wi, dc=dc, kc=kc, psb=psb: e.matmul(psb, w["rot"][:, kc, wi * 256 + dc * 128:wi * 256 + (dc + 1) * 128], hT[:, kc, :],
                                                                                start=(kc == 0), stop=(kc == 7)), [w["rr"], rh], prb)
                        i = cctr[0] % 2
                        cctr[0] += 1
                        P.op("dve", lambda e, i=i, dc=dc, psa=psa: e.tensor_tensor(t1[i][:], psa, csr[s3][:, 0, dc, :], ALU.mult), [rcsr[s3]], [rt1[i]] + pra)
                        P.op("dve", lambda e, i=i, dc=dc, psb=psb: e.tensor_tensor(t2[i][:], psb, csr[s3][:, 1, dc, :], ALU.mult), [rcsr[s3]], [rt2[i]] + prb)
                        P.op("pool", lambda e, i=i, dst=dst, dc=dc: e.tensor_tensor(dst[:, dc, :], t1[i][:], t2[i][:], ALU.add), [rt1[i], rt2[i]], [rdst])
                qd_b = qdec[:, dh, :][:, None, :].to_broadcast([128, 4, 128])
                P.op("pool", lambda e: e.tensor_tensor(qd[:].rearrange("p c (b i) -> p (c b) i", i=128), qT[:].rearrange("p c (b i) -> p (c b) i", i=128), qd_b, ALU.mult),
                     [rqT, rdec], [rqdx])
                order = (0, 1) if d == 0 else (1, 0)
                idx = []
                for ci, cb in enumerate(order):
                    i = cctr4[0] % 4
                    cctr4[0] += 1
                    idx.append(i)
                for ci, cb in enumerate(order):
                    ts = slice(cb * 128, (cb + 1) * 128)
                    i = idx[ci]
                    bank, br = self.ps_full()
                    for kc in range(8):
                        P.op("pe", lambda e, kc=kc, bank=bank, ts=ts: e.matmul(bank[:, :], hT[:, kc, ts], w["v"][:, kc, :], start=(kc == 0), stop=(kc == 7)), [w["r"], rh], br)
                    P.op("act", lambda e, i=i, bank=bank: e.copy(vv[i][:], bank[:, :]), [], [rvv[i]] + br)
                    bank, br = self.ps_full()
                    for kc in range(8):
                        P.op("pe", lambda e, kc=kc, bank=bank, ts=ts: e.matmul(bank[:, :], hT[:, kc, ts], w["g"][:, kc, :], start=(kc == 0), stop=(kc == 7)), [w["r"], rh], br)
                    P.op("act", lambda e, i=i, bank=bank: e.activation(sg[i][:], bank[:, :], AF.Silu), [], [rsg[i]] + br)
                for ci, cb in enumerate(order):
                    ts = slice(cb * 128, (cb + 1) * 128)
                    i = idx[ci]
                    bank, br = self.ps_full()
                    bbf = bank[:, 0:128].bitcast(BF16)
                    for dc in range(2):
                        P.op("pe", lambda e, dc=dc, bbf=bbf, ts=ts: e.transpose(bbf[:, dc * 128:(dc + 1) * 128], kT[:, dc, ts], self.identb[:]), [rkT, rc], br)
                    P.op("dve", lambda e, i=i, bbf=bbf: e.tensor_scalar(kd[i][:], bbf, kdec[:, dh:dh + 1], None, ALU.mult), [rdec], [rkd[i]] + br)
                    pss, prs = self.ps_half()
                    for dc in range(2):
                        P.op("pe", lambda e, dc=dc, pss=pss, ts=ts: e.matmul(pss[:, 0:128], kT[:, dc, ts], qT[:, dc, ts], start=(dc == 0), stop=(dc == 1)), [rkT, rqT], prs)
                    P.op("dve", lambda e, i=i, pss=pss: e.tensor_tensor(sm[i][:], pss[:, 0:128], intra[:, dh, :], ALU.mult), [rdec], [rsm[i]] + prs)
                for ci, cb in enumerate(order):
                    i = idx[ci]
                    if sample and last and ci == 1:
                        continue
                    for dc in range(2):
                        bank, br = self.ps_full()
                        P.op("pe", lambda e, i=i, dc=dc, bank=bank: e.matmul(bank[:, :], kd[i][:, dc * 128:(dc + 1) * 128], vv[i][:], start=True, stop=True), [rkd[i], rvv[i]], br)
                        P.op("act", lambda e, i=i, dc=dc, bank=bank: e.copy(Usb[i][:, dc, :], bank[:, :]), [], [rU[i]] + br)

                def scan():
                    if first:
                        if sample:
                            r0 = (d * 4 + h) * 256
                            P.dma("sp", lambda e: e.dma_start(out=S[:], in_=D["sret"][r0:r0 + 256, :].rearrange("(c p) n -> p c n", p=128)), [], rS)
                        else:
                            P.op("pool", lambda e: e.memset(S[:], 0.0), [], rS)
                        for dc in range(2):
                            P.op("act", lambda e, dc=dc: e.copy(Sbf[:, dc, :], S[:, dc, :]), [rS[dc]], [rSb[dc]])
                    for ci, cb in enumerate(order):
                        ts = slice(cb * 128, (cb + 1) * 128)
                        i = idx[ci]
                        j = i % 2
                        bank, br = self.ps_full()
                        P.op("pe", lambda e, i=i, bank=bank: e.matmul(bank[:, :], sm[i][:], vv[i][:], start=True, stop=False), [rsm[i], rvv[i]], br)
                        for dc in range(2):
                            P.op("pe", lambda e, dc=dc, bank=bank, ts=ts: e.matmul(bank[:, :], qd[:, dc, ts], Sbf[:, dc, :], start=False, stop=(dc == 1)), [rqdx, rSb[dc]], br)
                        if not (sample and last and ci == 1):
                            for dc in range(2):
                                P.op("dve", lambda e, i=i, dc=dc: e.scalar_tensor_tensor(S[:, dc, :], S[:, dc, :], cdec[:, dh:dh + 1], Usb[i][:, dc, :], ALU.mult, ALU.add),
                                     [rdec, rSb[dc], rU[i]], [rS[dc]])
                                P.op("act", lambda e, dc=dc: e.copy(Sbf[:, dc, :], S[:, dc, :]), [rS[dc]], [rSb[dc]])
                        P.op("dve", lambda e, j=j, bank=bank: e.bn_stats(stats[j][:], bank[:, :]), [], [rmv[j]] + br)
                        P.op("dve", lambda e, j=j: e.bn_aggr(mv[j][:], stats[j][:]), [rmv[j]], [rmv[j]])
                        P.op("dve", lambda e, j=j: e.tensor_scalar(mv[j][:, 1:2], mv[j][:, 1:2], LN_EPS, None, ALU.add), [rmv[j]], [rmv[j]])
                        P.op("dve", lambda e, j=j: e.reciprocal(mv[j][:, 1:2], mv[j][:, 1:2]), [rmv[j]], [rmv[j]])
                        P.op("act", lambda e, j=j: e.activation(mv[j][:, 1:2], mv[j][:, 1:2], AF.Sqrt), [rmv[j]], [rmv[j]])
                        P.op("dve", lambda e, j=j, bank=bank: e.tensor_scalar(yy[j][:], bank[:, :], mv[j][:, 0:1], mv[j][:, 1:2], ALU.subtract, ALU.mult), [rmv[j]], [ryy[j]] + br)
                        P.op("pool", lambda e, j=j: e.tensor_tensor(yy[j][:], yy[j][:], w["gn"][:], ALU.mult), [ryy[j], w["r"]], [ryy[j]])
                        P.op("pool", lambda e, j=j, i=i: e.tensor_tensor(yy[j][:], yy[j][:], sg[i][:], ALU.mult), [ryy[j], rsg[i]], [ryy[j]])
                        tk0 = g * G + cb * 128
                        P.dma("sp", lambda e, j=j, tk0=tk0: e.dma_start(out=D["Yd"][d, tk0:tk0 + 128, h * 512:(h + 1) * 512], in_=yy[j][:]), [ryy[j]], [self.yreg[d][g]])
                    if (not sample) and last:
                        row0 = ((g * 2 + d) * 4 + h) * 256
                        self.fin.append(P.dma("sp", lambda e: e.dma_start(out=D["o_ret"][row0:row0 + 256, :].rearrange("(c p) n -> p c n", p=128), in_=S[:]), rS, []))
                return scan

            self.yreg = [[Reg() for _ in range(NPG + NSG)] for _ in range(2)]
            wl(0)
            prev = None
            allitems = []
            for pi in range(8):
                d = passes[pi][0]
                gl = list(range(NPG, NPG + NSG))
                if d == 1:
                    gl = gl[::-1]
                for n_, (g, first, last) in enumerate([(g, True, True) for g in range(NPG)] + [(g, k == 0, k == NSG - 1) for k, g in enumerate(gl)]):
                    allitems.append((pi, g, first, last, n_))
            prefetch(allitems[0][1], 0)
            prefetch(allitems[1][1], 1)
            self.modulate(allitems[0][1], l, 0, xt[0], rxt[0], hTs[0], rhs_[0])
            for n2, (pi, g, first, last, n_) in enumerate(allitems):
                sc = group(pi, g, first, last, n2)
                if prev is not None:
                    prev()
                prev = sc
                if n_ == 0 and pi + 1 < 8:
                    wl(pi + 1)
            prev()
        with self.phase():
            ya = [self.sb([128, 2048], F32, "ya") for _ in range(2)]
            yb = [self.sb([128, 2048], F32, "yb") for _ in range(2)]
            ys = [self.sb([128, 2048], BF16, "ys") for _ in range(2)]
            rya, ryb, rys = [Reg(), Reg()], [Reg(), Reg()], [Reg(), Reg()]
            yT = [self.sb([128, 16, G], BF16, "yT") for _ in range(2)]
            ryT = [Reg(), Reg()]
            it = 0
            for g in range(NPG + NSG):
                sT = g % 2
                for b in range(2):
                    i = it % 2
                    it += 1
                    tk0 = g * G + b * 128
                    P.dma("sp", lambda e, i=i, tk0=tk0: e.dma_start(out=ya[i][:], in_=D["Yd"][0, tk0:tk0 + 128, :]), [self.yreg[0][g]], [rya[i]])
                    P.dma("sp", lambda e, i=i, tk0=tk0: e.dma_start(out=yb[i][:], in_=D["Yd"][1, tk0:tk0 + 128, :]), [self.yreg[1][g]], [ryb[i]])
                    P.op("dve", lambda e, i=i: e.tensor_tensor(ys[i][:], ya[i][:], yb[i][:], ALU.add), [rya[i], ryb[i]], [rys[i]])
                    for k2 in range(2):
                        bank, br = self.ps_full()
                        bbf = bank[:, :].bitcast(BF16)
                        for kk in range(8):
                            c = k2 * 8 + kk
                            P.op("pe", lambda e, i=i, c=c, kk=kk, bbf=bbf: e.transpose(bbf[:, kk * 128:(kk + 1) * 128], ys[i][:, c * 128:(c + 1) * 128], self.identb[:]), [rys[i], rc], br)
                        if k2 == 0:
                            P.op("act", lambda e, sT=sT, b=b, k2=k2, bbf=bbf: e.copy(yT[sT][:, k2 * 8:(k2 + 1) * 8, b * 128:(b + 1) * 128], bbf.rearrange("p (c t) -> p c t", t=128)), [], [ryT[sT]] + br)
                        else:
                            P.op("dve", lambda e, sT=sT, b=b, k2=k2, bbf=bbf: e.tensor_copy(yT[sT][:, k2 * 8:(k2 + 1) * 8, b * 128:(b + 1) * 128], bbf.rearrange("p (c t) -> p c t", t=128)), [], [ryT[sT]] + br)
                t0 = g * G
                P.dma("sp", lambda e, sT=sT, t0=t0: e.dma_start(out=D["mixR_d"][:, :, t0:t0 + G], in_=yT[sT][:]), [ryT[sT]], [self.mixreg[g]])


def build(seq=None):
    kb = KB()
    kb.declare()
    kb.prep()
    kb.na_table()
    if seq is None:
        seq = []
        for l in range(4):
            seq += [("mix", l), ("mlp", l)]
    i = 0
    while i < len(seq):
        kind, l = seq[i]
        if kind == "mix" and i + 1 < len(seq) and seq[i + 1] == ("mlp", l):
            kb.mixer(l, with_mlp=True)
            i += 2
        elif kind == "mlp":
            st = ExitStack()
            prev = kb.cur
            kb.cur = st
            pre = kb.mlp_prefetch(l)
            kb.mlp(l, pre)
            kb.cur = prev
            st.close()
            i += 1
        else:
            kb.mixer(l)
            i += 1
    kb.final()
    kb.P.op("pool", lambda e: e.memset(kb.ident[0:1, 0:1], 1.0), [], [])
    kb.P.flush(final_waits=kb.fin)
    kb.top.close()
    return kb


def _per_core_inputs(inp, consts, c):
    f = lambda a: np.ascontiguousarray(a, dtype=np.float32)
    m = {}
    m["xp"] = f(inp["x_prompt"][4 * c:4 * c + 4].reshape(1024, 1024))
    m["xs"] = f(inp["x_sample"][c])
    m["cswak"] = f(inp["cache_swa_k"][c, 0].reshape(512, 256)); m["cswav"] = f(inp["cache_swa_v"][c, 0].reshape(512, 256))
    m["cnak"] = f(inp["cache_na_k"][c, 0].reshape(512, 1024)); m["cnav"] = f(inp["cache_na_v"][c, 0].reshape(512, 1024))
    m["sret"] = f(inp["state_ret"][c, 0].reshape(2048, 512))
    m["cgqak"] = f(inp["cache_gqa_k"][c, 0].reshape(512, 256)); m["cgqav"] = f(inp["cache_gqa_v"][c, 0].reshape(512, 256))
    m["cT_h"] = f(np.concatenate([inp["c_ctx"].reshape(8, 128).T, inp["c"][c].reshape(8, 128).T], axis=1))
    return m


def _shared_inputs(inp, consts):
    f = lambda a: np.ascontiguousarray(a, dtype=np.float32)
    m = {}
    m["lngT_h"] = f(inp["ln_g"].reshape(64, 128).T); m["lnbT_h"] = f(inp["ln_b"].reshape(64, 128).T)
    m["modbT_h"] = f(inp["mod_b"].reshape(192, 128).T)
    p = np.arange(128) % 64
    partner = np.where((p % 32) < 16, p + 16, p - 16)
    qn, kn = inp["gqa_q_norm"][0], inp["gqa_k_norm"][0]
    m["gqn"] = f(np.stack([qn[p], qn[partner], kn[p], kn[partner]], axis=1))
    m["mod_w"] = f(inp["mod_w"]); m["mlp_w1"] = f(inp["mlp_w1"]); m["mlp_w2"] = f(inp["mlp_w2"])
    order = []
    for cch in range(8):
        for b in range(2):
            order.append(4 * (2 * (cch // 4) + b) + cch % 4)
    cols = np.concatenate([np.arange(h * 64, (h + 1) * 64) for h in order] + [np.arange(1024, 1536)])
    m["swa_wqkv_p"] = f(inp["swa_wqkv"][0][:, cols]); m["gqa_wqkv_p"] = f(inp["gqa_wqkv"][0][:, cols])
    m["swa_wo"] = f(inp["swa_wo"][0]); m["gqa_wo"] = f(inp["gqa_wo"][0]); m["swa_sink"] = f(inp["swa_sink"].reshape(1, 16))
    m["na_wqkv"] = f(inp["na_wqkv"][0]); m["na_wo"] = f(inp["na_wo"][0])
    m["na_rpbT"] = f(inp["na_rpb"][0].reshape(240, 31).T)
    m["ret_w"] = f(inp["ret_wqkvg"][0]); m["ret_wo"] = f(inp["ret_wo"][0])
    m["ret_decay"] = f(inp["ret_decay"].reshape(1, 8)); m["ret_gn"] = f(inp["ret_gn_g"][0])
    for k, v in consts.items():
        m[k] = f(v)
    return m


_CACHE = {}


def kernel(**inp):
    inp = {k: np.asarray(v) for k, v in inp.items()}
    seq = None
    env = os.environ.get("KSEQ")
    if env is not None:
        seq = [(s[:3], int(s[3:])) for s in env.split(",") if s]
    key = str(seq)
    if key not in _CACHE:
        _CACHE[key] = build(seq)
    kb = _CACHE[key]
    consts = _host_consts()
    shared = _shared_inputs(inp, consts)
    in_maps = []
    ncores = int(os.environ.get("KCORES", "8"))
    for c in range(ncores):
        m = dict(shared)
        m.update(_per_core_inputs(inp, consts, c))
        in_maps.append({k: m[k] for k in kb.in_names})
    res = run_bass_kernel_spmd(kb.nc, in_maps, core_ids=list(range(ncores)))
    R = list(res.results)
    while len(R) < 8:
        R.append(R[0])
    cat = lambda n: np.concatenate([np.asarray(R[c][n]) for c in range(8)], axis=0)
    y_p = cat("y_p").reshape(32, 256, 1024)
    y_s = cat("y_s").reshape(8, 4096, 1024)
    swak = cat("o_swak").reshape(32, 1, 256, 4, 64); swav = cat("o_swav").reshape(32, 1, 256, 4, 64)
    nak = cat("o_nak").reshape(32, 1, 256, 16, 64); nav = cat("o_nav").reshape(32, 1, 256, 16, 64)
    ret = cat("o_ret").reshape(32, 1, 2, 4, 256, 512)
    gqak = cat("o_gqak").reshape(32, 1, 256, 4, 64); gqav = cat("o_gqav").reshape(32, 1, 256, 4, 64)
    return tuple(np.ascontiguousarray(a, dtype=np.float32) for a in (y_p, y_s, swak, swav, nak, nav, ret, gqak, gqav))
```

```python
import os
import numpy as np
from contextlib import ExitStack
import concourse.bass as bass
import concourse.mybir as mybir
from concourse.bass_utils import run_bass_kernel_spmd

F32 = mybir.dt.float32
BF16 = mybir.dt.bfloat16
ALU = mybir.AluOpType
AF = mybir.ActivationFunctionType

ENGS = ("pe", "act", "dve", "pool", "sp")
NDSEM = 24
ALPHA = 8.0 ** 0.25
LN_EPS = 1e-5
RMS_EPS = 1e-6
G = 256
NPG = 4
NSG = 16
NTOK = 5120


class Reg:
    __slots__ = ("name", "w", "rs", "drs")

    def __init__(self, name=""):
        self.name = name
        self.w = None
        self.rs = {}
        self.drs = []


class Ins:
    __slots__ = ("eng", "fn", "deps", "dma", "marked", "semval", "dsem", "dval", "emitted")

    def __init__(self, eng, fn, dma):
        self.eng = eng
        self.fn = fn
        self.dma = dma
        self.deps = []
        self.marked = False
        self.semval = 0
        self.dsem = None
        self.dval = 0
        self.emitted = False


class Prog:
    def __init__(self, nc, stack):
        self.nc = nc
        self.lists = {e: [] for e in ENGS}
        self.sem = {e: stack.enter_context(nc.semaphore("s_" + e)) for e in ENGS if e != "sp"}
        self.count = {e: 0 for e in ENGS}
        self.dsems = {q: [stack.enter_context(nc.semaphore("d_%s%d" % (q, i))) for i in range(NDSEM)]
                      for q in ("sp", "pool")}
        self.dcount = {"sp": 0, "pool": 0}
        self.dhist = {"sp": [], "pool": []}
        self.waited = {e: {} for e in ENGS}
        self.last_marked = {e: None for e in ENGS}
        self.n_ins = 0
        self.n_wait = 0

    def _add(self, eng, fn, reads, writes, dma):
        ins = Ins(eng, fn, dma)
        deps = {}

        def dep(d):
            if d is None or d is ins:
                return
            if (not d.dma) and (not dma) and d.eng == eng and eng == "pe":
                return
            if (not d.dma) and d.emitted and not d.marked:
                d = self.last_marked[d.eng]
                if d is None:
                    return
            deps[id(d)] = d

        for r in reads:
            dep(r.w)
        for w in writes:
            dep(w.w)
            for d in w.rs.values():
                dep(d)
            for d in w.drs:
                dep(d)
        if dma:
            q = eng
            j = self.dcount[q]
            self.dcount[q] += 1
            ins.dsem = self.dsems[q][j % NDSEM]
            ins.dval = 16 * (j // NDSEM + 1)
            if j >= NDSEM:
                dep(self.dhist[q][j - NDSEM])
            self.dhist[q].append(ins)
        for d in deps.values():
            if not d.dma:
                d.marked = True
        ins.deps = list(deps.values())
        for r in reads:
            if dma:
                r.drs.append(ins)
            else:
                r.rs[eng] = ins
        for w in writes:
            w.w = ins
            w.rs = {}
            w.drs = []
        self.lists[eng].append(ins)
        self.n_ins += 1
        return ins

    def op(self, eng, fn, reads=(), writes=()):
        return self._add(eng, fn, reads, writes, False)

    def dma(self, q, fn, reads=(), writes=()):
        return self._add(q, fn, reads, writes, True)

    def flush(self, final_waits=()):
        nc = self.nc
        for e in ENGS:
            lst = [i for i in self.lists[e] if not i.dma]
            if lst:
                lst[-1].marked = True
        for e in ENGS:
            for i in self.lists[e]:
                if i.marked and not i.dma:
                    self.count[e] += 1
                    i.semval = self.count[e]
        lists = self.lists
        self.lists = {e: [] for e in ENGS}
        handles = {"pe": "tensor", "act": "scalar", "dve": "vector", "pool": "gpsimd", "sp": "sync"}
        with nc.Block() as block:
            for e in ENGS:
                def body(eng, e=e):
                    wt = self.waited[e]
                    for ins in lists[e]:
                        for d in ins.deps:
                            if d.dma:
                                key = id(d.dsem)
                                if wt.get(key, 0) < d.dval:
                                    eng.wait_ge(d.dsem, d.dval)
                                    wt[key] = d.dval
                                    self.n_wait += 1
                            else:
                                if wt.get(d.eng, 0) < d.semval:
                                    eng.wait_ge(self.sem[d.eng], d.semval)
                                    wt[d.eng] = d.semval
                                    self.n_wait += 1
                        bi = ins.fn(eng)
                        if ins.dma:
                            bi.then_inc(ins.dsem, 16)
                        elif ins.marked:
                            bi.then_inc(self.sem[e], 1)
                            self.last_marked[e] = ins
                        ins.emitted = True
                    if e == "sp":
                        for q in ("sp", "pool"):
                            for d in self.dhist[q][-NDSEM:]:
                                key = id(d.dsem)
                                if wt.get(key, 0) < d.dval:
                                    eng.wait_ge(d.dsem, d.dval)
                                    wt[key] = d.dval
                        for d in final_waits:
                            key = id(d.dsem)
                            if wt.get(key, 0) < d.dval:
                                eng.wait_ge(d.dsem, d.dval)
                                wt[key] = d.dval
                getattr(block, handles[e])(body)


def _host_consts():
    c = {}
    t = np.arange(4096)
    row, col = (t // 64).astype(np.float64), (t % 64).astype(np.float64)
    p = np.arange(128) % 64
    inv16 = 10000.0 ** (-(np.arange(16)) / 16.0)
    pos = np.where((p < 32)[:, None], row[None, :], col[None, :])
    ang = pos * inv16[p % 16][:, None]
    c["rope64"] = np.stack([np.cos(ang), np.sin(ang)]).astype(np.float32)
    inv64 = 10000.0 ** (-(np.arange(64)) / 64.0)
    p2 = np.arange(128) % 64
    a0 = row[None, :] * inv64[p2][:, None]
    a1 = col[None, :] * inv64[p2][:, None]
    c["rope256"] = np.stack([np.stack([np.cos(a0), np.cos(a1)]), np.stack([np.sin(a0), np.sin(a1)])]).astype(np.float32)
    j = np.arange(128)[:, None].astype(np.float64)
    i = np.arange(128)[None, :].astype(np.float64)
    ret = np.zeros((128, 5, 128), np.float32)
    ret[:, 0] = i - j
    ret[:, 1] = (i >= j) / 16.0
    ret[:, 2] = (j >= i) / 16.0
    ret[:, 3] = np.broadcast_to(i + 1.0, (128, 128))
    ret[:, 4] = np.broadcast_to(128.0 - i, (128, 128))
    c["retc"] = ret.reshape(128, 640)
    rc = np.zeros((128, 2), np.float32)
    rc[:, 0] = 127.0 - np.arange(128)
    rc[:, 1] = np.arange(128)
    c["retcol"] = rc
    kk = np.arange(128)[:, None]
    qq = np.arange(128)[None, :]
    mp = np.where(kk >= qq, 0.0, -1e30)
    mn = np.where(kk <= qq, 0.0, -1e30)
    c["swamask"] = np.stack([np.tile(mp, (1, 4)), np.tile(mn, (1, 4))]).astype(np.float32)
    E = np.zeros((31, 64, 2, 64), np.float32)
    for qc in range(64):
        for kc in range(64):
            jj = kc - qc + 15
            if 0 <= jj <= 30:
                E[jj, qc, :, kc] = 1.0
    c["naE"] = E.reshape(31, 8192)
    cs = np.clip(np.arange(64) - 8, 0, 48)
    kc = np.arange(64)[:, None]
    ok = (kc >= cs[None, :]) & (kc < cs[None, :] + 16)
    c["namask"] = np.tile(np.where(ok, 0.0, -1e30), (2, 1)).astype(np.float32)
    return c


class KB:
    def __init__(self, nsub=8):
        self.nsub = nsub
        self.nc = bass.Bass("TRN2", target_bir_lowering=False)
        self.top = ExitStack()
        self.cur = self.top
        self.P = Prog(self.nc, self.top)
        self.uid = 0
        self.D = {}
        self.fin = []
        self.in_names = []
        self.out_names = []

    def din(self, n, shape, dt=F32):
        self.D[n] = self.nc.dram_tensor(n, list(shape), dt, kind="ExternalInput").ap()
        self.in_names.append(n)
        return self.D[n]

    def dout(self, n, shape):
        self.D[n] = self.nc.dram_tensor(n, list(shape), F32, kind="ExternalOutput").ap()
        self.out_names.append(n)
        return self.D[n]

    def dscr(self, n, shape, dt=F32):
        self.D[n] = self.nc.dram_tensor(n, list(shape), dt, kind="Internal").ap()
        return self.D[n]

    def sb(self, shape, dt=F32, name="t"):
        self.uid += 1
        return self.cur.enter_context(self.nc.sbuf_tensor("%s_%d" % (name, self.uid), list(shape), dt))

    def phase(self):
        kb = self

        class _Ph:
            def __enter__(s):
                s.st = ExitStack()
                s.prev = kb.cur
                kb.cur = s.st
                return s

            def __exit__(s, et, ev, tb):
                if et is None:
                    kb.P.flush()
                kb.cur = s.prev
                s.st.close()
                return False
        return _Ph()

    def init_psum(self):
        self.banks = [self.top.enter_context(self.nc.psum_tensor("psb%d" % i, [128, 512], F32)) for i in range(8)]
        self.bregs = [[Reg("b%d" % i)] for i in range(8)]
        self.full_ctr = 0
        self.ntrans = 7
        self.half_ctr = 0

    def ps_full(self):
        b = self.full_ctr % self.ntrans
        self.full_ctr += 1
        return self.banks[b], self.bregs[b]

    def ps_half(self):
        s = self.half_ctr % (2 * self.ntrans)
        self.half_ctr += 1
        b, h = s % self.ntrans, s // self.ntrans
        return self.banks[b][:, h * 256:(h + 1) * 256], self.bregs[b]

    def xgrp(self, g):
        return g * G

    def load_x(self, g, dst, reg):
        xT = self.D["xT_d"]
        t0 = g * G
        return self.P.dma("sp", lambda e: e.dma_start(out=dst, in_=xT[:, :, t0:t0 + G].rearrange("c p t -> p c t")), [self.xreg[g]], [reg])

    def store_x(self, g, src, reg):
        xT = self.D["xT_d"]
        t0 = g * G
        return self.P.dma("sp", lambda e: e.dma_start(out=xT[:, :, t0:t0 + G].rearrange("c p t -> p c t"), in_=src), [reg], [self.xreg[g]])

    def modv(self, g, l, j, c):
        grp = 0 if g < NPG else 1
        return self.modT[:, grp, l, j * 8 + c:j * 8 + c + 1]

    def modp1(self, g, l, which, c):
        grp = 0 if g < NPG else 1
        return self.modP1[:, grp, l, which, c:c + 1]

    def modulate(self, g, l, which, xt, rx, hT, rh):
        P = self.P
        for c in range(8):
            sc = self.modp1(g, l, which, c)
            sh = self.modv(g, l, 3 * which, c)
            P.op("dve", lambda e, c=c, sc=sc, sh=sh: e.tensor_scalar(hT[:, c, :], xt[:, c, :], sc, sh, ALU.mult, ALU.add),
                 [rx, self.rmod], [rh])

    def ln_store(self, g, l, i, zz, rz0, rz1):
        P = self.P
        nc = self.nc
        st = self.lnst
        self.ln_p1(zz, rz0, rz1)
        self.ln_p2(g, l, i, zz, rz0, rz1)

    def ln_p1(self, zz, rz0, rz1):
        P = self.P
        red, rred = self.lnst["red"], self.lnst["rred"]
        P.op("act", lambda e: e.activation(zz[:, 1], zz[:, 0], AF.Square), [rz0], [rz1])
        P.op("dve", lambda e: e.tensor_reduce(red[:], zz[:].rearrange("p a c t -> p a t c"), mybir.AxisListType.X, ALU.add), [rz0, rz1], [rred])

    def ln_p2(self, g, l, i, zz, rz0, rz1):
        P = self.P
        st = self.lnst
        bank, br = self.ps_full()
        red, rred = st["red"], st["rred"]
        P.op("pe", lambda e: e.matmul(bank[:, :].rearrange("p (a t) -> p a t", a=2), self.ones32[:, :], red[:], start=True, stop=True), [rred, self.rconst], br)
        m, msq, var, rstd = st["m"], st["msq"], st["var"], st["rstd"]
        rm, rv, rr = st["rm"], st["rv"], st["rr"]
        P.op("act", lambda e: e.mul(m[:], bank[:, 0:256], 1.0 / 1024.0), [], [rm] + br)
        P.op("dve", lambda e: e.tensor_tensor(msq[:], m[:], m[:], ALU.mult), [rm], [rv])
        P.op("dve", lambda e: e.scalar_tensor_tensor(var[:], bank[:, 256:512], 1.0 / 1024.0, msq[:], ALU.mult, ALU.subtract), [rv], [rv] + br)
        P.op("dve", lambda e: e.tensor_scalar(var[:], var[:], LN_EPS, None, ALU.add), [rv], [rv])
        P.op("dve", lambda e: e.reciprocal(var[:], var[:]), [rv], [rv])
        P.op("act", lambda e: e.activation(rstd[:], var[:], AF.Sqrt), [rv], [rr])
        mb = m[:, None, :].to_broadcast([128, 8, G])
        rb = rstd[:, None, :].to_broadcast([128, 8, G])
        P.op("dve", lambda e: e.tensor_tensor(zz[:, 0], zz[:, 0], mb, ALU.subtract), [rz0, rm], [rz0])
        P.op("dve", lambda e: e.tensor_tensor(zz[:, 0], zz[:, 0], rb, ALU.mult), [rz0, rr], [rz0])
        gb = self.lngT[:, l * 16 + i * 8:l * 16 + i * 8 + 8][:, :, None].to_broadcast([128, 8, G])
        bb = self.lnbT[:, l * 16 + i * 8:l * 16 + i * 8 + 8][:, :, None].to_broadcast([128, 8, G])
        P.op("pool", lambda e: e.tensor_tensor(zz[:, 0], zz[:, 0], gb, ALU.mult), [rz0, self.rmod], [rz0])
        P.op("pool", lambda e: e.tensor_tensor(zz[:, 1], zz[:, 0], bb, ALU.add), [rz0, self.rmod], [rz1])
        self.store_x(g, zz[:, 1], rz1)

    def alloc_ln(self):
        st = {}
        for n in ("m", "msq", "var", "rstd"):
            st[n] = self.sb([128, G], F32, "ln" + n)
        st["rm"], st["rv"], st["rr"] = Reg(), Reg(), Reg()
        st["red"] = self.sb([128, 2, G], F32, "lnred")
        st["rred"] = Reg()
        self.lnst = st

    def wload(self, dst, src, reg, reads=()):
        return self.P.dma("pool", lambda e: e.dma_start(out=dst, in_=src), list(reads), [reg])

    def declare(self):
        d = self.din
        d("xp", [1024, 1024]); d("xs", [4096, 1024])
        d("cswak", [512, 256]); d("cswav", [512, 256]); d("cnak", [512, 1024]); d("cnav", [512, 1024])
        d("sret", [2048, 512]); d("cgqak", [512, 256]); d("cgqav", [512, 256])
        d("cT_h", [128, 16]); d("lngT_h", [128, 64]); d("lnbT_h", [128, 64]); d("modbT_h", [128, 192]); d("gqn", [128, 4])
        d("mod_w", [4, 1024, 6144]); d("mlp_w1", [4, 1024, 4096]); d("mlp_w2", [4, 4096, 1024])
        d("swa_wqkv_p", [1024, 1536]); d("swa_wo", [1024, 1024]); d("swa_sink", [1, 16])
        d("na_wqkv", [1024, 3072]); d("na_wo", [1024, 1024]); d("na_rpbT", [31, 240])
        d("ret_w", [1024, 8192]); d("ret_wo", [2048, 1024]); d("ret_decay", [1, 8]); d("ret_gn", [2, 2048])
        d("gqa_wqkv_p", [1024, 1536]); d("gqa_wo", [1024, 1024])
        d("rope64", [2, 128, 4096]); d("rope256", [2, 2, 128, 4096]); d("retc", [128, 640]); d("retcol", [128, 2])
        d("swamask", [2, 128, 512]); d("naE", [31, 8192]); d("namask", [128, 64])
        o = self.dout
        o("y_p", [1024, 1024]); o("y_s", [4096, 1024])
        o("o_swak", [1024, 256]); o("o_swav", [1024, 256]); o("o_nak", [1024, 1024]); o("o_nav", [1024, 1024])
        o("o_ret", [8192, 512]); o("o_gqak", [1024, 256]); o("o_gqav", [1024, 256])
        s = self.dscr
        s("xT_d", [8, 128, NTOK]); s("mixA_d", [64, 16, NTOK], BF16); s("mixR_d", [128, 16, NTOK], BF16)
        s("Yd", [2, NTOK, 2048]); s("natab_d", [128, 14336], BF16)
        self.xreg = [Reg("xg%d" % g) for g in range(NPG + NSG)]
        self.mixreg = [Reg("mx%d" % g) for g in range(NPG + NSG)]
        self.ident = self.sb([128, 128], F32, "ident")
        self.identb = self.sb([128, 128], BF16, "identb")
        self.ones32 = self.sb([128, 128], F32, "ones32")
        self.blk32 = self.sb([128, 128], F32, "blk32")
        self.sel64 = self.sb([128, 128], F32, "sel64")
        self.cT = self.sb([128, 16], F32, "cT")
        self.lngT = self.sb([128, 64], F32, "lngT")
        self.lnbT = self.sb([128, 64], F32, "lnbT")
        self.modbT = self.sb([128, 192], F32, "modbT")
        self.gqn = self.sb([128, 4], F32, "gqn")
        self.modT = self.sb([128, 2, 4, 48], F32, "modT")
        self.modP1 = self.sb([128, 2, 4, 2, 8], F32, "modP1")
        self.rconst = Reg("const")
        self.rmod = Reg("mod")
        self.init_psum()

    def prep(self):
        P, nc, D = self.P, self.nc, self.D
        rc, rmod = self.rconst, self.rmod
        with self.phase():
            ident, identb, ones32, blk32 = self.ident, self.identb, self.ones32, self.blk32
            P.op("pool", lambda e: e.memset(ident[:], 0.0), [], [rc])
            P.op("pool", lambda e: e.affine_select(out=ident[:], in_=ident[:], pattern=[[-1, 128]], compare_op=ALU.not_equal,
                                                   fill=1.0, base=0, channel_multiplier=1), [rc], [rc])
            P.op("pool", lambda e: e.memset(ones32[:], 1.0), [], [rc])
            P.op("pool", lambda e: e.memset(blk32[:], 0.0), [], [rc])
            P.op("pool", lambda e: e.memset(blk32[0:64, 0:64], 1.0), [rc], [rc])
            P.op("pool", lambda e: e.memset(blk32[64:128, 64:128], 1.0), [rc], [rc])
            P.op("pool", lambda e: e.memset(self.sel64[:], 0.0), [], [rc])
            P.op("pool", lambda e: e.memset(self.sel64[64:65, :], 1.0), [rc], [rc])
            P.op("dve", lambda e: e.tensor_copy(identb[:], ident[:]), [rc], [rc])
            for dst, src in ((self.cT, "cT_h"), (self.lngT, "lngT_h"), (self.lnbT, "lnbT_h"), (self.modbT, "modbT_h"), (self.gqn, "gqn")):
                P.dma("sp", lambda e, dst=dst, src=src: e.dma_start(out=dst[:], in_=D[src][:, :]), [], [rmod])
            sil = self.sb([128, 8, 2], F32, "sil")
            rs = Reg()
            P.op("act", lambda e: e.activation(sil[:, :, 0], self.cT[:, 0:8], AF.Silu), [rmod], [rs])
            P.op("act", lambda e: e.activation(sil[:, :, 1], self.cT[:, 8:16], AF.Silu), [rmod, rs], [rs])
            mw = [self.sb([128, 8, 768], F32, "mw") for _ in range(2)]
            rmw = [Reg(), Reg()]
            pm, prm = self.banks[7], self.bregs[7]
            it = 0
            for l in range(4):
                for nb in range(8):
                    s = it % 2
                    it += 1
                    P.dma("sp", lambda e, l=l, nb=nb, s=s: e.dma_start(
                        out=mw[s][:], in_=D["mod_w"][l, :, nb * 768:(nb + 1) * 768].rearrange("(k p) n -> p k n", p=128)), [], [rmw[s]])
                    for n6 in range(6):
                        n = nb * 6 + n6
                        for kc in range(8):
                            P.op("pe", lambda e, s=s, n6=n6, kc=kc, l=l, n=n: e.matmul(
                                pm[:, l * 96 + n * 2:l * 96 + n * 2 + 2], mw[s][:, kc, n6 * 128:(n6 + 1) * 128], sil[:, kc, :],
                                start=(kc == 0), stop=(kc == 7)), [rmw[s], rs], prm)
            for grp in range(2):
                P.op("dve", lambda e, grp=grp: e.tensor_tensor(
                    self.modT[:, grp].rearrange("p l n -> p (l n)"),
                    pm[:, 0:384].rearrange("p (x g) -> p x g", g=2)[:, :, grp], self.modbT[:, :], ALU.add), [rmod], [rmod] + prm)
            for grp in range(2):
                for which, j in ((0, 1), (1, 4)):
                    P.op("dve", lambda e, grp=grp, which=which, j=j: e.tensor_scalar(
                        self.modP1[:, grp, :, which, :], self.modT[:, grp, :, j * 8:(j + 1) * 8], 1.0, None, ALU.add), [rmod], [rmod])
            xin = [self.sb([128, 2, 1024], F32, "xin") for _ in range(2)]
            rxin = [Reg(), Reg()]
            xt = [self.sb([128, 8, G], F32, "xt0") for _ in range(2)]
            rxt = [Reg(), Reg()]
            for g in range(NPG + NSG):
                s = g % 2
                src = D["xp"][g * G:(g + 1) * G, :] if g < NPG else D["xs"][(g - NPG) * G:(g - NPG + 1) * G, :]
                P.dma("sp", lambda e, s=s, src=src: e.dma_start(out=xin[s][:], in_=src.rearrange("(b p) f -> p b f", p=128)), [], [rxin[s]])
                for c2 in range(4):
                    bank, br = self.ps_full()
                    for cc in range(2):
                        c = c2 * 2 + cc
                        for b in range(2):
                            P.op("pe", lambda e, s=s, c=c, cc=cc, b=b, bank=bank: e.transpose(
                                bank[:, cc * 256 + b * 128:cc * 256 + b * 128 + 128], xin[s][:, b, c * 128:(c + 1) * 128], ident[:]),
                                [rxin[s], rc], br)
                    eng = "act" if c2 % 2 == 0 else "dve"
                    if eng == "act":
                        P.op("act", lambda e, s=s, c2=c2, bank=bank: e.copy(xt[s][:, 2 * c2:2 * c2 + 2, :], bank[:, :].rearrange("p (a t) -> p a t", a=2)), [], [rxt[s]] + br)
                    else:
                        P.op("dve", lambda e, s=s, c2=c2, bank=bank: e.tensor_copy(xt[s][:, 2 * c2:2 * c2 + 2, :], bank[:, :].rearrange("p (a t) -> p a t", a=2)), [], [rxt[s]] + br)
                self.store_x(g, xt[s][:], rxt[s])

    def final(self):
        P, D = self.P, self.D
        with self.phase():
            xt = [self.sb([128, 8, G], F32, "xtf") for _ in range(2)]
            rxt = [Reg(), Reg()]
            yo = [self.sb([128, 2, 1024], F32, "yo") for _ in range(2)]
            ryo = [Reg(), Reg()]
            for g in range(NPG + NSG):
                s = g % 2
                self.load_x(g, xt[s][:], rxt[s])
                for b in range(2):
                    for c4 in range(2):
                        bank, br = self.ps_full()
                        for cc in range(4):
                            c = c4 * 4 + cc
                            P.op("pe", lambda e, s=s, c=c, cc=cc, b=b, bank=bank: e.transpose(
                                bank[:, cc * 128:(cc + 1) * 128], xt[s][:, c, b * 128:(b + 1) * 128], self.ident[:]), [rxt[s], self.rconst], br)
                        if c4 == 0:
                            P.op("act", lambda e, s=s, b=b, c4=c4, bank=bank: e.copy(yo[s][:, b, c4 * 512:(c4 + 1) * 512], bank[:, :]), [], [ryo[s]] + br)
                        else:
                            P.op("dve", lambda e, s=s, b=b, c4=c4, bank=bank: e.tensor_copy(yo[s][:, b, c4 * 512:(c4 + 1) * 512], bank[:, :]), [], [ryo[s]] + br)
                dst = D["y_p"][g * G:(g + 1) * G, :] if g < NPG else D["y_s"][(g - NPG) * G:(g - NPG + 1) * G, :]
                self.fin.append(P.dma("sp", lambda e, s=s, dst=dst: e.dma_start(out=dst.rearrange("(b p) f -> p b f", p=128), in_=yo[s][:]), [ryo[s]], []))

    def mlp(self, l):
        P, D = self.P, self.D
        with self.phase():
            W1 = self.sb([128, 8, 4096], BF16, "W1")
            W2 = self.sb([128, 32, 1024], BF16, "W2")
            rW1 = [[Reg() for _ in range(2)] for _ in range(8)]
            rW2 = [Reg() for _ in range(8)]
            for kc in range(8):
                for h in range(2):
                    self.wload(W1[:, kc, h * 2048:(h + 1) * 2048], D["mlp_w1"][l, kc * 128:(kc + 1) * 128, h * 2048:(h + 1) * 2048], rW1[kc][h])
            for f4 in range(8):
                self.wload(W2[:, f4 * 4:(f4 + 1) * 4, :], D["mlp_w2"][l, f4 * 512:(f4 + 1) * 512, :].rearrange("(f p) n -> p f n", p=128), rW2[f4])
            xt = [self.sb([128, 8, G], F32, "xt") for _ in range(2)]
            rxt = [Reg(), Reg()]
            hT = self.sb([128, 8, G], BF16, "hT")
            rh = Reg()
            hid = self.sb([128, 32, G], BF16, "hid")
            rhid = [Reg() for _ in range(32)]
            zz = self.sb([128, 2, 8, G], F32, "zz")
            rz0, rz1 = Reg(), Reg()
            rt = [self.sb([128, G], F32, "rt") for _ in range(3)]
            rrt = [Reg() for _ in range(3)]
            self.alloc_ln()
            ng = NPG + NSG
            self.load_x(0, xt[0][:], rxt[0])
            self.modulate(0, l, 1, xt[0], rxt[0], hT, rh)

            def step3(g, f):
                ps, pr = self.ps_half()
                for kc in range(8):
                    P.op("pe", lambda e, f=f, kc=kc, ps=ps: e.matmul(ps, W1[:, kc, f * 128:(f + 1) * 128], hT[:, kc, :],
                                                             start=(kc == 0), stop=(kc == 7)), [rW1[kc][f // 16], rh], pr)
                k = f % 3
                P.op("act", lambda e, k=k, ps=ps: e.activation(rt[k][:], ps, AF.Relu), [], [rrt[k]] + pr)
                eng = "dve" if f % 2 == 0 else "pool"
                P.op(eng, lambda e, k=k, f=f: e.tensor_tensor(hid[:, f, :], rt[k][:], rt[k][:], ALU.mult), [rrt[k]], [rhid[f]])

            for g in range(ng):
                s = g % 2
                if g + 1 < ng:
                    self.load_x(g + 1, xt[1 - s][:], rxt[1 - s])
                for f in range(8):
                    step3(g, f)
                if g > 0:
                    self.ln_p2(g - 1, l, 1, zz, rz0, rz1)
                for f in range(8, 32):
                    step3(g, f)
                P.op("act", lambda e, s=s: e.mul(zz[:, 0], xt[s][:], ALPHA), [rxt[s]], [rz0])
                for n in range(8):
                    ps, pr = self.ps_half()
                    for f in range(32):
                        P.op("pe", lambda e, f=f, n=n, ps=ps: e.matmul(ps, W2[:, f, n * 128:(n + 1) * 128], hid[:, f, :],
                                                                start=(f == 0), stop=(f == 31)), [rW2[f // 4], rhid[f]], pr)
                    g2 = self.modv(g, l, 5, n)
                    P.op("dve", lambda e, n=n, ps=ps, g2=g2: e.scalar_tensor_tensor(zz[:, 0, n, :], ps, g2, zz[:, 0, n, :], ALU.mult, ALU.add),
                         [rz0, self.rmod], [rz0] + pr)
                self.ln_p1(zz, rz0, rz1)
                if g + 1 < ng:
                    self.modulate(g + 1, l, 1, xt[1 - s], rxt[1 - s], hT, rh)
            self.ln_p2(ng - 1, l, 1, zz, rz0, rz1)

    def mixer(self, l):
        m = l % 4
        if m == 2:
            self.ret_layer(l)
            self.outproj(l, "ret")
        else:
            kind = ("swa", "na", None, "gqa")[m]
            self.attn_layer(l, kind)
            self.outproj(l, kind)

    def outproj(self, l, kind):
        P, D = self.P, self.D
        with self.phase():
            if kind == "ret":
                nj, mixd, wsrc = 16, D["mixR_d"], D["ret_wo"].rearrange("(c p) n -> p c n", p=128)
            else:
                wn = {"swa": "swa_wo", "na": "na_wo", "gqa": "gqa_wo"}[kind]
                nj, mixd, wsrc = 8, D["mixA_d"].rearrange("d (c two) t -> d c two t", two=2), D[wn].rearrange("(c p) n -> p c n", p=128)
            Wo = self.sb([128, nj, 1024], BF16, "Wo")
            rWo = [Reg() for _ in range(4)]
            q4 = nj // 4
            for q in range(4):
                self.wload(Wo[:, q * q4:(q + 1) * q4, :], wsrc[:, q * q4:(q + 1) * q4, :], rWo[q])
            xt = [self.sb([128, 8, G], F32, "xt") for _ in range(2)]
            rxt = [Reg(), Reg()]
            mx = [self.sb([128, nj, G], BF16, "mx") for _ in range(2)]
            rmx = [Reg(), Reg()]
            rmx2 = [[Reg(), Reg()], [Reg(), Reg()]]
            zzs = [self.sb([128, 2, 8, G], F32, "zz") for _ in range(2)]
            rzs = [(Reg(), Reg()) for _ in range(2)]
            self.alloc_ln()
            ng = NPG + NSG

            def loads(g):
                s = g % 2
                self.load_x(g, xt[s][:], rxt[s])
                t0 = g * G
                if kind == "ret":
                    P.dma("sp", lambda e: e.dma_start(out=mx[s][:], in_=mixd[:, :, t0:t0 + G]), [self.mixreg[g]], [rmx[s]])
                else:
                    for two in range(2):
                        P.dma("sp", lambda e, two=two: e.dma_start(out=mx[s][two * 64:(two + 1) * 64], in_=mixd[:, :, two, t0:t0 + G]), [self.mixreg[g]], [rmx2[s][two]])

            def mm(g, n):
                s = g % 2
                zz, (rz0, rz1) = zzs[s], rzs[s]
                ps, pr = self.ps_half()
                for j in range(nj):
                    P.op("pe", lambda e, j=j, n=n, ps=ps, s=s: e.matmul(ps, Wo[:, j, n * 128:(n + 1) * 128], mx[s][:, j, :],
                                                                   start=(j == 0), stop=(j == nj - 1)), [rWo[j // q4], rmx[s]] + rmx2[s], pr)
                g1 = self.modv(g, l, 2, n)
                P.op("dve", lambda e, n=n, ps=ps, g1=g1, zz=zz: e.scalar_tensor_tensor(zz[:, 0, n, :], ps, g1, zz[:, 0, n, :], ALU.mult, ALU.add),
                     [rz0, self.rmod], [rz0] + pr)
            loads(0)
            for g in range(ng):
                s = g % 2
                zz, (rz0, rz1) = zzs[s], rzs[s]
                if g + 1 < ng:
                    loads(g + 1)
                P.op("act", lambda e, s=s, zz=zz: e.mul(zz[:, 0], xt[s][:], ALPHA), [rxt[s]], [rz0])
                for n in range(4):
                    mm(g, n)
                if g > 0:
                    self.ln_p2(g - 1, l, 0, zzs[1 - s], rzs[1 - s][0], rzs[1 - s][1])
                for n in range(4, 8):
                    mm(g, n)
                self.ln_p1(zz, rz0, rz1)
            sl = (ng - 1) % 2
            self.ln_p2(ng - 1, l, 0, zzs[sl], rzs[sl][0], rzs[sl][1])

    def attn_layer(self, l, kind):
        P, D, nc = self.P, self.D, self.nc
        cfg = {"swa": dict(nk=2, vh=4, rope=True, rms=False, w="swa_wqkv_p", ck="cswak", cv="cswav", ok="o_swak", ov="o_swav", ns=16, roll=True),
               "na": dict(nk=8, vh=16, rope=False, rms=False, w="na_wqkv", ck="cnak", cv="cnav", ok="o_nak", ov="o_nav", ns=4, roll=True),
               "gqa": dict(nk=2, vh=4, rope=True, rms=True, w="gqa_wqkv_p", ck="cgqak", cv="cgqav", ok="o_gqak", ov="o_gqav", ns=16, roll=False)}[kind]
        nk, vh, ns = cfg["nk"], cfg["vh"], cfg["ns"]
        kvw = nk * 128
        nqk = 8 + nk
        self.ntrans = 4
        with self.phase():
            rc = self.rconst
            Wqk = self.sb([128, 8, nqk * 128], BF16, "Wqk")
            Wv = self.sb([128, 8, kvw], BF16, "Wv")
            rW = [Reg() for _ in range(8)]
            for kc in range(8):
                self.wload(Wqk[:, kc, :], D[cfg["w"]][kc * 128:(kc + 1) * 128, 0:nqk * 128], rW[kc])
            rWv = [Reg() for _ in range(8)]
            for kc in range(8):
                self.wload(Wv[:, kc, :], D[cfg["w"]][kc * 128:(kc + 1) * 128, nqk * 128:nqk * 128 + kvw], rWv[kc])
            Wrot, rWrot = None, Reg()
            if cfg["rope"]:
                Wrot = self.sb([128, 8, nqk * 128], BF16, "Wrot")
                src = Wqk[:].rearrange("p k (b t i) -> p k b t i", t=2, i=16)
                dst = Wrot[:].rearrange("p k (b t i) -> p k b t i", t=2, i=16)
                P.op("pool", lambda e: e.tensor_scalar(dst[:, :, :, 0, :], src[:, :, :, 1, :], -1.0, None, ALU.mult), rW, [rWrot])
                P.op("pool", lambda e: e.tensor_copy(dst[:, :, :, 1, :], src[:, :, :, 0, :]), rW + [rWrot], [rWrot])
            hoist = kind != "na"
            nxt = 3 if hoist else 1
            nht = 2 if hoist else 1
            xt = [self.sb([128, 8, G], F32, "xt") for _ in range(nxt)]
            rxt = [Reg() for _ in range(nxt)]
            hTs = [self.sb([128, 8, G], BF16, "hT") for _ in range(nht)]
            rhs_ = [Reg() for _ in range(nht)]
            QTe = [self.sb([128, 8, G], BF16, "QTe") for _ in range(2)]
            QTo = [self.sb([128, 8, G], BF16, "QTo") for _ in range(2)]
            rQ = [Reg() for _ in range(2)]
            for qi in range(2):
                P.op("pool", lambda e, qi=qi: e.memset(QTe[qi][:], 0.0), [], [rQ[qi]])
                P.op("pool", lambda e, qi=qi: e.memset(QTo[qi][:], 0.0), [], [rQ[qi]])
            KTp = self.sb([128, nk, G], BF16, "KTp")
            Vp = self.sb([128, 2, vh, 65], BF16, "Vp")
            rKp, rVp = Reg(), Reg()
            KTs = self.sb([128, nk, ns * G], BF16, "KTs")
            Vs = self.sb([128, ns * 2, vh, 65], BF16, "Vs")
            rK = [Reg() for _ in range(ns)]
            rV = [Reg() for _ in range(ns)]
            cKT = self.sb([128, nk, 512], BF16, "cKT")
            cV = self.sb([128, 4, vh, 65], BF16, "cV")
            rcK, rcV = Reg(), Reg()
            stg = self.sb([128, 4, 1024], F32, "stg")
            rstg = Reg()
            NPT = 6
            self.PT = [self.sb([128, 256 if kind == "na" else 512], BF16, "PT") for _ in range(NPT)]
            self.rPT = [Reg() for _ in range(NPT)]
            self.pt_ctr = 0
            OT = self.sb([64, 16, G], BF16, "OT")
            rOT = Reg()
            NW = 3
            NMAX = 256 if kind == "na" else 512
            dns = [self.sb([128, NMAX], F32, "dn") for _ in range(NW)]
            rdns = [Reg() for _ in range(NW)]
            for wi in range(NW):
                P.op("pool", lambda e, wi=wi: e.memset(dns[wi][:], 0.0), [], [rdns[wi]])
            bcss = [self.sb([128, NMAX], F32, "bcs") for _ in range(NW)]
            rbcss = [Reg() for _ in range(NW)]
            tmp = {n: [self.sb([128, G], F32, n) for _ in range(2)] for n in (("t1", "t2", "sq", "rstd") if cfg["rope"] else ())}
            rtmp = {n: [Reg(), Reg()] for n in tmp}
            tctr = [0]
            cs = [self.sb([128, 2, G], F32, "cs") for _ in range(3 if cfg["rope"] else 0)]
            rcs = [Reg(), Reg(), Reg()]
            P.op("pool", lambda e: e.memset(Vp[:, :, :, 64:65], 1.0), [], [rVp])
            P.op("pool", lambda e: e.memset(Vs[:, :, :, 64:65], 1.0), [], rV)
            P.op("pool", lambda e: e.memset(cV[:, :, :, 64:65], 1.0), [], [rcV])
            exps, rsink = None, Reg()
            if kind == "swa":
                exps = self.sb([128, 16], F32, "exps")
                P.dma("sp", lambda e: e.dma_start(out=exps[:], in_=D["swa_sink"].partition_broadcast(128)), [], [rsink])
                P.op("act", lambda e: e.activation(exps[:], exps[:], AF.Exp), [rsink], [rsink])
                mk = self.sb([128, 2, 512], BF16, "mk")
                rmk = Reg()
                self.wload(mk[:], D["swamask"].rearrange("m p n -> p m n"), rmk)
            if kind == "na":
                tab = self.sb([128, 14336], BF16, "tab")
                rtab = Reg()
                P.dma("sp", lambda e: e.dma_start(out=tab[:], in_=D["natab_d"][:, :]), [self.rnatab], [rtab])
            P.dma("sp", lambda e: e.dma_start(out=stg[:, :, 0:kvw], in_=D[cfg["ck"]].rearrange("(b p) f -> p b f", p=128)), [], [rstg])
            for j in range(nk):
                bank, br = self.ps_full()
                for tb in range(4):
                    P.op("pe", lambda e, j=j, tb=tb, bank=bank: e.transpose(bank[:, tb * 128:(tb + 1) * 128], stg[:, tb, j * 128:(j + 1) * 128], self.ident[:]),
                         [rstg, rc], br)
                P.op("act" if j % 2 == 0 else "dve",
                     (lambda e, j=j, bank=bank: e.copy(cKT[:, j, :], bank[:, :])) if j % 2 == 0 else (lambda e, j=j, bank=bank: e.tensor_copy(cKT[:, j, :], bank[:, :])),
                     [], [rcK] + br)
            P.dma("sp", lambda e: e.dma_start(out=stg[:, :, 0:kvw], in_=D[cfg["cv"]].rearrange("(b p) f -> p b f", p=128)), [], [rstg])
            P.op("act", lambda e: e.copy(cV[:, :, :, 0:64], stg[:, :, 0:kvw].rearrange("p b (h d) -> p b h d", d=64)), [rstg], [rcV])

            def qk_post(sample, is_k, psa, pra, psb, prb, dests, rdest, csg, rcsg, k32, rk32):
                i = tctr[0] % 2
                tctr[0] += 1
                if not cfg["rms"]:
                    if not (cfg["rope"] and sample):
                        if k32 is not None:
                            P.op("dve", lambda e: e.tensor_copy(k32, psa), [], [rk32] + pra)
                            P.op("act", lambda e: e.copy(dests[0][0], k32), [rk32], [rdest])
                        else:
                            for di, (dst, lo, hi) in enumerate(dests):
                                if (i + di) % 2 == 0:
                                    P.op("act", lambda e, dst=dst, lo=lo, hi=hi: e.copy(dst, psa[lo:hi]), [], [rdest] + pra)
                                else:
                                    P.op("dve", lambda e, dst=dst, lo=lo, hi=hi: e.tensor_copy(dst, psa[lo:hi]), [], [rdest] + pra)
                        return
                    t1, t2 = tmp["t1"][i], tmp["t2"][i]
                    r1, r2 = rtmp["t1"][i], rtmp["t2"][i]
                    P.op("dve", lambda e: e.tensor_tensor(t1[:], psa, csg[:, 0, :], ALU.mult), [rcsg], [r1] + pra)
                    P.op("dve", lambda e: e.tensor_tensor(t2[:], psb, csg[:, 1, :], ALU.mult), [rcsg], [r2] + prb)
                    for dst, lo, hi in dests:
                        P.op("pool", lambda e, dst=dst, lo=lo, hi=hi: e.tensor_tensor(dst, t1[lo:hi], t2[lo:hi], ALU.add), [r1, r2], [rdest])
                    return
                sq, rstd = tmp["sq"][i], tmp["rstd"][i]
                rsq, rrs = rtmp["sq"][i], rtmp["rstd"][i]
                gc = self.gqn[:, 2:3] if is_k else self.gqn[:, 0:1]
                gcp = self.gqn[:, 3:4] if is_k else self.gqn[:, 1:2]
                P.op("act", lambda e: e.activation(sq[:], psa, AF.Square), [], [rsq] + pra)
                pss, prs = self.ps_half()
                P.op("pe", lambda e: e.matmul(pss, self.blk32[:, :], sq[:], start=True, stop=True), [rsq, rc], prs)
                P.op("dve", lambda e: e.tensor_scalar(rstd[:], pss, 1.0 / 64.0, RMS_EPS, ALU.mult, ALU.add), [], [rrs] + prs)
                P.op("dve", lambda e: e.reciprocal(rstd[:], rstd[:]), [rrs], [rrs])
                P.op("act", lambda e: e.activation(rstd[:], rstd[:], AF.Sqrt), [rrs], [rrs])
                if not sample:
                    if k32 is not None:
                        P.op("dve", lambda e: e.scalar_tensor_tensor(k32, psa, gc, rstd[:], ALU.mult, ALU.mult), [rrs, self.rmod], [rk32] + pra)
                        P.op("act", lambda e: e.copy(dests[0][0], k32), [rk32], [rdest])
                    else:
                        for dst, lo, hi in dests:
                            P.op("dve", lambda e, dst=dst, lo=lo, hi=hi: e.scalar_tensor_tensor(dst, psa[lo:hi], gc[lo:hi], rstd[lo:hi], ALU.mult, ALU.mult), [rrs, self.rmod], [rdest] + pra)
                    return
                t1, t2 = tmp["t1"][i], tmp["t2"][i]
                r1, r2 = rtmp["t1"][i], rtmp["t2"][i]
                P.op("dve", lambda e: e.scalar_tensor_tensor(t1[:], psa, gc, csg[:, 0, :], ALU.mult, ALU.mult), [rcsg, self.rmod], [r1] + pra)
                P.op("dve", lambda e: e.scalar_tensor_tensor(t2[:], psb, gcp, csg[:, 1, :], ALU.mult, ALU.mult), [rcsg, self.rmod], [r2] + prb)
                P.op("pool", lambda e: e.tensor_tensor(t1[:], t1[:], t2[:], ALU.add), [r1, r2], [r1])
                for dst, lo, hi in dests:
                    P.op("pool", lambda e, dst=dst, lo=lo, hi=hi: e.tensor_tensor(dst, t1[lo:hi], rstd[lo:hi], ALU.mult), [r1, rrs], [rdest])

            k32T = self.sb([128, nk, G], F32, "k32T")
            rk32 = Reg()
            ktok = stg[:, 2:4, 0:kvw]
            rktok = rstg
            v32 = stg[:, 0:2, 0:kvw]
            rv32 = rstg
            xs_ctr = [0]

            plist = [(g, True, True) for g in range(NPG)]
            if cfg["roll"]:
                plist += [(NPG + gi, True, True) for gi in range(NSG)]
            else:
                plist += [(NPG + gi, False, True) for gi in range(NSG)] + [(NPG + gi, True, False) for gi in range(NSG)]
            pn = [0]

            def prefetch(n):
                g = plist[n][0]
                self.load_x(g, xt[n % nxt][:], rxt[n % nxt])
                if cfg["rope"] and g >= NPG:
                    gi = g - NPG
                    P.dma("sp", lambda e: e.dma_start(out=cs[n % 3][:], in_=D["rope64"][:, :, gi * G:(gi + 1) * G].rearrange("c p t -> p c t")), [], [rcs[n % 3]])

            def do_mod(n):
                g = plist[n][0]
                self.modulate(g, l, 0, xt[n % nxt], rxt[n % nxt], hTs[n % nht], rhs_[n % nht])
            if hoist:
                prefetch(0)
                prefetch(1)
                do_mod(0)

            def proj(g, do_q, do_kv):
                n = pn[0]
                pn[0] += 1
                assert plist[n] == (g, do_q, do_kv), (n, plist[n], g, do_q, do_kv)
                sample = g >= NPG
                gi = g - NPG
                if hoist:
                    if n + 2 < len(plist):
                        prefetch(n + 2)
                    if n + 1 < len(plist):
                        do_mod(n + 1)
                else:
                    prefetch(n)
                    do_mod(n)
                hT, rh = hTs[n % nht], rhs_[n % nht]
                roped = cfg["rope"] and sample
                csg, rcsg = None, None
                if roped:
                    csg, rcsg = cs[n % 3], rcs[n % 3]
                js = (list(range(8)) if do_q else []) + (list(range(8, 8 + nk)) if do_kv else [])
                qs = g % 2
                for j in js:
                    psa, pra = self.ps_half()
                    for kc in range(8):
                        P.op("pe", lambda e, j=j, kc=kc, psa=psa: e.matmul(psa, Wqk[:, kc, j * 128:(j + 1) * 128], hT[:, kc, :], start=(kc == 0), stop=(kc == 7)),
                             [rW[kc], rh], pra)
                    psb, prb = None, None
                    if roped:
                        psb, prb = self.ps_half()
                        for kc in range(8):
                            P.op("pe", lambda e, j=j, kc=kc, psb=psb: e.matmul(psb, Wrot[:, kc, j * 128:(j + 1) * 128], hT[:, kc, :], start=(kc == 0), stop=(kc == 7)),
                                 [rWrot, rh], prb)
                    if j < 8:
                        qk_post(sample, False, psa, pra, psb, prb, [(QTe[qs][0:64, j, :], 0, 64), (QTo[qs][64:128, j, :], 64, 128)], rQ[qs], csg, rcsg, None, None)
                    elif sample:
                        sl = gi % ns
                        qk_post(True, True, psa, pra, psb, prb, [(KTs[:, j - 8, sl * G:(sl + 1) * G], 0, 128)], rK[sl], csg, rcsg, None, None)
                    else:
                        qk_post(False, True, psa, pra, psb, prb, [(KTp[:, j - 8, :], 0, 128)], rKp, csg, rcsg, k32T[:, j - 8, :], rk32)
                if not do_kv:
                    return
                for b in range(2):
                    for cb in range((kvw + 511) // 512):
                        w = min(512, kvw - cb * 512)
                        bank, br = self.ps_full()
                        for kc in range(8):
                            P.op("pe", lambda e, b=b, cb=cb, w=w, kc=kc, bank=bank: e.matmul(bank[:, 0:w], hT[:, kc, b * 128:(b + 1) * 128], Wv[:, kc, cb * 512:cb * 512 + w],
                                                                                     start=(kc == 0), stop=(kc == 7)), [rWv[kc], rh], br)
                        nh = w // 64
                        h0 = cb * 8
                        if sample:
                            sl = gi % ns
                            P.op("act", lambda e, b=b, sl=sl, h0=h0, nh=nh, w=w, bank=bank: e.copy(Vs[:, sl * 2 + b, h0:h0 + nh, 0:64], bank[:, 0:w].rearrange("p (h d) -> p h d", d=64)),
                                 [], [rV[sl]] + br)
                        else:
                            P.op("act", lambda e, b=b, h0=h0, nh=nh, w=w, bank=bank: e.copy(Vp[:, b, h0:h0 + nh, 0:64], bank[:, 0:w].rearrange("p (h d) -> p h d", d=64)),
                                 [], [rVp] + br)
                            P.op("dve", lambda e, b=b, cb=cb, w=w, bank=bank: e.tensor_copy(v32[:, b, cb * 512:cb * 512 + w], bank[:, 0:w]), [], [rv32] + br)
                if not sample:
                    self.fin.append(P.dma("sp", lambda e: e.dma_start(out=D[cfg["ov"]][g * G:(g + 1) * G, :].rearrange("(b p) f -> p b f", p=128), in_=v32), [rv32], []))
                    for b in range(2):
                        for j4 in range((nk + 3) // 4):
                            nj = min(4, nk - j4 * 4)
                            bank, br = self.ps_full()
                            for jj in range(nj):
                                j = j4 * 4 + jj
                                P.op("pe", lambda e, b=b, j=j, jj=jj, bank=bank: e.transpose(bank[:, jj * 128:(jj + 1) * 128], k32T[:, j, b * 128:(b + 1) * 128], self.ident[:]),
                                     [rk32, rc], br)
                            P.op("dve", lambda e, b=b, j4=j4, nj=nj, bank=bank: e.tensor_copy(ktok[:, b, j4 * 512:j4 * 512 + nj * 128], bank[:, 0:nj * 128]), [], [rktok] + br)
                    self.fin.append(P.dma("sp", lambda e: e.dma_start(out=D[cfg["ok"]][g * G:(g + 1) * G, :].rearrange("(b p) f -> p b f", p=128), in_=ktok), [rktok], []))

            acc_ctr = [0]

            def core(ai, qap, rq, N, a3, chunks, sinkap, dest):
                acc, racc = self.banks[4 + ai], self.bregs[4 + ai]
                dn, rdn, bcs, rbcs = dns[ai], rdns[ai], bcss[ai], rbcss[ai]
                n = len(chunks)
                pts = []
                for i in range(n + 1):
                    if i < n:
                        ch = chunks[i]
                        bank, br = self.ps_full()
                        pb, kp = ch["pb"], ch["kp"]
                        ni = ch.get("n", N)
                        qa = ch.get("q", qap)
                        sv = bank[pb:pb + kp, 0:ni]
                        sv3 = sv.rearrange("p (a t) -> p a t", a=a3) if a3 > 1 else sv
                        hasb = ch.get("bias") is not None
                        P.op("pe", lambda e, ch=ch, sv3=sv3, hasb=hasb, qa=qa: e.matmul(sv3, ch["kt"], qa, start=True, stop=not hasb), [ch["rk"], rq], br)
                        if hasb:
                            P.op("pe", lambda e, ch=ch, sv=sv: e.matmul(sv, ch["bl"], ch["bias"], start=False, stop=True), [ch["rb"], rc], br)
                        k = self.pt_ctr % len(self.PT)
                        self.pt_ctr += 1
                        pt, rpt = self.PT[k], self.rPT[k]
                        P.op("act", lambda e, pt=pt, pb=pb, kp=kp, sv=sv, ni=ni: e.activation(pt[pb:pb + kp, 0:ni], sv, AF.Exp, scale=0.125), [], [rpt] + br)
                        if ch.get("zero") is not None:
                            lo, hi = ch["zero"]
                            P.op("pool", lambda e, pt=pt, lo=lo, hi=hi, ni=ni: e.memset(pt[lo:hi, 0:ni], 0.0), [], [rpt])
                        pts.append((pt, rpt))
                    if i >= 1:
                        ch = chunks[i - 1]
                        pt, rpt = pts[i - 1]
                        pb, kp = ch["pb"], ch["kp"]
                        ni = ch.get("n", N)
                        c0 = ch.get("c0", 0)
                        P.op("pe", lambda e, ch=ch, pt=pt, pb=pb, kp=kp, i=i, ni=ni, c0=c0: e.matmul(acc[0:65, c0:c0 + ni], ch["v"], pt[pb:pb + kp, 0:ni], start=(i == 1), stop=(i == n)),
                             [ch["rv"], rpt], racc)
                    yield
                v3 = (lambda ap: ap.rearrange("p (a t) -> p a t", a=a3)) if a3 > 1 else (lambda ap: ap)
                if sinkap is not None:
                    P.op("dve", lambda e: e.tensor_tensor(v3(dn[64:65, 0:N]), v3(acc[64:65, 0:N]), sinkap, ALU.add), [rsink], [rdn] + racc)
                else:
                    P.op("dve", lambda e: e.tensor_copy(dn[64:65, 0:N], acc[64:65, 0:N]), [], [rdn] + racc)
                P.op("dve", lambda e: e.reciprocal(dn[64:65, 0:N], dn[64:65, 0:N]), [rdn], [rdn])
                yield
                yield
                bcb, rbc = self.banks[7], self.bregs[7]
                P.op("pe", lambda e: e.matmul(bcb[:, 0:N], self.sel64[:, :], dn[:, 0:N], start=True, stop=True), [rdn, rc], rbc)
                P.op("act", lambda e: e.copy(bcs[0:64, 0:N], bcb[0:64, 0:N]), [], [rbcs] + rbc)
                yield
                P.op("dve", lambda e: e.tensor_tensor(dest, v3(acc[0:64, 0:N]), v3(bcs[0:64, 0:N]), ALU.mult), [rbcs], [rOT] + racc)

            def run_units(gens):
                active = {}
                pending = list(gens)
                while active or pending:
                    for slot in range(NW):
                        if slot not in active and pending:
                            active[slot] = pending.pop(0)(slot)
                    for slot in list(active):
                        try:
                            next(active[slot])
                        except StopIteration:
                            del active[slot]

            def attend(g):
                sample = g >= NPG
                gi = g - NPG
                qs = g % 2
                rq = rQ[qs]
                units = []
                if kind in ("swa", "gqa"):
                    for kvh in range(4):
                        b64 = (kvh % 2) * 64
                        c0 = 4 * (kvh // 2)
                        for qb in range(2):
                            qap = (QTe if kvh % 2 == 0 else QTo)[qs][:, c0:c0 + 4, qb * 128:(qb + 1) * 128]
                            chunks = []
                            if not sample:
                                for kb in range(2):
                                    chunks.append(dict(kt=KTp[:, kvh // 2, kb * 128:(kb + 1) * 128], rk=rKp, v=Vp[:, kb, kvh, :], rv=rVp, pb=0, kp=128))
                            else:
                                i = 2 * gi + qb
                                blks = [i - 1, i, i + 1] if kind == "swa" else list(range(32))
                                for bi in blks:
                                    if bi < 0 or bi > 31:
                                        continue
                                    ch = dict(kt=KTs[:, kvh // 2, bi * 128:(bi + 1) * 128], rk=rK[bi // 2], v=Vs[:, bi, kvh, :], rv=rV[bi // 2], pb=0, kp=128)
                                    if kind == "swa" and bi != i:
                                        ch["bias"] = mk[:, 0 if bi < i else 1, :]
                                        ch["bl"] = self.identb[:, :]
                                        ch["rb"] = rmk
                                    chunks.append(ch)
                                for tb in range(4):
                                    chunks.append(dict(kt=cKT[:, kvh // 2, tb * 128:(tb + 1) * 128], rk=rcK, v=cV[:, tb, kvh, :], rv=rcV, pb=0, kp=128))
                            sinkap = None
                            if kind == "swa":
                                sinkap = exps[64:65, 4 * kvh:4 * kvh + 4][:, :, None].to_broadcast([1, 4, 128])
                            units.append(lambda ai, qap=qap, chunks=chunks, sinkap=sinkap, kvh=kvh, qb=qb: core(ai, qap, rq, 512, 4, chunks, sinkap, OT[0:64, 4 * kvh:4 * kvh + 4, qb * 128:(qb + 1) * 128]))
                else:
                    for h in range(16):
                        b64 = (h % 2) * 64
                        if not sample:
                            qap = (QTe if h % 2 == 0 else QTo)[qs][:, h // 2, :]
                            chunks = [dict(kt=KTp[:, h // 2, kb * 128:(kb + 1) * 128], rk=rKp, v=Vp[:, kb, h, :], rv=rVp, pb=0, kp=128) for kb in range(2)]
                            units.append(lambda ai, qap=qap, chunks=chunks, h=h: core(ai, qap, rq, 256, 1, chunks, None, OT[0:64, h, :]))
                        else:
                            qsel = (QTe if h % 2 == 0 else QTo)[qs]
                            chunks = []
                            for tb in range(4):
                                chunks.append(dict(kt=cKT[:, h // 2, tb * 128:(tb + 1) * 128], rk=rcK, v=cV[:, tb, h, :], rv=rcV, pb=0, kp=128,
                                                   q=qsel[:, h // 2, :], c0=0, n=256))
                            for rl in range(4):
                                r = 4 * gi + rl
                                r0 = min(max(r - 4, 0), 56)
                                for m in range(r0 // 2, (r0 + 7) // 2 + 1):
                                    a_in = r0 <= 2 * m <= r0 + 7
                                    b_in = r0 <= 2 * m + 1 <= r0 + 7
                                    ee = 2 * m - r + 7
                                    assert 0 <= ee <= 13, (r, m, ee)
                                    sl = (m // 2) % ns
                                    lb = m % 2
                                    ch = dict(kt=KTs[:, h // 2, sl * G + lb * 128:sl * G + lb * 128 + 128], rk=rK[sl],
                                              v=Vs[:, sl * 2 + lb, h, :], rv=rV[sl], pb=0, kp=128,
                                              bias=tab[:, (h * 14 + ee) * 64:(h * 14 + ee + 1) * 64], bl=self.identb[:, :], rb=rtab,
                                              q=qsel[:, h // 2, rl * 64:(rl + 1) * 64], c0=rl * 64, n=64)
                                    if not a_in:
                                        ch["zero"] = (0, 64)
                                    if not b_in:
                                        ch["zero"] = (64, 128)
                                    chunks.append(ch)
                            units.append(lambda ai, chunks=chunks, h=h: core(ai, None, rq, 256, 1, chunks, None, OT[0:64, h, :]))
                run_units(units)
                t0 = g * G
                P.dma("sp", lambda e: e.dma_start(out=D["mixA_d"][:, :, t0:t0 + G], in_=OT[:]), [rOT], [self.mixreg[g]])

            for g in range(NPG):
                proj(g, True, True)
                attend(g)
            if cfg["roll"]:
                proj(NPG, True, True)
                for gi in range(NSG):
                    if gi + 1 < NSG:
                        proj(NPG + gi + 1, True, True)
                    attend(NPG + gi)
            else:
                for gi in range(NSG):
                    proj(NPG + gi, False, True)
                for gi in range(NSG):
                    proj(NPG + gi, True, False)
                    attend(NPG + gi)
        self.ntrans = 7

    def na_table(self):
        P, D = self.P, self.D
        self.rnatab = Reg("natab")
        with self.phase():
            E = self.sb([31, 8192], F32, "naE")
            rpbT = self.sb([31, 240], F32, "rpbT")
            msk = self.sb([128, 64], F32, "namsk")
            T = self.sb([128, 14336], BF16, "naT")
            rE, rT = Reg(), Reg()
            P.dma("sp", lambda e: e.dma_start(out=E[:], in_=D["naE"][:, :]), [], [rE])
            P.dma("sp", lambda e: e.dma_start(out=rpbT[:], in_=D["na_rpbT"][:, :]), [], [rE])
            P.dma("sp", lambda e: e.dma_start(out=msk[:], in_=D["namask"][:, :]), [], [rE])
            T4 = T[:].rearrange("p (h x q) -> p h x q", x=14, q=64)
            for qc in range(64):
                bank, br = self.ps_full()
                P.op("pe", lambda e, qc=qc, bank=bank: e.matmul(bank[:, 0:240], E[:, qc * 128:(qc + 1) * 128], rpbT[:, :], start=True, stop=True), [rE], br)
                for half in range(2):
                    lo = half * 64
                    P.op("dve", lambda e, qc=qc, bank=bank, lo=lo, half=half: e.tensor_scalar(
                        T4[lo:lo + 64, :, :, qc], bank[lo:lo + 64, 0:240].rearrange("p (h d) -> p h d", d=15)[:, :, half:half + 14],
                        8.0, msk[lo:lo + 64, qc:qc + 1], ALU.mult, ALU.add), [rE], [rT] + br)
            P.dma("sp", lambda e: e.dma_start(out=D["natab_d"][:, :], in_=T[:]), [rT], [self.rnatab])

    def ret_layer(self, l):
        P, D = self.P, self.D
        rc = self.rconst
        with self.phase():
            dec = self.sb([128, 8], F32, "dec")
            negl = self.sb([128, 8], F32, "negl")
            lg = self.sb([128, 8], F32, "lg")
            retc = self.sb([128, 5, 128], F32, "retc")
            retcol = self.sb([128, 2], F32, "retcol")
            intra = self.sb([128, 8, 128], F32, "intra")
            qdec = self.sb([128, 8, 128], F32, "qdec")
            kdec = self.sb([128, 8], F32, "kdec")
            cdec = self.sb([128, 8], F32, "cdec")
            rdec = Reg()
            P.dma("sp", lambda e: e.dma_start(out=dec[:], in_=D["ret_decay"].partition_broadcast(128)), [], [rdec])
            P.dma("sp", lambda e: e.dma_start(out=retc[:], in_=D["retc"].rearrange("p (a i) -> p a i", a=5)), [], [rdec])
            P.dma("sp", lambda e: e.dma_start(out=retcol[:], in_=D["retcol"][:, :]), [], [rdec])
            P.op("act", lambda e: e.activation(negl[:], dec[:], AF.Exp, scale=-1.0), [rdec], [rdec])
            P.op("act", lambda e: e.activation(negl[:], negl[:], AF.Ln, bias=1.0), [rdec], [rdec])
            P.op("act", lambda e: e.mul(lg[:], negl[:], -1.0), [rdec], [rdec])
            for dh in range(8):
                d = dh // 4
                sc = lg[:, dh:dh + 1] if d == 0 else negl[:, dh:dh + 1]
                P.op("act", lambda e, dh=dh, sc=sc: e.activation(intra[:, dh, :], retc[:, 0, :], AF.Exp, scale=sc), [rdec], [rdec])
                P.op("dve", lambda e, dh=dh, d=d: e.tensor_tensor(intra[:, dh, :], intra[:, dh, :], retc[:, 1 + d, :], ALU.mult), [rdec], [rdec])
                P.op("act", lambda e, dh=dh, d=d: e.activation(qdec[:, dh, :], retc[:, 3 + d, :], AF.Exp, scale=lg[:, dh:dh + 1]), [rdec], [rdec])
                P.op("act", lambda e, dh=dh, d=d: e.activation(kdec[:, dh:dh + 1], retcol[:, d:d + 1], AF.Exp, scale=lg[:, dh:dh + 1]), [rdec], [rdec])
                P.op("act", lambda e, dh=dh: e.activation(cdec[:, dh:dh + 1], lg[:, dh:dh + 1], AF.Exp, scale=128.0), [rdec], [rdec])
            P.op("dve", lambda e: e.tensor_scalar(kdec[:], kdec[:], 1.0 / 16.0, None, ALU.mult), [rdec], [rdec])
            Wr = [dict(q=self.sb([128, 8, 256], BF16, "Wq"), k=self.sb([128, 8, 256], BF16, "Wk"), v=self.sb([128, 8, 512], BF16, "Wv"),
                       g=self.sb([128, 8, 512], BF16, "Wg"), rot=self.sb([128, 8, 512], BF16, "Wrot"), gn=self.sb([128, 512], F32, "gn"),
                       r=Reg(), rr=Reg()) for _ in range(2)]
            xt = [self.sb([128, 8, G], F32, "xt") for _ in range(3)]
            rxt = [Reg() for _ in range(3)]
            hTs = [self.sb([128, 8, G], BF16, "hT") for _ in range(2)]
            rhs_ = [Reg(), Reg()]
            csr = [self.sb([128, 2, 2, G], F32, "csr") for _ in range(3)]
            rcsr = [Reg() for _ in range(3)]
            qT = self.sb([128, 2, G], BF16, "qT"); kT = self.sb([128, 2, G], BF16, "kT")
            qdT = [self.sb([128, 2, G], BF16, "qdT") for _ in range(2)]
            rqT, rkT, rqd = Reg(), Reg(), [Reg(), Reg()]
            t1 = [self.sb([128, G], F32, "t1") for _ in range(2)]
            t2 = [self.sb([128, G], F32, "t2") for _ in range(2)]
            rt1, rt2 = [Reg(), Reg()], [Reg(), Reg()]
            kd = [self.sb([128, 256], BF16, "kd") for _ in range(4)]
            vv = [self.sb([128, 512], BF16, "vv") for _ in range(4)]
            sg = [self.sb([128, 512], F32, "sg") for _ in range(4)]
            sm = [self.sb([128, 128], BF16, "sm") for _ in range(4)]
            Usb = [self.sb([128, 2, 512], F32, "Usb") for _ in range(4)]
            yy = [self.sb([128, 512], F32, "yy") for _ in range(2)]
            rkd, rvv, rsg, rsm, rU = ([Reg() for _ in range(4)] for _ in range(5))
            ryy = [Reg(), Reg()]
            cctr4 = [0]
            S = self.sb([128, 2, 512], F32, "S")
            Sbf = self.sb([128, 2, 512], BF16, "Sbf")
            rS = [Reg(), Reg()]
            rSb = [Reg(), Reg()]
            stats = [self.sb([128, 6], F32, "bst") for _ in range(2)]
            mv = [self.sb([128, 2], F32, "mv") for _ in range(2)]
            rmv = [Reg(), Reg()]
            cctr = [0]
            tctr = [0]
            passes = [(d, h) for d in (1, 0) for h in range(4)]

            def wl(pi):
                d, h = passes[pi]
                w = Wr[pi % 2]
                src = D["ret_w"]
                P.dma("pool", lambda e: e.dma_start(out=w["q"][:], in_=src[:, h * 256:(h + 1) * 256].rearrange("(k p) n -> p k n", p=128)), [], [w["r"]])
                P.dma("pool", lambda e: e.dma_start(out=w["k"][:], in_=src[:, 1024 + h * 256:1024 + (h + 1) * 256].rearrange("(k p) n -> p k n", p=128)), [], [w["r"]])
                P.dma("pool", lambda e: e.dma_start(out=w["v"][:], in_=src[:, 2048 + h * 512:2048 + (h + 1) * 512].rearrange("(k p) n -> p k n", p=128)), [], [w["r"]])
                c0 = 4096 + d * 2048 + h * 512
                P.dma("pool", lambda e: e.dma_start(out=w["g"][:], in_=src[:, c0:c0 + 512].rearrange("(k p) n -> p k n", p=128)), [], [w["r"]])
                P.dma("sp", lambda e: e.dma_start(out=w["gn"][:], in_=D["ret_gn"][d:d + 1, h * 512:(h + 1) * 512].partition_broadcast(128)), [], [w["r"]])
                for wi, nm in enumerate(("q", "k")):
                    sv = w[nm][:].rearrange("p k (b t i) -> p k b t i", t=2, i=64)
                    dv = w["rot"][:, :, wi * 256:(wi + 1) * 256].rearrange("p k (b t i) -> p k b t i", t=2, i=64)
                    P.op("pool", lambda e, sv=sv, dv=dv: e.tensor_scalar(dv[:, :, :, 0, :], sv[:, :, :, 1, :], -1.0, None, ALU.mult), [w["r"]], [w["rr"]])
                    P.op("pool", lambda e, sv=sv, dv=dv: e.tensor_copy(dv[:, :, :, 1, :], sv[:, :, :, 0, :]), [w["r"], w["rr"]], [w["rr"]])

            def prefetch(g, s):
                self.load_x(g, xt[s][:], rxt[s])
                if g >= NPG:
                    gi = g - NPG
                    P.dma("sp", lambda e: e.dma_start(out=csr[s][:], in_=D["rope256"][:, :, :, gi * G:(gi + 1) * G].rearrange("c d p t -> p c d t")), [], [rcsr[s]])

            def group(pi, g, first, last, n2):
                d, h = passes[pi]
                dh = d * 4 + h
                w = Wr[pi % 2]
                sample = g >= NPG
                gi = g - NPG
                s3 = n2 % 3
                s = n2 % 2
                hT, rh = hTs[s], rhs_[s]
                if n2 + 2 < len(allitems):
                    prefetch(allitems[n2 + 2][1], (n2 + 2) % 3)
                if n2 + 1 < len(allitems):
                    g1 = allitems[n2 + 1][1]
                    self.modulate(g1, l, 0, xt[(n2 + 1) % 3], rxt[(n2 + 1) % 3], hTs[1 - s], rhs_[1 - s])
                qd, rqdx = qdT[s], rqd[s]
                for wi, (nm, dst, rdst) in enumerate((("q", qT, rqT), ("k", kT, rkT))):
                    for dc in range(2):
                        psa, pra = self.ps_half()
                        for kc in range(8):
                            P.op("pe", lambda e, nm=nm, dc=dc, kc=kc, psa=psa: e.matmul(psa, w[nm][:, kc, dc * 128:(dc + 1) * 128], hT[:, kc, :], start=(kc == 0), stop=(kc == 7)),
                                 [w["r"], rh], pra)
                        if not sample:
                            P.op("act", lambda e, dst=dst, dc=dc, psa=psa: e.copy(dst[:, dc, :], psa), [], [rdst] + pra)
                            continue
                        psb, prb = self.ps_half()
                        for kc in range(8):
                            P.op("pe", lambda e, wi=wi, dc=dc, kc=kc, psb=psb: e.matmul(psb, w["rot"][:, kc, wi * 256 + dc * 128:wi * 256 + (dc + 1) * 128], hT[:, kc, :],
                                                                                start=(kc == 0), stop=(kc == 7)), [w["rr"], rh], prb)
                        i = cctr[0] % 2
                        cctr[0] += 1
                        P.op("dve", lambda e, i=i, dc=dc, psa=psa: e.tensor_tensor(t1[i][:], psa, csr[s3][:, 0, dc, :], ALU.mult), [rcsr[s3]], [rt1[i]] + pra)
                        P.op("dve", lambda e, i=i, dc=dc, psb=psb: e.tensor_tensor(t2[i][:], psb, csr[s3][:, 1, dc, :], ALU.mult), [rcsr[s3]], [rt2[i]] + prb)
                        P.op("pool", lambda e, i=i, dst=dst, dc=dc: e.tensor_tensor(dst[:, dc, :], t1[i][:], t2[i][:], ALU.add), [rt1[i], rt2[i]], [rdst])
                qd_b = qdec[:, dh, :][:, None, :].to_broadcast([128, 4, 128])
                P.op("pool", lambda e: e.tensor_tensor(qd[:].rearrange("p c (b i) -> p (c b) i", i=128), qT[:].rearrange("p c (b i) -> p (c b) i", i=128), qd_b, ALU.mult),
                     [rqT, rdec], [rqdx])
                order = (0, 1) if d == 0 else (1, 0)
                idx = []
                for ci, cb in enumerate(order):
                    i = cctr4[0] % 4
                    cctr4[0] += 1
                    idx.append(i)
                for ci, cb in enumerate(order):
                    ts = slice(cb * 128, (cb + 1) * 128)
                    i = idx[ci]
                    bank, br = self.ps_full()
                    for kc in range(8):
                        P.op("pe", lambda e, kc=kc, bank=bank, ts=ts: e.matmul(bank[:, :], hT[:, kc, ts], w["v"][:, kc, :], start=(kc == 0), stop=(kc == 7)), [w["r"], rh], br)
                    P.op("act", lambda e, i=i, bank=bank: e.copy(vv[i][:], bank[:, :]), [], [rvv[i]] + br)
                    bank, br = self.ps_full()
                    for kc in range(8):
                        P.op("pe", lambda e, kc=kc, bank=bank, ts=ts: e.matmul(bank[:, :], hT[:, kc, ts], w["g"][:, kc, :], start=(kc == 0), stop=(kc == 7)), [w["r"], rh], br)
                    P.op("act", lambda e, i=i, bank=bank: e.activation(sg[i][:], bank[:, :], AF.Silu), [], [rsg[i]] + br)
                for ci, cb in enumerate(order):
                    ts = slice(cb * 128, (cb + 1) * 128)
                    i = idx[ci]
                    bank, br = self.ps_full()
                    bbf = bank[:, 0:128].bitcast(BF16)
                    for dc in range(2):
                        P.op("pe", lambda e, dc=dc, bbf=bbf, ts=ts: e.transpose(bbf[:, dc * 128:(dc + 1) * 128], kT[:, dc, ts], self.identb[:]), [rkT, rc], br)
                    P.op("dve", lambda e, i=i, bbf=bbf: e.tensor_scalar(kd[i][:], bbf, kdec[:, dh:dh + 1], None, ALU.mult), [rdec], [rkd[i]] + br)
                    pss, prs = self.ps_half()
                    for dc in range(2):
                        P.op("pe", lambda e, dc=dc, pss=pss, ts=ts: e.matmul(pss[:, 0:128], kT[:, dc, ts], qT[:, dc, ts], start=(dc == 0), stop=(dc == 1)), [rkT, rqT], prs)
                    P.op("dve", lambda e, i=i, pss=pss: e.tensor_tensor(sm[i][:], pss[:, 0:128], intra[:, dh, :], ALU.mult), [rdec], [rsm[i]] + prs)
                for ci, cb in enumerate(order):
                    i = idx[ci]
                    if sample and last and ci == 1:
                        continue
                    for dc in range(2):
                        bank, br = self.ps_full()
                        P.op("pe", lambda e, i=i, dc=dc, bank=bank: e.matmul(bank[:, :], kd[i][:, dc * 128:(dc + 1) * 128], vv[i][:], start=True, stop=True), [rkd[i], rvv[i]], br)
                        P.op("act", lambda e, i=i, dc=dc, bank=bank: e.copy(Usb[i][:, dc, :], bank[:, :]), [], [rU[i]] + br)

                def scan():
                    if first:
                        if sample:
                            r0 = (d * 4 + h) * 256
                            P.dma("sp", lambda e: e.dma_start(out=S[:], in_=D["sret"][r0:r0 + 256, :].rearrange("(c p) n -> p c n", p=128)), [], rS)
                        else:
                            P.op("pool", lambda e: e.memset(S[:], 0.0), [], rS)
                        for dc in range(2):
                            P.op("act", lambda e, dc=dc: e.copy(Sbf[:, dc, :], S[:, dc, :]), [rS[dc]], [rSb[dc]])
                    for ci, cb in enumerate(order):
                        ts = slice(cb * 128, (cb + 1) * 128)
                        i = idx[ci]
                        j = i % 2
                        bank, br = self.ps_full()
                        P.op("pe", lambda e, i=i, bank=bank: e.matmul(bank[:, :], sm[i][:], vv[i][:], start=True, stop=False), [rsm[i], rvv[i]], br)
                        for dc in range(2):
                            P.op("pe", lambda e, dc=dc, bank=bank, ts=ts: e.matmul(bank[:, :], qd[:, dc, ts], Sbf[:, dc, :], start=False, stop=(dc == 1)), [rqdx, rSb[dc]], br)
                        if not (sample and last and ci == 1):
                            for dc in range(2):
                                P.op("dve", lambda e, i=i, dc=dc: e.scalar_tensor_tensor(S[:, dc, :], S[:, dc, :], cdec[:, dh:dh + 1], Usb[i][:, dc, :], ALU.mult, ALU.add),
                                     [rdec, rSb[dc], rU[i]], [rS[dc]])
                                P.op("act", lambda e, dc=dc: e.copy(Sbf[:, dc, :], S[:, dc, :]), [rS[dc]], [rSb[dc]])
                        P.op("dve", lambda e, j=j, bank=bank: e.bn_stats(stats[j][:], bank[:, :]), [], [rmv[j]] + br)
                        P.op("dve", lambda e, j=j: e.bn_aggr(mv[j][:], stats[j][:]), [rmv[j]], [rmv[j]])
                        P.op("dve", lambda e, j=j: e.tensor_scalar(mv[j][:, 1:2], mv[j][:, 1:2], LN_EPS, None, ALU.add), [rmv[j]], [rmv[j]])
                        P.op("dve", lambda e, j=j: e.reciprocal(mv[j][:, 1:2], mv[j][:, 1:2]), [rmv[j]], [rmv[j]])
                        P.op("act", lambda e, j=j: e.activation(mv[j][:, 1:2], mv[j][:, 1:2], AF.Sqrt), [rmv[j]], [rmv[j]])
                        P.op("dve", lambda e, j=j, bank=bank: e.tensor_scalar(yy[j][:], bank[:, :], mv[j][:, 0:1], mv[j][:, 1:2], ALU.subtract, ALU.mult), [rmv[j]], [ryy[j]] + br)
                        P.op("pool", lambda e, j=j: e.tensor_tensor(yy[j][:], yy[j][:], w["gn"][:], ALU.mult), [ryy[j], w["r"]], [ryy[j]])
                        P.op("pool", lambda e, j=j, i=i: e.tensor_tensor(yy[j][:], yy[j][:], sg[i][:], ALU.mult), [ryy[j], rsg[i]], [ryy[j]])
                        tk0 = g * G + cb * 128
                        P.dma("sp", lambda e, j=j, tk0=tk0: e.dma_start(out=D["Yd"][d, tk0:tk0 + 128, h * 512:(h + 1) * 512], in_=yy[j][:]), [ryy[j]], [self.yreg[d][g]])
                    if (not sample) and last:
                        row0 = ((g * 2 + d) * 4 + h) * 256
                        self.fin.append(P.dma("sp", lambda e: e.dma_start(out=D["o_ret"][row0:row0 + 256, :].rearrange("(c p) n -> p c n", p=128), in_=S[:]), rS, []))
                return scan

            self.yreg = [[Reg() for _ in range(NPG + NSG)] for _ in range(2)]
            wl(0)
            prev = None
            allitems = []
            for pi in range(8):
                d = passes[pi][0]
                gl = list(range(NPG, NPG + NSG))
                if d == 1:
                    gl = gl[::-1]
                for n_, (g, first, last) in enumerate([(g, True, True) for g in range(NPG)] + [(g, k == 0, k == NSG - 1) for k, g in enumerate(gl)]):
                    allitems.append((pi, g, first, last, n_))
            prefetch(allitems[0][1], 0)
            prefetch(allitems[1][1], 1)
            self.modulate(allitems[0][1], l, 0, xt[0], rxt[0], hTs[0], rhs_[0])
            for n2, (pi, g, first, last, n_) in enumerate(allitems):
                sc = group(pi, g, first, last, n2)
                if prev is not None:
                    prev()
                prev = sc
                if n_ == 0 and pi + 1 < 8:
                    wl(pi + 1)
            prev()
        with self.phase():
            ya = [self.sb([128, 2048], F32, "ya") for _ in range(2)]
            yb = [self.sb([128, 2048], F32, "yb") for _ in range(2)]
            ys = [self.sb([128, 2048], BF16, "ys") for _ in range(2)]
            rya, ryb, rys = [Reg(), Reg()], [Reg(), Reg()], [Reg(), Reg()]
            yT = [self.sb([128, 16, G], BF16, "yT") for _ in range(2)]
            ryT = [Reg(), Reg()]
            it = 0
            for g in range(NPG + NSG):
                sT = g % 2
                for b in range(2):
                    i = it % 2
                    it += 1
                    tk0 = g * G + b * 128
                    P.dma("sp", lambda e, i=i, tk0=tk0: e.dma_start(out=ya[i][:], in_=D["Yd"][0, tk0:tk0 + 128, :]), [self.yreg[0][g]], [rya[i]])
                    P.dma("sp", lambda e, i=i, tk0=tk0: e.dma_start(out=yb[i][:], in_=D["Yd"][1, tk0:tk0 + 128, :]), [self.yreg[1][g]], [ryb[i]])
                    P.op("dve", lambda e, i=i: e.tensor_tensor(ys[i][:], ya[i][:], yb[i][:], ALU.add), [rya[i], ryb[i]], [rys[i]])
                    for k2 in range(2):
                        bank, br = self.ps_full()
                        bbf = bank[:, :].bitcast(BF16)
                        for kk in range(8):
                            c = k2 * 8 + kk
                            P.op("pe", lambda e, i=i, c=c, kk=kk, bbf=bbf: e.transpose(bbf[:, kk * 128:(kk + 1) * 128], ys[i][:, c * 128:(c + 1) * 128], self.identb[:]), [rys[i], rc], br)
                        if k2 == 0:
                            P.op("act", lambda e, sT=sT, b=b, k2=k2, bbf=bbf: e.copy(yT[sT][:, k2 * 8:(k2 + 1) * 8, b * 128:(b + 1) * 128], bbf.rearrange("p (c t) -> p c t", t=128)), [], [ryT[sT]] + br)
                        else:
                            P.op("dve", lambda e, sT=sT, b=b, k2=k2, bbf=bbf: e.tensor_copy(yT[sT][:, k2 * 8:(k2 + 1) * 8, b * 128:(b + 1) * 128], bbf.rearrange("p (c t) -> p c t", t=128)), [], [ryT[sT]] + br)
                t0 = g * G
                P.dma("sp", lambda e, sT=sT, t0=t0: e.dma_start(out=D["mixR_d"][:, :, t0:t0 + G], in_=yT[sT][:]), [ryT[sT]], [self.mixreg[g]])


def build(seq=None):
    kb = KB()
    kb.declare()
    kb.prep()
    kb.na_table()
    if seq is None:
        seq = []
        for l in range(4):
            seq += [("mix", l), ("mlp", l)]
    for kind, l in seq:
        if kind == "mlp":
            kb.mlp(l)
        else:
            kb.mixer(l)
    kb.final()
    kb.P.op("pool", lambda e: e.memset(kb.ident[0:1, 0:1], 1.0), [], [])
    kb.P.flush(final_waits=kb.fin)
    kb.top.close()
    return kb


def _per_core_inputs(inp, consts, c):
    f = lambda a: np.ascontiguousarray(a, dtype=np.float32)
    m = {}
    m["xp"] = f(inp["x_prompt"][4 * c:4 * c + 4].reshape(1024, 1024))
    m["xs"] = f(inp["x_sample"][c])
    m["cswak"] = f(inp["cache_swa_k"][c, 0].reshape(512, 256)); m["cswav"] = f(inp["cache_swa_v"][c, 0].reshape(512, 256))
    m["cnak"] = f(inp["cache_na_k"][c, 0].reshape(512, 1024)); m["cnav"] = f(inp["cache_na_v"][c, 0].reshape(512, 1024))
    m["sret"] = f(inp["state_ret"][c, 0].reshape(2048, 512))
    m["cgqak"] = f(inp["cache_gqa_k"][c, 0].reshape(512, 256)); m["cgqav"] = f(inp["cache_gqa_v"][c, 0].reshape(512, 256))
    m["cT_h"] = f(np.concatenate([inp["c_ctx"].reshape(8, 128).T, inp["c"][c].reshape(8, 128).T], axis=1))
    return m


def _shared_inputs(inp, consts):
    f = lambda a: np.ascontiguousarray(a, dtype=np.float32)
    m = {}
    m["lngT_h"] = f(inp["ln_g"].reshape(64, 128).T); m["lnbT_h"] = f(inp["ln_b"].reshape(64, 128).T)
    m["modbT_h"] = f(inp["mod_b"].reshape(192, 128).T)
    p = np.arange(128) % 64
    partner = np.where((p % 32) < 16, p + 16, p - 16)
    qn, kn = inp["gqa_q_norm"][0], inp["gqa_k_norm"][0]
    m["gqn"] = f(np.stack([qn[p], qn[partner], kn[p], kn[partner]], axis=1))
    m["mod_w"] = f(inp["mod_w"]); m["mlp_w1"] = f(inp["mlp_w1"]); m["mlp_w2"] = f(inp["mlp_w2"])
    order = []
    for cch in range(8):
        for b in range(2):
            order.append(4 * (2 * (cch // 4) + b) + cch % 4)
    cols = np.concatenate([np.arange(h * 64, (h + 1) * 64) for h in order] + [np.arange(1024, 1536)])
    m["swa_wqkv_p"] = f(inp["swa_wqkv"][0][:, cols]); m["gqa_wqkv_p"] = f(inp["gqa_wqkv"][0][:, cols])
    m["swa_wo"] = f(inp["swa_wo"][0]); m["gqa_wo"] = f(inp["gqa_wo"][0]); m["swa_sink"] = f(inp["swa_sink"].reshape(1, 16))
    m["na_wqkv"] = f(inp["na_wqkv"][0]); m["na_wo"] = f(inp["na_wo"][0])
    m["na_rpbT"] = f(inp["na_rpb"][0].reshape(240, 31).T)
    m["ret_w"] = f(inp["ret_wqkvg"][0]); m["ret_wo"] = f(inp["ret_wo"][0])
    m["ret_decay"] = f(inp["ret_decay"].reshape(1, 8)); m["ret_gn"] = f(inp["ret_gn_g"][0])
    for k, v in consts.items():
        m[k] = f(v)
    return m


_CACHE = {}


def kernel(**inp):
    inp = {k: np.asarray(v) for k, v in inp.items()}
    seq = None
    env = os.environ.get("KSEQ")
    if env is not None:
        seq = [(s[:3], int(s[3:])) for s in env.split(",") if s]
    key = str(seq)
    if key not in _CACHE:
        _CACHE[key] = build(seq)
    kb = _CACHE[key]
    consts = _host_consts()
    shared = _shared_inputs(inp, consts)
    in_maps = []
    ncores = int(os.environ.get("KCORES", "8"))
    for c in range(ncores):
        m = dict(shared)
        m.update(_per_core_inputs(inp, consts, c))
        in_maps.append({k: m[k] for k in kb.in_names})
    res = run_bass_kernel_spmd(kb.nc, in_maps, core_ids=list(range(ncores)))
    R = list(res.results)
    while len(R) < 8:
        R.append(R[0])
    cat = lambda n: np.concatenate([np.asarray(R[c][n]) for c in range(8)], axis=0)
    y_p = cat("y_p").reshape(32, 256, 1024)
    y_s = cat("y_s").reshape(8, 4096, 1024)
    swak = cat("o_swak").reshape(32, 1, 256, 4, 64); swav = cat("o_swav").reshape(32, 1, 256, 4, 64)
    nak = cat("o_nak").reshape(32, 1, 256, 16, 64); nav = cat("o_nav").reshape(32, 1, 256, 16, 64)
    ret = cat("o_ret").reshape(32, 1, 2, 4, 256, 512)
    gqak = cat("o_gqak").reshape(32, 1, 256, 4, 64); gqav = cat("o_gqav").reshape(32, 1, 256, 4, 64)
    return tuple(np.ascontiguousarray(a, dtype=np.float32) for a in (y_p, y_s, swak, swav, nak, nav, ret, gqak, gqav))
```

```python
import os
import numpy as np
from contextlib import ExitStack
import concourse.bass as bass
import concourse.mybir as mybir
from concourse.bass_utils import run_bass_kernel_spmd

F32 = mybir.dt.float32
BF16 = mybir.dt.bfloat16
ALU = mybir.AluOpType
AF = mybir.ActivationFunctionType

ENGS = ("pe", "act", "dve", "pool", "sp")
NDSEM = 24
ALPHA = 8.0 ** 0.25
LN_EPS = 1e-5
RMS_EPS = 1e-6
G = 256
NPG = 4
NSG = 16
NTOK = 5120


class Reg:
    __slots__ = ("name", "w", "rs", "drs")

    def __init__(self, name=""):
        self.name = name
        self.w = None
        self.rs = {}
        self.drs = []


class Ins:
    __slots__ = ("eng", "fn", "deps", "dma", "marked", "semval", "dsem", "dval", "emitted")

    def __init__(self, eng, fn, dma):
        self.eng = eng
        self.fn = fn
        self.dma = dma
        self.deps = []
        self.marked = False
        self.semval = 0
        self.dsem = None
        self.dval = 0
        self.emitted = False


class Prog:
    def __init__(self, nc, stack):
        self.nc = nc
        self.lists = {e: [] for e in ENGS}
        self.sem = {e: stack.enter_context(nc.semaphore("s_" + e)) for e in ENGS if e != "sp"}
        self.count = {e: 0 for e in ENGS}
        self.dsems = {q: [stack.enter_context(nc.semaphore("d_%s%d" % (q, i))) for i in range(NDSEM)]
                      for q in ("sp", "pool")}
        self.dcount = {"sp": 0, "pool": 0}
        self.dhist = {"sp": [], "pool": []}
        self.waited = {e: {} for e in ENGS}
        self.last_marked = {e: None for e in ENGS}
        self.n_ins = 0
        self.n_wait = 0

    def _add(self, eng, fn, reads, writes, dma):
        ins = Ins(eng, fn, dma)
        deps = {}

        def dep(d):
            if d is None or d is ins:
                return
            if (not d.dma) and (not dma) and d.eng == eng and eng == "pe":
                return
            if (not d.dma) and d.emitted and not d.marked:
                d = self.last_marked[d.eng]
                if d is None:
                    return
            deps[id(d)] = d

        for r in reads:
            dep(r.w)
        for w in writes:
            dep(w.w)
            for d in w.rs.values():
                dep(d)
            for d in w.drs:
                dep(d)
        if dma:
            q = eng
            j = self.dcount[q]
            self.dcount[q] += 1
            ins.dsem = self.dsems[q][j % NDSEM]
            ins.dval = 16 * (j // NDSEM + 1)
            if j >= NDSEM:
                dep(self.dhist[q][j - NDSEM])
            self.dhist[q].append(ins)
        for d in deps.values():
            if not d.dma:
                d.marked = True
        ins.deps = list(deps.values())
        for r in reads:
            if dma:
                r.drs.append(ins)
            else:
                r.rs[eng] = ins
        for w in writes:
            w.w = ins
            w.rs = {}
            w.drs = []
        self.lists[eng].append(ins)
        self.n_ins += 1
        return ins

    def op(self, eng, fn, reads=(), writes=()):
        return self._add(eng, fn, reads, writes, False)

    def dma(self, q, fn, reads=(), writes=()):
        return self._add(q, fn, reads, writes, True)

    def flush(self, final_waits=()):
        nc = self.nc
        for e in ENGS:
            lst = [i for i in self.lists[e] if not i.dma]
            if lst:
                lst[-1].marked = True
        for e in ENGS:
            for i in self.lists[e]:
                if i.marked and not i.dma:
                    self.count[e] += 1
                    i.semval = self.count[e]
        lists = self.lists
        self.lists = {e: [] for e in ENGS}
        handles = {"pe": "tensor", "act": "scalar", "dve": "vector", "pool": "gpsimd", "sp": "sync"}
        with nc.Block() as block:
            for e in ENGS:
                def body(eng, e=e):
                    wt = self.waited[e]
                    for ins in lists[e]:
                        for d in ins.deps:
                            if d.dma:
                                key = id(d.dsem)
                                if wt.get(key, 0) < d.dval:
                                    eng.wait_ge(d.dsem, d.dval)
                                    wt[key] = d.dval
                                    self.n_wait += 1
                            else:
                                if wt.get(d.eng, 0) < d.semval:
                                    eng.wait_ge(self.sem[d.eng], d.semval)
                                    wt[d.eng] = d.semval
                                    self.n_wait += 1
                        bi = ins.fn(eng)
                        if ins.dma:
                            bi.then_inc(ins.dsem, 16)
                        elif ins.marked:
                            bi.then_inc(self.sem[e], 1)
                            self.last_marked[e] = ins
                        ins.emitted = True
                    if e == "sp":
                        for q in ("sp", "pool"):
                            for d in self.dhist[q][-NDSEM:]:
                                key = id(d.dsem)
                                if wt.get(key, 0) < d.dval:
                                    eng.wait_ge(d.dsem, d.dval)
                                    wt[key] = d.dval
                        for d in final_waits:
                            key = id(d.dsem)
                            if wt.get(key, 0) < d.dval:
                                eng.wait_ge(d.dsem, d.dval)
                                wt[key] = d.dval
                getattr(block, handles[e])(body)


def _host_consts():
    c = {}
    t = np.arange(4096)
    row, col = (t // 64).astype(np.float64), (t % 64).astype(np.float64)
    p = np.arange(128) % 64
    inv16 = 10000.0 ** (-(np.arange(16)) / 16.0)
    pos = np.where((p < 32)[:, None], row[None, :], col[None, :])
    ang = pos * inv16[p % 16][:, None]
    c["rope64"] = np.stack([np.cos(ang), np.sin(ang)]).astype(np.float32)
    inv64 = 10000.0 ** (-(np.arange(64)) / 64.0)
    p2 = np.arange(128) % 64
    a0 = row[None, :] * inv64[p2][:, None]
    a1 = col[None, :] * inv64[p2][:, None]
    c["rope256"] = np.stack([np.stack([np.cos(a0), np.cos(a1)]), np.stack([np.sin(a0), np.sin(a1)])]).astype(np.float32)
    j = np.arange(128)[:, None].astype(np.float64)
    i = np.arange(128)[None, :].astype(np.float64)
    ret = np.zeros((128, 5, 128), np.float32)
    ret[:, 0] = i - j
    ret[:, 1] = (i >= j) / 16.0
    ret[:, 2] = (j >= i) / 16.0
    ret[:, 3] = np.broadcast_to(i + 1.0, (128, 128))
    ret[:, 4] = np.broadcast_to(128.0 - i, (128, 128))
    c["retc"] = ret.reshape(128, 640)
    rc = np.zeros((128, 2), np.float32)
    rc[:, 0] = 127.0 - np.arange(128)
    rc[:, 1] = np.arange(128)
    c["retcol"] = rc
    kk = np.arange(128)[:, None]
    qq = np.arange(128)[None, :]
    mp = np.where(kk >= qq, 0.0, -1e30)
    mn = np.where(kk <= qq, 0.0, -1e30)
    c["swamask"] = np.stack([np.tile(mp, (1, 4)), np.tile(mn, (1, 4))]).astype(np.float32)
    E = np.zeros((31, 64, 2, 64), np.float32)
    for qc in range(64):
        for kc in range(64):
            jj = kc - qc + 15
            if 0 <= jj <= 30:
                E[jj, qc, :, kc] = 1.0
    c["naE"] = E.reshape(31, 8192)
    cs = np.clip(np.arange(64) - 8, 0, 48)
    kc = np.arange(64)[:, None]
    ok = (kc >= cs[None, :]) & (kc < cs[None, :] + 16)
    c["namask"] = np.tile(np.where(ok, 0.0, -1e30), (2, 1)).astype(np.float32)
    return c


class KB:
    def __init__(self, nsub=8):
        self.nsub = nsub
        self.nc = bass.Bass("TRN2", target_bir_lowering=False)
        self.top = ExitStack()
        self.cur = self.top
        self.P = Prog(self.nc, self.top)
        self.uid = 0
        self.D = {}
        self.fin = []
        self.in_names = []
        self.out_names = []

    def din(self, n, shape, dt=F32):
        self.D[n] = self.nc.dram_tensor(n, list(shape), dt, kind="ExternalInput").ap()
        self.in_names.append(n)
        return self.D[n]

    def dout(self, n, shape):
        self.D[n] = self.nc.dram_tensor(n, list(shape), F32, kind="ExternalOutput").ap()
        self.out_names.append(n)
        return self.D[n]

    def dscr(self, n, shape, dt=F32):
        self.D[n] = self.nc.dram_tensor(n, list(shape), dt, kind="Internal").ap()
        return self.D[n]

    def sb(self, shape, dt=F32, name="t"):
        self.uid += 1
        return self.cur.enter_context(self.nc.sbuf_tensor("%s_%d" % (name, self.uid), list(shape), dt))

    def phase(self):
        kb = self

        class _Ph:
            def __enter__(s):
                s.st = ExitStack()
                s.prev = kb.cur
                kb.cur = s.st
                return s

            def __exit__(s, et, ev, tb):
                if et is None:
                    kb.P.flush()
                kb.cur = s.prev
                s.st.close()
                return False
        return _Ph()

    def init_psum(self):
        self.banks = [self.top.enter_context(self.nc.psum_tensor("psb%d" % i, [128, 512], F32)) for i in range(8)]
        self.bregs = [[Reg("b%d" % i)] for i in range(8)]
        self.full_ctr = 0
        self.ntrans = 7
        self.half_ctr = 0

    def ps_full(self):
        b = self.full_ctr % self.ntrans
        self.full_ctr += 1
        return self.banks[b], self.bregs[b]

    def ps_half(self):
        s = self.half_ctr % (2 * self.ntrans)
        self.half_ctr += 1
        b, h = s % self.ntrans, s // self.ntrans
        return self.banks[b][:, h * 256:(h + 1) * 256], self.bregs[b]

    def xgrp(self, g):
        return g * G

    def load_x(self, g, dst, reg):
        xT = self.D["xT_d"]
        t0 = g * G
        return self.P.dma("sp", lambda e: e.dma_start(out=dst, in_=xT[:, :, t0:t0 + G].rearrange("c p t -> p c t")), [self.xreg[g]], [reg])

    def store_x(self, g, src, reg):
        xT = self.D["xT_d"]
        t0 = g * G
        return self.P.dma("sp", lambda e: e.dma_start(out=xT[:, :, t0:t0 + G].rearrange("c p t -> p c t"), in_=src), [reg], [self.xreg[g]])

    def modv(self, g, l, j, c):
        grp = 0 if g < NPG else 1
        return self.modT[:, grp, l, j * 8 + c:j * 8 + c + 1]

    def modp1(self, g, l, which, c):
        grp = 0 if g < NPG else 1
        return self.modP1[:, grp, l, which, c:c + 1]

    def modulate(self, g, l, which, xt, rx, hT, rh):
        P = self.P
        for c in range(8):
            sc = self.modp1(g, l, which, c)
            sh = self.modv(g, l, 3 * which, c)
            P.op("dve", lambda e, c=c, sc=sc, sh=sh: e.tensor_scalar(hT[:, c, :], xt[:, c, :], sc, sh, ALU.mult, ALU.add),
                 [rx, self.rmod], [rh])

    def ln_store(self, g, l, i, zz, rz0, rz1):
        P = self.P
        nc = self.nc
        st = self.lnst
        self.ln_p1(zz, rz0, rz1)
        self.ln_p2(g, l, i, zz, rz0, rz1)

    def ln_p1(self, zz, rz0, rz1):
        P = self.P
        red, rred = self.lnst["red"], self.lnst["rred"]
        P.op("act", lambda e: e.activation(zz[:, 1], zz[:, 0], AF.Square), [rz0], [rz1])
        P.op("dve", lambda e: e.tensor_reduce(red[:], zz[:].rearrange("p a c t -> p a t c"), mybir.AxisListType.X, ALU.add), [rz0, rz1], [rred])

    def ln_p2(self, g, l, i, zz, rz0, rz1):
        P = self.P
        st = self.lnst
        bank, br = self.ps_full()
        red, rred = st["red"], st["rred"]
        P.op("pe", lambda e: e.matmul(bank[:, :].rearrange("p (a t) -> p a t", a=2), self.ones32[:, :], red[:], start=True, stop=True), [rred, self.rconst], br)
        m, msq, var, rstd = st["m"], st["msq"], st["var"], st["rstd"]
        rm, rv, rr = st["rm"], st["rv"], st["rr"]
        P.op("act", lambda e: e.mul(m[:], bank[:, 0:256], 1.0 / 1024.0), [], [rm] + br)
        P.op("dve", lambda e: e.tensor_tensor(msq[:], m[:], m[:], ALU.mult), [rm], [rv])
        P.op("dve", lambda e: e.scalar_tensor_tensor(var[:], bank[:, 256:512], 1.0 / 1024.0, msq[:], ALU.mult, ALU.subtract), [rv], [rv] + br)
        P.op("dve", lambda e: e.tensor_scalar(var[:], var[:], LN_EPS, None, ALU.add), [rv], [rv])
        P.op("dve", lambda e: e.reciprocal(var[:], var[:]), [rv], [rv])
        P.op("act", lambda e: e.activation(rstd[:], var[:], AF.Sqrt), [rv], [rr])
        mb = m[:, None, :].to_broadcast([128, 8, G])
        rb = rstd[:, None, :].to_broadcast([128, 8, G])
        P.op("dve", lambda e: e.tensor_tensor(zz[:, 0], zz[:, 0], mb, ALU.subtract), [rz0, rm], [rz0])
        P.op("dve", lambda e: e.tensor_tensor(zz[:, 0], zz[:, 0], rb, ALU.mult), [rz0, rr], [rz0])
        gb = self.lngT[:, l * 16 + i * 8:l * 16 + i * 8 + 8][:, :, None].to_broadcast([128, 8, G])
        bb = self.lnbT[:, l * 16 + i * 8:l * 16 + i * 8 + 8][:, :, None].to_broadcast([128, 8, G])
        P.op("pool", lambda e: e.tensor_tensor(zz[:, 0], zz[:, 0], gb, ALU.mult), [rz0, self.rmod], [rz0])
        P.op("pool", lambda e: e.tensor_tensor(zz[:, 1], zz[:, 0], bb, ALU.add), [rz0, self.rmod], [rz1])
        self.store_x(g, zz[:, 1], rz1)

    def alloc_ln(self):
        st = {}
        for n in ("m", "msq", "var", "rstd"):
            st[n] = self.sb([128, G], F32, "ln" + n)
        st["rm"], st["rv"], st["rr"] = Reg(), Reg(), Reg()
        st["red"] = self.sb([128, 2, G], F32, "lnred")
        st["rred"] = Reg()
        self.lnst = st

    def wload(self, dst, src, reg, reads=()):
        return self.P.dma("pool", lambda e: e.dma_start(out=dst, in_=src), list(reads), [reg])

    def declare(self):
        d = self.din
        d("xp", [1024, 1024]); d("xs", [4096, 1024])
        d("cswak", [512, 256]); d("cswav", [512, 256]); d("cnak", [512, 1024]); d("cnav", [512, 1024])
        d("sret", [2048, 512]); d("cgqak", [512, 256]); d("cgqav", [512, 256])
        d("cT_h", [128, 16]); d("lngT_h", [128, 64]); d("lnbT_h", [128, 64]); d("modbT_h", [128, 192]); d("gqn", [128, 4])
        d("mod_w", [4, 1024, 6144]); d("mlp_w1", [4, 1024, 4096]); d("mlp_w2", [4, 4096, 1024])
        d("swa_wqkv_p", [1024, 1536]); d("swa_wo", [1024, 1024]); d("swa_sink", [1, 16])
        d("na_wqkv", [1024, 3072]); d("na_wo", [1024, 1024]); d("na_rpbT", [31, 240])
        d("ret_w", [1024, 8192]); d("ret_wo", [2048, 1024]); d("ret_decay", [1, 8]); d("ret_gn", [2, 2048])
        d("gqa_wqkv_p", [1024, 1536]); d("gqa_wo", [1024, 1024])
        d("rope64", [2, 128, 4096]); d("rope256", [2, 2, 128, 4096]); d("retc", [128, 640]); d("retcol", [128, 2])
        d("swamask", [2, 128, 512]); d("naE", [31, 8192]); d("namask", [128, 64])
        o = self.dout
        o("y_p", [1024, 1024]); o("y_s", [4096, 1024])
        o("o_swak", [1024, 256]); o("o_swav", [1024, 256]); o("o_nak", [1024, 1024]); o("o_nav", [1024, 1024])
        o("o_ret", [8192, 512]); o("o_gqak", [1024, 256]); o("o_gqav", [1024, 256])
        s = self.dscr
        s("xT_d", [8, 128, NTOK]); s("mixA_d", [64, 16, NTOK], BF16); s("mixR_d", [128, 16, NTOK], BF16)
        s("Yd", [2, NTOK, 2048]); s("natab_d", [128, 14336], BF16)
        self.xreg = [Reg("xg%d" % g) for g in range(NPG + NSG)]
        self.mixreg = [Reg("mx%d" % g) for g in range(NPG + NSG)]
        self.ident = self.sb([128, 128], F32, "ident")
        self.identb = self.sb([128, 128], BF16, "identb")
        self.ones32 = self.sb([128, 128], F32, "ones32")
        self.blk32 = self.sb([128, 128], F32, "blk32")
        self.sel64 = self.sb([128, 128], F32, "sel64")
        self.cT = self.sb([128, 16], F32, "cT")
        self.lngT = self.sb([128, 64], F32, "lngT")
        self.lnbT = self.sb([128, 64], F32, "lnbT")
        self.modbT = self.sb([128, 192], F32, "modbT")
        self.gqn = self.sb([128, 4], F32, "gqn")
        self.modT = self.sb([128, 2, 4, 48], F32, "modT")
        self.modP1 = self.sb([128, 2, 4, 2, 8], F32, "modP1")
        self.rconst = Reg("const")
        self.rmod = Reg("mod")
        self.init_psum()

    def prep(self):
        P, nc, D = self.P, self.nc, self.D
        rc, rmod = self.rconst, self.rmod
        with self.phase():
            ident, identb, ones32, blk32 = self.ident, self.identb, self.ones32, self.blk32
            P.op("pool", lambda e: e.memset(ident[:], 0.0), [], [rc])
            P.op("pool", lambda e: e.affine_select(out=ident[:], in_=ident[:], pattern=[[-1, 128]], compare_op=ALU.not_equal,
                                                   fill=1.0, base=0, channel_multiplier=1), [rc], [rc])
            P.op("pool", lambda e: e.memset(ones32[:], 1.0), [], [rc])
            P.op("pool", lambda e: e.memset(blk32[:], 0.0), [], [rc])
            P.op("pool", lambda e: e.memset(blk32[0:64, 0:64], 1.0), [rc], [rc])
            P.op("pool", lambda e: e.memset(blk32[64:128, 64:128], 1.0), [rc], [rc])
            P.op("pool", lambda e: e.memset(self.sel64[:], 0.0), [], [rc])
            P.op("pool", lambda e: e.memset(self.sel64[64:65, :], 1.0), [rc], [rc])
            P.op("dve", lambda e: e.tensor_copy(identb[:], ident[:]), [rc], [rc])
            for dst, src in ((self.cT, "cT_h"), (self.lngT, "lngT_h"), (self.lnbT, "lnbT_h"), (self.modbT, "modbT_h"), (self.gqn, "gqn")):
                P.dma("sp", lambda e, dst=dst, src=src: e.dma_start(out=dst[:], in_=D[src][:, :]), [], [rmod])
            sil = self.sb([128, 8, 2], F32, "sil")
            rs = Reg()
            P.op("act", lambda e: e.activation(sil[:, :, 0], self.cT[:, 0:8], AF.Silu), [rmod], [rs])
            P.op("act", lambda e: e.activation(sil[:, :, 1], self.cT[:, 8:16], AF.Silu), [rmod, rs], [rs])
            mw = [self.sb([128, 8, 768], F32, "mw") for _ in range(2)]
            rmw = [Reg(), Reg()]
            pm, prm = self.banks[7], self.bregs[7]
            it = 0
            for l in range(4):
                for nb in range(8):
                    s = it % 2
                    it += 1
                    P.dma("sp", lambda e, l=l, nb=nb, s=s: e.dma_start(
                        out=mw[s][:], in_=D["mod_w"][l, :, nb * 768:(nb + 1) * 768].rearrange("(k p) n -> p k n", p=128)), [], [rmw[s]])
                    for n6 in range(6):
                        n = nb * 6 + n6
                        for kc in range(8):
                            P.op("pe", lambda e, s=s, n6=n6, kc=kc, l=l, n=n: e.matmul(
                                pm[:, l * 96 + n * 2:l * 96 + n * 2 + 2], mw[s][:, kc, n6 * 128:(n6 + 1) * 128], sil[:, kc, :],
                                start=(kc == 0), stop=(kc == 7)), [rmw[s], rs], prm)
            for grp in range(2):
                P.op("dve", lambda e, grp=grp: e.tensor_tensor(
                    self.modT[:, grp].rearrange("p l n -> p (l n)"),
                    pm[:, 0:384].rearrange("p (x g) -> p x g", g=2)[:, :, grp], self.modbT[:, :], ALU.add), [rmod], [rmod] + prm)
            for grp in range(2):
                for which, j in ((0, 1), (1, 4)):
                    P.op("dve", lambda e, grp=grp, which=which, j=j: e.tensor_scalar(
                        self.modP1[:, grp, :, which, :], self.modT[:, grp, :, j * 8:(j + 1) * 8], 1.0, None, ALU.add), [rmod], [rmod])
            xin = [self.sb([128, 2, 1024], F32, "xin") for _ in range(2)]
            rxin = [Reg(), Reg()]
            xt = [self.sb([128, 8, G], F32, "xt0") for _ in range(2)]
            rxt = [Reg(), Reg()]
            for g in range(NPG + NSG):
                s = g % 2
                src = D["xp"][g * G:(g + 1) * G, :] if g < NPG else D["xs"][(g - NPG) * G:(g - NPG + 1) * G, :]
                P.dma("sp", lambda e, s=s, src=src: e.dma_start(out=xin[s][:], in_=src.rearrange("(b p) f -> p b f", p=128)), [], [rxin[s]])
                for c2 in range(4):
                    bank, br = self.ps_full()
                    for cc in range(2):
                        c = c2 * 2 + cc
                        for b in range(2):
                            P.op("pe", lambda e, s=s, c=c, cc=cc, b=b, bank=bank: e.transpose(
                                bank[:, cc * 256 + b * 128:cc * 256 + b * 128 + 128], xin[s][:, b, c * 128:(c + 1) * 128], ident[:]),
                                [rxin[s], rc], br)
                    eng = "act" if c2 % 2 == 0 else "dve"
                    if eng == "act":
                        P.op("act", lambda e, s=s, c2=c2, bank=bank: e.copy(xt[s][:, 2 * c2:2 * c2 + 2, :], bank[:, :].rearrange("p (a t) -> p a t", a=2)), [], [rxt[s]] + br)
                    else:
                        P.op("dve", lambda e, s=s, c2=c2, bank=bank: e.tensor_copy(xt[s][:, 2 * c2:2 * c2 + 2, :], bank[:, :].rearrange("p (a t) -> p a t", a=2)), [], [rxt[s]] + br)
                self.store_x(g, xt[s][:], rxt[s])

    def final(self):
        P, D = self.P, self.D
        with self.phase():
            xt = [self.sb([128, 8, G], F32, "xtf") for _ in range(2)]
            rxt = [Reg(), Reg()]
            yo = [self.sb([128, 2, 1024], F32, "yo") for _ in range(2)]
            ryo = [Reg(), Reg()]
            for g in range(NPG + NSG):
                s = g % 2
                self.load_x(g, xt[s][:], rxt[s])
                for b in range(2):
                    for c4 in range(2):
                        bank, br = self.ps_full()
                        for cc in range(4):
                            c = c4 * 4 + cc
                            P.op("pe", lambda e, s=s, c=c, cc=cc, b=b, bank=bank: e.transpose(
                                bank[:, cc * 128:(cc + 1) * 128], xt[s][:, c, b * 128:(b + 1) * 128], self.ident[:]), [rxt[s], self.rconst], br)
                        if c4 == 0:
                            P.op("act", lambda e, s=s, b=b, c4=c4, bank=bank: e.copy(yo[s][:, b, c4 * 512:(c4 + 1) * 512], bank[:, :]), [], [ryo[s]] + br)
                        else:
                            P.op("dve", lambda e, s=s, b=b, c4=c4, bank=bank: e.tensor_copy(yo[s][:, b, c4 * 512:(c4 + 1) * 512], bank[:, :]), [], [ryo[s]] + br)
                dst = D["y_p"][g * G:(g + 1) * G, :] if g < NPG else D["y_s"][(g - NPG) * G:(g - NPG + 1) * G, :]
                self.fin.append(P.dma("sp", lambda e, s=s, dst=dst: e.dma_start(out=dst.rearrange("(b p) f -> p b f", p=128), in_=yo[s][:]), [ryo[s]], []))

    def mlp(self, l):
        P, D = self.P, self.D
        with self.phase():
            W1 = self.sb([128, 8, 4096], BF16, "W1")
            W2 = self.sb([128, 32, 1024], BF16, "W2")
            rW1 = [[Reg() for _ in range(2)] for _ in range(8)]
            rW2 = [Reg() for _ in range(8)]
            for kc in range(8):
                for h in range(2):
                    self.wload(W1[:, kc, h * 2048:(h + 1) * 2048], D["mlp_w1"][l, kc * 128:(kc + 1) * 128, h * 2048:(h + 1) * 2048], rW1[kc][h])
            for f4 in range(8):
                self.wload(W2[:, f4 * 4:(f4 + 1) * 4, :], D["mlp_w2"][l, f4 * 512:(f4 + 1) * 512, :].rearrange("(f p) n -> p f n", p=128), rW2[f4])
            xt = [self.sb([128, 8, G], F32, "xt") for _ in range(2)]
            rxt = [Reg(), Reg()]
            hT = self.sb([128, 8, G], BF16, "hT")
            rh = Reg()
            hid = self.sb([128, 32, G], BF16, "hid")
            rhid = [Reg() for _ in range(32)]
            zz = self.sb([128, 2, 8, G], F32, "zz")
            rz0, rz1 = Reg(), Reg()
            rt = [self.sb([128, G], F32, "rt") for _ in range(3)]
            rrt = [Reg() for _ in range(3)]
            self.alloc_ln()
            ng = NPG + NSG
            self.load_x(0, xt[0][:], rxt[0])
            self.modulate(0, l, 1, xt[0], rxt[0], hT, rh)

            def step3(g, f):
                ps, pr = self.ps_half()
                for kc in range(8):
                    P.op("pe", lambda e, f=f, kc=kc, ps=ps: e.matmul(ps, W1[:, kc, f * 128:(f + 1) * 128], hT[:, kc, :],
                                                             start=(kc == 0), stop=(kc == 7)), [rW1[kc][f // 16], rh], pr)
                k = f % 3
                P.op("act", lambda e, k=k, ps=ps: e.activation(rt[k][:], ps, AF.Relu), [], [rrt[k]] + pr)
                eng = "dve" if f % 2 == 0 else "pool"
                P.op(eng, lambda e, k=k, f=f: e.tensor_tensor(hid[:, f, :], rt[k][:], rt[k][:], ALU.mult), [rrt[k]], [rhid[f]])

            for g in range(ng):
                s = g % 2
                if g + 1 < ng:
                    self.load_x(g + 1, xt[1 - s][:], rxt[1 - s])
                for f in range(8):
                    step3(g, f)
                if g > 0:
                    self.ln_p2(g - 1, l, 1, zz, rz0, rz1)
                for f in range(8, 32):
                    step3(g, f)
                P.op("act", lambda e, s=s: e.mul(zz[:, 0], xt[s][:], ALPHA), [rxt[s]], [rz0])
                for n in range(8):
                    ps, pr = self.ps_half()
                    for f in range(32):
                        P.op("pe", lambda e, f=f, n=n, ps=ps: e.matmul(ps, W2[:, f, n * 128:(n + 1) * 128], hid[:, f, :],
                                                                start=(f == 0), stop=(f == 31)), [rW2[f // 4], rhid[f]], pr)
                    g2 = self.modv(g, l, 5, n)
                    P.op("dve", lambda e, n=n, ps=ps, g2=g2: e.scalar_tensor_tensor(zz[:, 0, n, :], ps, g2, zz[:, 0, n, :], ALU.mult, ALU.add),
                         [rz0, self.rmod], [rz0] + pr)
                self.ln_p1(zz, rz0, rz1)
                if g + 1 < ng:
                    self.modulate(g + 1, l, 1, xt[1 - s], rxt[1 - s], hT, rh)
            self.ln_p2(ng - 1, l, 1, zz, rz0, rz1)

    def mixer(self, l):
        m = l % 4
        if m == 2:
            self.ret_layer(l)
            self.outproj(l, "ret")
        else:
            kind = ("swa", "na", None, "gqa")[m]
            self.attn_layer(l, kind)
            self.outproj(l, kind)

    def outproj(self, l, kind):
        P, D = self.P, self.D
        with self.phase():
            if kind == "ret":
                nj, mixd, wsrc = 16, D["mixR_d"], D["ret_wo"].rearrange("(c p) n -> p c n", p=128)
            else:
                wn = {"swa": "swa_wo", "na": "na_wo", "gqa": "gqa_wo"}[kind]
                nj, mixd, wsrc = 8, D["mixA_d"].rearrange("d (c two) t -> d c two t", two=2), D[wn].rearrange("(c p) n -> p c n", p=128)
            Wo = self.sb([128, nj, 1024], BF16, "Wo")
            rWo = [Reg() for _ in range(4)]
            q4 = nj // 4
            for q in range(4):
                self.wload(Wo[:, q * q4:(q + 1) * q4, :], wsrc[:, q * q4:(q + 1) * q4, :], rWo[q])
            xt = [self.sb([128, 8, G], F32, "xt") for _ in range(2)]
            rxt = [Reg(), Reg()]
            mx = [self.sb([128, nj, G], BF16, "mx") for _ in range(2)]
            rmx = [Reg(), Reg()]
            rmx2 = [[Reg(), Reg()], [Reg(), Reg()]]
            zzs = [self.sb([128, 2, 8, G], F32, "zz") for _ in range(2)]
            rzs = [(Reg(), Reg()) for _ in range(2)]
            self.alloc_ln()
            ng = NPG + NSG

            def loads(g):
                s = g % 2
                self.load_x(g, xt[s][:], rxt[s])
                t0 = g * G
                if kind == "ret":
                    P.dma("sp", lambda e: e.dma_start(out=mx[s][:], in_=mixd[:, :, t0:t0 + G]), [self.mixreg[g]], [rmx[s]])
                else:
                    for two in range(2):
                        P.dma("sp", lambda e, two=two: e.dma_start(out=mx[s][two * 64:(two + 1) * 64], in_=mixd[:, :, two, t0:t0 + G]), [self.mixreg[g]], [rmx2[s][two]])

            def mm(g, n):
                s = g % 2
                zz, (rz0, rz1) = zzs[s], rzs[s]
                ps, pr = self.ps_half()
                for j in range(nj):
                    P.op("pe", lambda e, j=j, n=n, ps=ps, s=s: e.matmul(ps, Wo[:, j, n * 128:(n + 1) * 128], mx[s][:, j, :],
                                                                   start=(j == 0), stop=(j == nj - 1)), [rWo[j // q4], rmx[s]] + rmx2[s], pr)
                g1 = self.modv(g, l, 2, n)
                P.op("dve", lambda e, n=n, ps=ps, g1=g1, zz=zz: e.scalar_tensor_tensor(zz[:, 0, n, :], ps, g1, zz[:, 0, n, :], ALU.mult, ALU.add),
                     [rz0, self.rmod], [rz0] + pr)
            loads(0)
            for g in range(ng):
                s = g % 2
                zz, (rz0, rz1) = zzs[s], rzs[s]
                if g + 1 < ng:
                    loads(g + 1)
                P.op("act", lambda e, s=s, zz=zz: e.mul(zz[:, 0], xt[s][:], ALPHA), [rxt[s]], [rz0])
                for n in range(4):
                    mm(g, n)
                if g > 0:
                    self.ln_p2(g - 1, l, 0, zzs[1 - s], rzs[1 - s][0], rzs[1 - s][1])
                for n in range(4, 8):
                    mm(g, n)
                self.ln_p1(zz, rz0, rz1)
            sl = (ng - 1) % 2
            self.ln_p2(ng - 1, l, 0, zzs[sl], rzs[sl][0], rzs[sl][1])

    def attn_layer(self, l, kind):
        P, D, nc = self.P, self.D, self.nc
        cfg = {"swa": dict(nk=2, vh=4, rope=True, rms=False, w="swa_wqkv_p", ck="cswak", cv="cswav", ok="o_swak", ov="o_swav", ns=16, roll=True),
               "na": dict(nk=8, vh=16, rope=False, rms=False, w="na_wqkv", ck="cnak", cv="cnav", ok="o_nak", ov="o_nav", ns=4, roll=True),
               "gqa": dict(nk=2, vh=4, rope=True, rms=True, w="gqa_wqkv_p", ck="cgqak", cv="cgqav", ok="o_gqak", ov="o_gqav", ns=16, roll=False)}[kind]
        nk, vh, ns = cfg["nk"], cfg["vh"], cfg["ns"]
        kvw = nk * 128
        nqk = 8 + nk
        self.ntrans = 4
        with self.phase():
            rc = self.rconst
            Wqk = self.sb([128, 8, nqk * 128], BF16, "Wqk")
            Wv = self.sb([128, 8, kvw], BF16, "Wv")
            rW = [Reg() for _ in range(8)]
            for kc in range(8):
                self.wload(Wqk[:, kc, :], D[cfg["w"]][kc * 128:(kc + 1) * 128, 0:nqk * 128], rW[kc])
            rWv = [Reg() for _ in range(8)]
            for kc in range(8):
                self.wload(Wv[:, kc, :], D[cfg["w"]][kc * 128:(kc + 1) * 128, nqk * 128:nqk * 128 + kvw], rWv[kc])
            Wrot, rWrot = None, Reg()
            if cfg["rope"]:
                Wrot = self.sb([128, 8, nqk * 128], BF16, "Wrot")
                src = Wqk[:].rearrange("p k (b t i) -> p k b t i", t=2, i=16)
                dst = Wrot[:].rearrange("p k (b t i) -> p k b t i", t=2, i=16)
                P.op("pool", lambda e: e.tensor_scalar(dst[:, :, :, 0, :], src[:, :, :, 1, :], -1.0, None, ALU.mult), rW, [rWrot])
                P.op("pool", lambda e: e.tensor_copy(dst[:, :, :, 1, :], src[:, :, :, 0, :]), rW + [rWrot], [rWrot])
            hoist = kind != "na"
            nxt = 3 if hoist else 1
            nht = 2 if hoist else 1
            xt = [self.sb([128, 8, G], F32, "xt") for _ in range(nxt)]
            rxt = [Reg() for _ in range(nxt)]
            hTs = [self.sb([128, 8, G], BF16, "hT") for _ in range(nht)]
            rhs_ = [Reg() for _ in range(nht)]
            QTe = [self.sb([128, 8, G], BF16, "QTe") for _ in range(2)]
            QTo = [self.sb([128, 8, G], BF16, "QTo") for _ in range(2)]
            rQ = [Reg() for _ in range(2)]
            for qi in range(2):
                P.op("pool", lambda e, qi=qi: e.memset(QTe[qi][:], 0.0), [], [rQ[qi]])
                P.op("pool", lambda e, qi=qi: e.memset(QTo[qi][:], 0.0), [], [rQ[qi]])
            KTp = self.sb([128, nk, G], BF16, "KTp")
            Vp = self.sb([128, 2, vh, 65], BF16, "Vp")
            rKp, rVp = Reg(), Reg()
            KTs = self.sb([128, nk, ns * G], BF16, "KTs")
            Vs = self.sb([128, ns * 2, vh, 65], BF16, "Vs")
            rK = [Reg() for _ in range(ns)]
            rV = [Reg() for _ in range(ns)]
            cKT = self.sb([128, nk, 512], BF16, "cKT")
            cV = self.sb([128, 4, vh, 65], BF16, "cV")
            rcK, rcV = Reg(), Reg()
            stg = self.sb([128, 4, 1024], F32, "stg")
            rstg = Reg()
            NPT = 6
            self.PT = [self.sb([128, 256 if kind == "na" else 512], BF16, "PT") for _ in range(NPT)]
            self.rPT = [Reg() for _ in range(NPT)]
            self.pt_ctr = 0
            OT = self.sb([64, 16, G], BF16, "OT")
            rOT = Reg()
            NW = 3
            NMAX = 256 if kind == "na" else 512
            dns = [self.sb([128, NMAX], F32, "dn") for _ in range(NW)]
            rdns = [Reg() for _ in range(NW)]
            for wi in range(NW):
                P.op("pool", lambda e, wi=wi: e.memset(dns[wi][:], 0.0), [], [rdns[wi]])
            bcss = [self.sb([128, NMAX], F32, "bcs") for _ in range(NW)]
            rbcss = [Reg() for _ in range(NW)]
            tmp = {n: [self.sb([128, G], F32, n) for _ in range(2)] for n in (("t1", "t2", "sq", "rstd") if cfg["rope"] else ())}
            rtmp = {n: [Reg(), Reg()] for n in tmp}
            tctr = [0]
            cs = [self.sb([128, 2, G], F32, "cs") for _ in range(3 if cfg["rope"] else 0)]
            rcs = [Reg(), Reg(), Reg()]
            P.op("pool", lambda e: e.memset(Vp[:, :, :, 64:65], 1.0), [], [rVp])
            P.op("pool", lambda e: e.memset(Vs[:, :, :, 64:65], 1.0), [], rV)
            P.op("pool", lambda e: e.memset(cV[:, :, :, 64:65], 1.0), [], [rcV])
            exps, rsink = None, Reg()
            if kind == "swa":
                exps = self.sb([128, 16], F32, "exps")
                P.dma("sp", lambda e: e.dma_start(out=exps[:], in_=D["swa_sink"].partition_broadcast(128)), [], [rsink])
                P.op("act", lambda e: e.activation(exps[:], exps[:], AF.Exp), [rsink], [rsink])
                mk = self.sb([128, 2, 512], BF16, "mk")
                rmk = Reg()
                self.wload(mk[:], D["swamask"].rearrange("m p n -> p m n"), rmk)
            if kind == "na":
                tab = self.sb([128, 14336], BF16, "tab")
                rtab = Reg()
                P.dma("sp", lambda e: e.dma_start(out=tab[:], in_=D["natab_d"][:, :]), [self.rnatab], [rtab])
            P.dma("sp", lambda e: e.dma_start(out=stg[:, :, 0:kvw], in_=D[cfg["ck"]].rearrange("(b p) f -> p b f", p=128)), [], [rstg])
            for j in range(nk):
                bank, br = self.ps_full()
                for tb in range(4):
                    P.op("pe", lambda e, j=j, tb=tb, bank=bank: e.transpose(bank[:, tb * 128:(tb + 1) * 128], stg[:, tb, j * 128:(j + 1) * 128], self.ident[:]),
                         [rstg, rc], br)
                P.op("act" if j % 2 == 0 else "dve",
                     (lambda e, j=j, bank=bank: e.copy(cKT[:, j, :], bank[:, :])) if j % 2 == 0 else (lambda e, j=j, bank=bank: e.tensor_copy(cKT[:, j, :], bank[:, :])),
                     [], [rcK] + br)
            P.dma("sp", lambda e: e.dma_start(out=stg[:, :, 0:kvw], in_=D[cfg["cv"]].rearrange("(b p) f -> p b f", p=128)), [], [rstg])
            P.op("act", lambda e: e.copy(cV[:, :, :, 0:64], stg[:, :, 0:kvw].rearrange("p b (h d) -> p b h d", d=64)), [rstg], [rcV])

            def qk_post(sample, is_k, psa, pra, psb, prb, dests, rdest, csg, rcsg, k32, rk32):
                i = tctr[0] % 2
                tctr[0] += 1
                if not cfg["rms"]:
                    if not (cfg["rope"] and sample):
                        if k32 is not None:
                            P.op("dve", lambda e: e.tensor_copy(k32, psa), [], [rk32] + pra)
                            P.op("act", lambda e: e.copy(dests[0][0], k32), [rk32], [rdest])
                        else:
                            for di, (dst, lo, hi) in enumerate(dests):
                                if (i + di) % 2 == 0:
                                    P.op("act", lambda e, dst=dst, lo=lo, hi=hi: e.copy(dst, psa[lo:hi]), [], [rdest] + pra)
                                else:
                                    P.op("dve", lambda e, dst=dst, lo=lo, hi=hi: e.tensor_copy(dst, psa[lo:hi]), [], [rdest] + pra)
                        return
                    t1, t2 = tmp["t1"][i], tmp["t2"][i]
                    r1, r2 = rtmp["t1"][i], rtmp["t2"][i]
                    P.op("dve", lambda e: e.tensor_tensor(t1[:], psa, csg[:, 0, :], ALU.mult), [rcsg], [r1] + pra)
                    P.op("dve", lambda e: e.tensor_tensor(t2[:], psb, csg[:, 1, :], ALU.mult), [rcsg], [r2] + prb)
                    for dst, lo, hi in dests:
                        P.op("pool", lambda e, dst=dst, lo=lo, hi=hi: e.tensor_tensor(dst, t1[lo:hi], t2[lo:hi], ALU.add), [r1, r2], [rdest])
                    return
                sq, rstd = tmp["sq"][i], tmp["rstd"][i]
                rsq, rrs = rtmp["sq"][i], rtmp["rstd"][i]
                gc = self.gqn[:, 2:3] if is_k else self.gqn[:, 0:1]
                gcp = self.gqn[:, 3:4] if is_k else self.gqn[:, 1:2]
                P.op("act", lambda e: e.activation(sq[:], psa, AF.Square), [], [rsq] + pra)
                pss, prs = self.ps_half()
                P.op("pe", lambda e: e.matmul(pss, self.blk32[:, :], sq[:], start=True, stop=True), [rsq, rc], prs)
                P.op("dve", lambda e: e.tensor_scalar(rstd[:], pss, 1.0 / 64.0, RMS_EPS, ALU.mult, ALU.add), [], [rrs] + prs)
                P.op("dve", lambda e: e.reciprocal(rstd[:], rstd[:]), [rrs], [rrs])
                P.op("act", lambda e: e.activation(rstd[:], rstd[:], AF.Sqrt), [rrs], [rrs])
                if not sample:
                    if k32 is not None:
                        P.op("dve", lambda e: e.scalar_tensor_tensor(k32, psa, gc, rstd[:], ALU.mult, ALU.mult), [rrs, self.rmod], [rk32] + pra)
                        P.op("act", lambda e: e.copy(dests[0][0], k32), [rk32], [rdest])
                    else:
                        for dst, lo, hi in dests:
                            P.op("dve", lambda e, dst=dst, lo=lo, hi=hi: e.scalar_tensor_tensor(dst, psa[lo:hi], gc[lo:hi], rstd[lo:hi], ALU.mult, ALU.mult), [rrs, self.rmod], [rdest] + pra)
                    return
                t1, t2 = tmp["t1"][i], tmp["t2"][i]
                r1, r2 = rtmp["t1"][i], rtmp["t2"][i]
                P.op("dve", lambda e: e.scalar_tensor_tensor(t1[:], psa, gc, csg[:, 0, :], ALU.mult, ALU.mult), [rcsg, self.rmod], [r1] + pra)
                P.op("dve", lambda e: e.scalar_tensor_tensor(t2[:], psb, gcp, csg[:, 1, :], ALU.mult, ALU.mult), [rcsg, self.rmod], [r2] + prb)
                P.op("pool", lambda e: e.tensor_tensor(t1[:], t1[:], t2[:], ALU.add), [r1, r2], [r1])
                for dst, lo, hi in dests:
                    P.op("pool", lambda e, dst=dst, lo=lo, hi=hi: e.tensor_tensor(dst, t1[lo:hi], rstd[lo:hi], ALU.mult), [r1, rrs], [rdest])

            k32T = self.sb([128, nk, G], F32, "k32T")
            rk32 = Reg()
            ktok = stg[:, 2:4, 0:kvw]
            rktok = rstg
            v32 = stg[:, 0:2, 0:kvw]
            rv32 = rstg
            xs_ctr = [0]

            plist = [(g, True, True) for g in range(NPG)]
            if cfg["roll"]:
                plist += [(NPG + gi, True, True) for gi in range(NSG)]
            else:
                plist += [(NPG + gi, False, True) for gi in range(NSG)] + [(NPG + gi, True, False) for gi in range(NSG)]
            pn = [0]

            def prefetch(n):
                g = plist[n][0]
                self.load_x(g, xt[n % nxt][:], rxt[n % nxt])
                if cfg["rope"] and g >= NPG:
                    gi = g - NPG
                    P.dma("sp", lambda e: e.dma_start(out=cs[n % 3][:], in_=D["rope64"][:, :, gi * G:(gi + 1) * G].rearrange("c p t -> p c t")), [], [rcs[n % 3]])

            def do_mod(n):
                g = plist[n][0]
                self.modulate(g, l, 0, xt[n % nxt], rxt[n % nxt], hTs[n % nht], rhs_[n % nht])
            if hoist:
                prefetch(0)
                prefetch(1)
                do_mod(0)

            def proj(g, do_q, do_kv):
                n = pn[0]
                pn[0] += 1
                assert plist[n] == (g, do_q, do_kv), (n, plist[n], g, do_q, do_kv)
                sample = g >= NPG
                gi = g - NPG
                if hoist:
                    if n + 2 < len(plist):
                        prefetch(n + 2)
                    if n + 1 < len(plist):
                        do_mod(n + 1)
                else:
                    prefetch(n)
                    do_mod(n)
                hT, rh = hTs[n % nht], rhs_[n % nht]
                roped = cfg["rope"] and sample
                csg, rcsg = None, None
                if roped:
                    csg, rcsg = cs[n % 3], rcs[n % 3]
                js = (list(range(8)) if do_q else []) + (list(range(8, 8 + nk)) if do_kv else [])
                qs = g % 2
                for j in js:
                    psa, pra = self.ps_half()
                    for kc in range(8):
                        P.op("pe", lambda e, j=j, kc=kc, psa=psa: e.matmul(psa, Wqk[:, kc, j * 128:(j + 1) * 128], hT[:, kc, :], start=(kc == 0), stop=(kc == 7)),
                             [rW[kc], rh], pra)
                    psb, prb = None, None
                    if roped:
                        psb, prb = self.ps_half()
                        for kc in range(8):
                            P.op("pe", lambda e, j=j, kc=kc, psb=psb: e.matmul(psb, Wrot[:, kc, j * 128:(j + 1) * 128], hT[:, kc, :], start=(kc == 0), stop=(kc == 7)),
                                 [rWrot, rh], prb)
                    if j < 8:
                        qk_post(sample, False, psa, pra, psb, prb, [(QTe[qs][0:64, j, :], 0, 64), (QTo[qs][64:128, j, :], 64, 128)], rQ[qs], csg, rcsg, None, None)
                    elif sample:
                        sl = gi % ns
                        qk_post(True, True, psa, pra, psb, prb, [(KTs[:, j - 8, sl * G:(sl + 1) * G], 0, 128)], rK[sl], csg, rcsg, None, None)
                    else:
                        qk_post(False, True, psa, pra, psb, prb, [(KTp[:, j - 8, :], 0, 128)], rKp, csg, rcsg, k32T[:, j - 8, :], rk32)
                if not do_kv:
                    return
                for b in range(2):
                    for cb in range((kvw + 511) // 512):
                        w = min(512, kvw - cb * 512)
                        bank, br = self.ps_full()
                        for kc in range(8):
                            P.op("pe", lambda e, b=b, cb=cb, w=w, kc=kc, bank=bank: e.matmul(bank[:, 0:w], hT[:, kc, b * 128:(b + 1) * 128], Wv[:, kc, cb * 512:cb * 512 + w],
                                                                                     start=(kc == 0), stop=(kc == 7)), [rWv[kc], rh], br)
                        nh = w // 64
                        h0 = cb * 8
                        if sample:
                            sl = gi % ns
                            P.op("act", lambda e, b=b, sl=sl, h0=h0, nh=nh, w=w, bank=bank: e.copy(Vs[:, sl * 2 + b, h0:h0 + nh, 0:64], bank[:, 0:w].rearrange("p (h d) -> p h d", d=64)),
                                 [], [rV[sl]] + br)
                        else:
                            P.op("act", lambda e, b=b, h0=h0, nh=nh, w=w, bank=bank: e.copy(Vp[:, b, h0:h0 + nh, 0:64], bank[:, 0:w].rearrange("p (h d) -> p h d", d=64)),
                                 [], [rVp] + br)
                            P.op("dve", lambda e, b=b, cb=cb, w=w, bank=bank: e.tensor_copy(v32[:, b, cb * 512:cb * 512 + w], bank[:, 0:w]), [], [rv32] + br)
                if not sample:
                    self.fin.append(P.dma("sp", lambda e: e.dma_start(out=D[cfg["ov"]][g * G:(g + 1) * G, :].rearrange("(b p) f -> p b f", p=128), in_=v32), [rv32], []))
                    for b in range(2):
                        for j4 in range((nk + 3) // 4):
                            nj = min(4, nk - j4 * 4)
                            bank, br = self.ps_full()
                            for jj in range(nj):
                                j = j4 * 4 + jj
                                P.op("pe", lambda e, b=b, j=j, jj=jj, bank=bank: e.transpose(bank[:, jj * 128:(jj + 1) * 128], k32T[:, j, b * 128:(b + 1) * 128], self.ident[:]),
                                     [rk32, rc], br)
                            P.op("dve", lambda e, b=b, j4=j4, nj=nj, bank=bank: e.tensor_copy(ktok[:, b, j4 * 512:j4 * 512 + nj * 128], bank[:, 0:nj * 128]), [], [rktok] + br)
                    self.fin.append(P.dma("sp", lambda e: e.dma_start(out=D[cfg["ok"]][g * G:(g + 1) * G, :].rearrange("(b p) f -> p b f", p=128), in_=ktok), [rktok], []))

            acc_ctr = [0]

            def core(ai, qap, rq, N, a3, chunks, sinkap, dest):
                acc, racc = self.banks[4 + ai], self.bregs[4 + ai]
                dn, rdn, bcs, rbcs = dns[ai], rdns[ai], bcss[ai], rbcss[ai]
                n = len(chunks)
                pts = []
                for i in range(n + 1):
                    if i < n:
                        ch = chunks[i]
                        bank, br = self.ps_full()
                        pb, kp = ch["pb"], ch["kp"]
                        ni = ch.get("n", N)
                        qa = ch.get("q", qap)
                        sv = bank[pb:pb + kp, 0:ni]
                        sv3 = sv.rearrange("p (a t) -> p a t", a=a3) if a3 > 1 else sv
                        hasb = ch.get("bias") is not None
                        P.op("pe", lambda e, ch=ch, sv3=sv3, hasb=hasb, qa=qa: e.matmul(sv3, ch["kt"], qa, start=True, stop=not hasb), [ch["rk"], rq], br)
                        if hasb:
                            P.op("pe", lambda e, ch=ch, sv=sv: e.matmul(sv, ch["bl"], ch["bias"], start=False, stop=True), [ch["rb"], rc], br)
                        k = self.pt_ctr % len(self.PT)
                        self.pt_ctr += 1
                        pt, rpt = self.PT[k], self.rPT[k]
                        P.op("act", lambda e, pt=pt, pb=pb, kp=kp, sv=sv, ni=ni: e.activation(pt[pb:pb + kp, 0:ni], sv, AF.Exp, scale=0.125), [], [rpt] + br)
                        if ch.get("zero") is not None:
                            lo, hi = ch["zero"]
                            P.op("pool", lambda e, pt=pt, lo=lo, hi=hi, ni=ni: e.memset(pt[lo:hi, 0:ni], 0.0), [], [rpt])
                        pts.append((pt, rpt))
                    if i >= 1:
                        ch = chunks[i - 1]
                        pt, rpt = pts[i - 1]
                        pb, kp = ch["pb"], ch["kp"]
                        ni = ch.get("n", N)
                        c0 = ch.get("c0", 0)
                        P.op("pe", lambda e, ch=ch, pt=pt, pb=pb, kp=kp, i=i, ni=ni, c0=c0: e.matmul(acc[0:65, c0:c0 + ni], ch["v"], pt[pb:pb + kp, 0:ni], start=(i == 1), stop=(i == n)),
                             [ch["rv"], rpt], racc)
                    yield
                v3 = (lambda ap: ap.rearrange("p (a t) -> p a t", a=a3)) if a3 > 1 else (lambda ap: ap)
                if sinkap is not None:
                    P.op("dve", lambda e: e.tensor_tensor(v3(dn[64:65, 0:N]), v3(acc[64:65, 0:N]), sinkap, ALU.add), [rsink], [rdn] + racc)
                else:
                    P.op("dve", lambda e: e.tensor_copy(dn[64:65, 0:N], acc[64:65, 0:N]), [], [rdn] + racc)
                P.op("dve", lambda e: e.reciprocal(dn[64:65, 0:N], dn[64:65, 0:N]), [rdn], [rdn])
                yield
                yield
                bcb, rbc = self.banks[7], self.bregs[7]
                P.op("pe", lambda e: e.matmul(bcb[:, 0:N], self.sel64[:, :], dn[:, 0:N], start=True, stop=True), [rdn, rc], rbc)
                P.op("act", lambda e: e.copy(bcs[0:64, 0:N], bcb[0:64, 0:N]), [], [rbcs] + rbc)
                yield
                P.op("dve", lambda e: e.tensor_tensor(dest, v3(acc[0:64, 0:N]), v3(bcs[0:64, 0:N]), ALU.mult), [rbcs], [rOT] + racc)

            def run_units(gens):
                active = {}
                pending = list(gens)
                while active or pending:
                    for slot in range(NW):
                        if slot not in active and pending:
                            active[slot] = pending.pop(0)(slot)
                    for slot in list(active):
                        try:
                            next(active[slot])
                        except StopIteration:
                            del active[slot]

            def attend(g):
                sample = g >= NPG
                gi = g - NPG
                qs = g % 2
                rq = rQ[qs]
                units = []
                if kind in ("swa", "gqa"):
                    for kvh in range(4):
                        b64 = (kvh % 2) * 64
                        c0 = 4 * (kvh // 2)
                        for qb in range(2):
                            qap = (QTe if kvh % 2 == 0 else QTo)[qs][:, c0:c0 + 4, qb * 128:(qb + 1) * 128]
                            chunks = []
                            if not sample:
                                for kb in range(2):
                                    chunks.append(dict(kt=KTp[:, kvh // 2, kb * 128:(kb + 1) * 128], rk=rKp, v=Vp[:, kb, kvh, :], rv=rVp, pb=0, kp=128))
                            else:
                                i = 2 * gi + qb
                                blks = [i - 1, i, i + 1] if kind == "swa" else list(range(32))
                                for bi in blks:
                                    if bi < 0 or bi > 31:
                                        continue
                                    ch = dict(kt=KTs[:, kvh // 2, bi * 128:(bi + 1) * 128], rk=rK[bi // 2], v=Vs[:, bi, kvh, :], rv=rV[bi // 2], pb=0, kp=128)
                                    if kind == "swa" and bi != i:
                                        ch["bias"] = mk[:, 0 if bi < i else 1, :]
                                        ch["bl"] = self.identb[:, :]
                                        ch["rb"] = rmk
                                    chunks.append(ch)
                                for tb in range(4):
                                    chunks.append(dict(kt=cKT[:, kvh // 2, tb * 128:(tb + 1) * 128], rk=rcK, v=cV[:, tb, kvh, :], rv=rcV, pb=0, kp=128))
                            sinkap = None
                            if kind == "swa":
                                sinkap = exps[64:65, 4 * kvh:4 * kvh + 4][:, :, None].to_broadcast([1, 4, 128])
                            units.append(lambda ai, qap=qap, chunks=chunks, sinkap=sinkap, kvh=kvh, qb=qb: core(ai, qap, rq, 512, 4, chunks, sinkap, OT[0:64, 4 * kvh:4 * kvh + 4, qb * 128:(qb + 1) * 128]))
                else:
                    for h in range(16):
                        b64 = (h % 2) * 64
                        if not sample:
                            qap = (QTe if h % 2 == 0 else QTo)[qs][:, h // 2, :]
                            chunks = [dict(kt=KTp[:, h // 2, kb * 128:(kb + 1) * 128], rk=rKp, v=Vp[:, kb, h, :], rv=rVp, pb=0, kp=128) for kb in range(2)]
                            units.append(lambda ai, qap=qap, chunks=chunks, h=h: core(ai, qap, rq, 256, 1, chunks, None, OT[0:64, h, :]))
                        else:
                            qsel = (QTe if h % 2 == 0 else QTo)[qs]
                            chunks = []
                            for tb in range(4):
                                chunks.append(dict(kt=cKT[:, h // 2, tb * 128:(tb + 1) * 128], rk=rcK, v=cV[:, tb, h, :], rv=rcV, pb=0, kp=128,
                                                   q=qsel[:, h // 2, :], c0=0, n=256))
                            for rl in range(4):
                                r = 4 * gi + rl
                                r0 = min(max(r - 4, 0), 56)
                                for m in range(r0 // 2, (r0 + 7) // 2 + 1):
                                    a_in = r0 <= 2 * m <= r0 + 7
                                    b_in = r0 <= 2 * m + 1 <= r0 + 7
                                    ee = 2 * m - r + 7
                                    assert 0 <= ee <= 13, (r, m, ee)
                                    sl = (m // 2) % ns
                                    lb = m % 2
                                    ch = dict(kt=KTs[:, h // 2, sl * G + lb * 128:sl * G + lb * 128 + 128], rk=rK[sl],
                                              v=Vs[:, sl * 2 + lb, h, :], rv=rV[sl], pb=0, kp=128,
                                              bias=tab[:, (h * 14 + ee) * 64:(h * 14 + ee + 1) * 64], bl=self.identb[:, :], rb=rtab,
                                              q=qsel[:, h // 2, rl * 64:(rl + 1) * 64], c0=rl * 64, n=64)
                                    if not a_in:
                                        ch["zero"] = (0, 64)
                                    if not b_in:
                                        ch["zero"] = (64, 128)
                                    chunks.append(ch)
                            units.append(lambda ai, chunks=chunks, h=h: core(ai, None, rq, 256, 1, chunks, None, OT[0:64, h, :]))
                run_units(units)
                t0 = g * G
                P.dma("sp", lambda e: e.dma_start(out=D["mixA_d"][:, :, t0:t0 + G], in_=OT[:]), [rOT], [self.mixreg[g]])

            for g in range(NPG):
                proj(g, True, True)
                attend(g)
            if cfg["roll"]:
                proj(NPG, True, True)
                for gi in range(NSG):
                    if gi + 1 < NSG:
                        proj(NPG + gi + 1, True, True)
                    attend(NPG + gi)
            else:
                for gi in range(NSG):
                    proj(NPG + gi, False, True)
                for gi in range(NSG):
                    proj(NPG + gi, True, False)
                    attend(NPG + gi)
        self.ntrans = 7

    def na_table(self):
        P, D = self.P, self.D
        self.rnatab = Reg("natab")
        with self.phase():
            E = self.sb([31, 8192], F32, "naE")
            rpbT = self.sb([31, 240], F32, "rpbT")
            msk = self.sb([128, 64], F32, "namsk")
            T = self.sb([128, 14336], BF16, "naT")
            rE, rT = Reg(), Reg()
            P.dma("sp", lambda e: e.dma_start(out=E[:], in_=D["naE"][:, :]), [], [rE])
            P.dma("sp", lambda e: e.dma_start(out=rpbT[:], in_=D["na_rpbT"][:, :]), [], [rE])
            P.dma("sp", lambda e: e.dma_start(out=msk[:], in_=D["namask"][:, :]), [], [rE])
            T4 = T[:].rearrange("p (h x q) -> p h x q", x=14, q=64)
            for qc in range(64):
                bank, br = self.ps_full()
                P.op("pe", lambda e, qc=qc, bank=bank: e.matmul(bank[:, 0:240], E[:, qc * 128:(qc + 1) * 128], rpbT[:, :], start=True, stop=True), [rE], br)
                for half in range(2):
                    lo = half * 64
                    P.op("dve", lambda e, qc=qc, bank=bank, lo=lo, half=half: e.tensor_scalar(
                        T4[lo:lo + 64, :, :, qc], bank[lo:lo + 64, 0:240].rearrange("p (h d) -> p h d", d=15)[:, :, half:half + 14],
                        8.0, msk[lo:lo + 64, qc:qc + 1], ALU.mult, ALU.add), [rE], [rT] + br)
            P.dma("sp", lambda e: e.dma_start(out=D["natab_d"][:, :], in_=T[:]), [rT], [self.rnatab])

    def ret_layer(self, l):
        P, D = self.P, self.D
        rc = self.rconst
        with self.phase():
            dec = self.sb([128, 8], F32, "dec")
            negl = self.sb([128, 8], F32, "negl")
            lg = self.sb([128, 8], F32, "lg")
            retc = self.sb([128, 5, 128], F32, "retc")
            retcol = self.sb([128, 2], F32, "retcol")
            intra = self.sb([128, 8, 128], F32, "intra")
            qdec = self.sb([128, 8, 128], F32, "qdec")
            kdec = self.sb([128, 8], F32, "kdec")
            cdec = self.sb([128, 8], F32, "cdec")
            rdec = Reg()
            P.dma("sp", lambda e: e.dma_start(out=dec[:], in_=D["ret_decay"].partition_broadcast(128)), [], [rdec])
            P.dma("sp", lambda e: e.dma_start(out=retc[:], in_=D["retc"].rearrange("p (a i) -> p a i", a=5)), [], [rdec])
            P.dma("sp", lambda e: e.dma_start(out=retcol[:], in_=D["retcol"][:, :]), [], [rdec])
            P.op("act", lambda e: e.activation(negl[:], dec[:], AF.Exp, scale=-1.0), [rdec], [rdec])
            P.op("act", lambda e: e.activation(negl[:], negl[:], AF.Ln, bias=1.0), [rdec], [rdec])
            P.op("act", lambda e: e.mul(lg[:], negl[:], -1.0), [rdec], [rdec])
            for dh in range(8):
                d = dh // 4
                sc = lg[:, dh:dh + 1] if d == 0 else negl[:, dh:dh + 1]
                P.op("act", lambda e, dh=dh, sc=sc: e.activation(intra[:, dh, :], retc[:, 0, :], AF.Exp, scale=sc), [rdec], [rdec])
                P.op("dve", lambda e, dh=dh, d=d: e.tensor_tensor(intra[:, dh, :], intra[:, dh, :], retc[:, 1 + d, :], ALU.mult), [rdec], [rdec])
                P.op("act", lambda e, dh=dh, d=d: e.activation(qdec[:, dh, :], retc[:, 3 + d, :], AF.Exp, scale=lg[:, dh:dh + 1]), [rdec], [rdec])
                P.op("act", lambda e, dh=dh, d=d: e.activation(kdec[:, dh:dh + 1], retcol[:, d:d + 1], AF.Exp, scale=lg[:, dh:dh + 1]), [rdec], [rdec])
                P.op("act", lambda e, dh=dh: e.activation(cdec[:, dh:dh + 1], lg[:, dh:dh + 1], AF.Exp, scale=128.0), [rdec], [rdec])
            P.op("dve", lambda e: e.tensor_scalar(kdec[:], kdec[:], 1.0 / 16.0, None, ALU.mult), [rdec], [rdec])
            Wr = [dict(q=self.sb([128, 8, 256], BF16, "Wq"), k=self.sb([128, 8, 256], BF16, "Wk"), v=self.sb([128, 8, 512], BF16, "Wv"),
                       g=self.sb([128, 8, 512], BF16, "Wg"), rot=self.sb([128, 8, 512], BF16, "Wrot"), gn=self.sb([128, 512], F32, "gn"),
                       r=Reg(), rr=Reg()) for _ in range(2)]
            xt = [self.sb([128, 8, G], F32, "xt") for _ in range(3)]
            rxt = [Reg() for _ in range(3)]
            hTs = [self.sb([128, 8, G], BF16, "hT") for _ in range(2)]
            rhs_ = [Reg(), Reg()]
            csr = [self.sb([128, 2, 2, G], F32, "csr") for _ in range(3)]
            rcsr = [Reg() for _ in range(3)]
            qT = self.sb([128, 2, G], BF16, "qT"); kT = self.sb([128, 2, G], BF16, "kT")
            qdT = [self.sb([128, 2, G], BF16, "qdT") for _ in range(2)]
            rqT, rkT, rqd = Reg(), Reg(), [Reg(), Reg()]
            t1 = [self.sb([128, G], F32, "t1") for _ in range(2)]
            t2 = [self.sb([128, G], F32, "t2") for _ in range(2)]
            rt1, rt2 = [Reg(), Reg()], [Reg(), Reg()]
            kd = [self.sb([128, 256], BF16, "kd") for _ in range(4)]
            vv = [self.sb([128, 512], BF16, "vv") for _ in range(4)]
            sg = [self.sb([128, 512], F32, "sg") for _ in range(4)]
            sm = [self.sb([128, 128], BF16, "sm") for _ in range(4)]
            Usb = [self.sb([128, 2, 512], F32, "Usb") for _ in range(4)]
            yy = [self.sb([128, 512], F32, "yy") for _ in range(2)]
            rkd, rvv, rsg, rsm, rU = ([Reg() for _ in range(4)] for _ in range(5))
            ryy = [Reg(), Reg()]
            cctr4 = [0]
            S = self.sb([128, 2, 512], F32, "S")
            Sbf = self.sb([128, 2, 512], BF16, "Sbf")
            rS = [Reg(), Reg()]
            rSb = [Reg(), Reg()]
            stats = [self.sb([128, 6], F32, "bst") for _ in range(2)]
            mv = [self.sb([128, 2], F32, "mv") for _ in range(2)]
            rmv = [Reg(), Reg()]
            cctr = [0]
            tctr = [0]
            passes = [(d, h) for d in (1, 0) for h in range(4)]

            def wl(pi):
                d, h = passes[pi]
                w = Wr[pi % 2]
                src = D["ret_w"]
                P.dma("pool", lambda e: e.dma_start(out=w["q"][:], in_=src[:, h * 256:(h + 1) * 256].rearrange("(k p) n -> p k n", p=128)), [], [w["r"]])
                P.dma("pool", lambda e: e.dma_start(out=w["k"][:], in_=src[:, 1024 + h * 256:1024 + (h + 1) * 256].rearrange("(k p) n -> p k n", p=128)), [], [w["r"]])
                P.dma("pool", lambda e: e.dma_start(out=w["v"][:], in_=src[:, 2048 + h * 512:2048 + (h + 1) * 512].rearrange("(k p) n -> p k n", p=128)), [], [w["r"]])
                c0 = 4096 + d * 2048 + h * 512
                P.dma("pool", lambda e: e.dma_start(out=w["g"][:], in_=src[:, c0:c0 + 512].rearrange("(k p) n -> p k n", p=128)), [], [w["r"]])
                P.dma("sp", lambda e: e.dma_start(out=w["gn"][:], in_=D["ret_gn"][d:d + 1, h * 512:(h + 1) * 512].partition_broadcast(128)), [], [w["r"]])
                for wi, nm in enumerate(("q", "k")):
                    sv = w[nm][:].rearrange("p k (b t i) -> p k b t i", t=2, i=64)
                    dv = w["rot"][:, :, wi * 256:(wi + 1) * 256].rearrange("p k (b t i) -> p k b t i", t=2, i=64)
                    P.op("pool", lambda e, sv=sv, dv=dv: e.tensor_scalar(dv[:, :, :, 0, :], sv[:, :, :, 1, :], -1.0, None, ALU.mult), [w["r"]], [w["rr"]])
                    P.op("pool", lambda e, sv=sv, dv=dv: e.tensor_copy(dv[:, :, :, 1, :], sv[:, :, :, 0, :]), [w["r"], w["rr"]], [w["rr"]])

            def prefetch(g, s):
                self.load_x(g, xt[s][:], rxt[s])
                if g >= NPG:
                    gi = g - NPG
                    P.dma("sp", lambda e: e.dma_start(out=csr[s][:], in_=D["rope256"][:, :, :, gi * G:(gi + 1) * G].rearrange("c d p t -> p c d t")), [], [rcsr[s]])

            def group(pi, g, first, last, n2):
                d, h = passes[pi]
                dh = d * 4 + h
                w = Wr[pi % 2]
                sample = g >= NPG
                gi = g - NPG
                s3 = n2 % 3
                s = n2 % 2
                hT, rh = hTs[s], rhs_[s]
                if n2 + 2 < len(allitems):
                    prefetch(allitems[n2 + 2][1], (n2 + 2) % 3)
                if n2 + 1 < len(allitems):
                    g1 = allitems[n2 + 1][1]
                    self.modulate(g1, l, 0, xt[(n2 + 1) % 3], rxt[(n2 + 1) % 3], hTs[1 - s], rhs_[1 - s])
                qd, rqdx = qdT[s], rqd[s]
                for wi, (nm, dst, rdst) in enumerate((("q", qT, rqT), ("k", kT, rkT))):
                    for dc in range(2):
                        psa, pra = self.ps_half()
                        for kc in range(8):
                            P.op("pe", lambda e, nm=nm, dc=dc, kc=kc, psa=psa: e.matmul(psa, w[nm][:, kc, dc * 128:(dc + 1) * 128], hT[:, kc, :], start=(kc == 0), stop=(kc == 7)),
                                 [w["r"], rh], pra)
                        if not sample:
                            P.op("act", lambda e, dst=dst, dc=dc, psa=psa: e.copy(dst[:, dc, :], psa), [], [rdst] + pra)
                            continue
                        psb, prb = self.ps_half()
                        for kc in range(8):
                            P.op("pe", lambda e, wi=wi, dc=dc, kc=kc, psb=psb: e.matmul(psb, w["rot"][:, kc, wi * 256 + dc * 128:wi * 256 + (dc + 1) * 128], hT[:, kc, :],
                                                                                start=(kc == 0), stop=(kc == 7)), [w["rr"], rh], prb)
                        i = cctr[0] % 2
                        cctr[0] += 1
                        P.op("dve", lambda e, i=i, dc=dc, psa=psa: e.tensor_tensor(t1[i][:], psa, csr[s3][:, 0, dc, :], ALU.mult), [rcsr[s3]], [rt1[i]] + pra)
                        P.op("dve", lambda e, i=i, dc=dc, psb=psb: e.tensor_tensor(t2[i][:], psb, csr[s3][:, 1, dc, :], ALU.mult), [rcsr[s3]], [rt2[i]] + prb)
                        P.op("pool", lambda e, i=i, dst=dst, dc=dc: e.tensor_tensor(dst[:, dc, :], t1[i][:], t2[i][:], ALU.add), [rt1[i], rt2[i]], [rdst])
                qd_b = qdec[:, dh, :][:, None, :].to_broadcast([128, 4, 128])
                P.op("pool", lambda e: e.tensor_tensor(qd[:].rearrange("p c (b i) -> p (c b) i", i=128), qT[:].rearrange("p c (b i) -> p (c b) i", i=128), qd_b, ALU.mult),
                     [rqT, rdec], [rqdx])
                order = (0, 1) if d == 0 else (1, 0)
                idx = []
                for ci, cb in enumerate(order):
                    i = cctr4[0] % 4
                    cctr4[0] += 1
                    idx.append(i)
                for ci, cb in enumerate(order):
                    ts = slice(cb * 128, (cb + 1) * 128)
                    i = idx[ci]
                    bank, br = self.ps_full()
                    for kc in range(8):
                        P.op("pe", lambda e, kc=kc, bank=bank, ts=ts: e.matmul(bank[:, :], hT[:, kc, ts], w["v"][:, kc, :], start=(kc == 0), stop=(kc == 7)), [w["r"], rh], br)
                    P.op("act", lambda e, i=i, bank=bank: e.copy(vv[i][:], bank[:, :]), [], [rvv[i]] + br)
                    bank, br = self.ps_full()
                    for kc in range(8):
                        P.op("pe", lambda e, kc=kc, bank=bank, ts=ts: e.matmul(bank[:, :], hT[:, kc, ts], w["g"][:, kc, :], start=(kc == 0), stop=(kc == 7)), [w["r"], rh], br)
                    P.op("act", lambda e, i=i, bank=bank: e.activation(sg[i][:], bank[:, :], AF.Silu), [], [rsg[i]] + br)
                for ci, cb in enumerate(order):
                    ts = slice(cb * 128, (cb + 1) * 128)
                    i = idx[ci]
                    bank, br = self.ps_full()
                    bbf = bank[:, 0:128].bitcast(BF16)
                    for dc in range(2):
                        P.op("pe", lambda e, dc=dc, bbf=bbf, ts=ts: e.transpose(bbf[:, dc * 128:(dc + 1) * 128], kT[:, dc, ts], self.identb[:]), [rkT, rc], br)
                    P.op("dve", lambda e, i=i, bbf=bbf: e.tensor_scalar(kd[i][:], bbf, kdec[:, dh:dh + 1], None, ALU.mult), [rdec], [rkd[i]] + br)
                    pss, prs = self.ps_half()
                    for dc in range(2):
                        P.op("pe", lambda e, dc=dc, pss=pss, ts=ts: e.matmul(pss[:, 0:128], kT[:, dc, ts], qT[:, dc, ts], start=(dc == 0), stop=(dc == 1)), [rkT, rqT], prs)
                    P.op("dve", lambda e, i=i, pss=pss: e.tensor_tensor(sm[i][:], pss[:, 0:128], intra[:, dh, :], ALU.mult), [rdec], [rsm[i]] + prs)
                for ci, cb in enumerate(order):
                    i = idx[ci]
                    if sample and last and ci == 1:
                        continue
                    for dc in range(2):
                        bank, br = self.ps_full()
                        P.op("pe", lambda e, i=i, dc=dc, bank=bank: e.matmul(bank[:, :], kd[i][:, dc * 128:(dc + 1) * 128], vv[i][:], start=True, stop=True), [rkd[i], rvv[i]], br)
                        P.op("act", lambda e, i=i, dc=dc, bank=bank: e.copy(Usb[i][:, dc, :], bank[:, :]), [], [rU[i]] + br)

                def scan():
                    if first:
                        if sample:
                            r0 = (d * 4 + h) * 256
                            P.dma("sp", lambda e: e.dma_start(out=S[:], in_=D["sret"][r0:r0 + 256, :].rearrange("(c p) n -> p c n", p=128)), [], rS)
                        else:
                            P.op("pool", lambda e: e.memset(S[:], 0.0), [], rS)
                        for dc in range(2):
                            P.op("act", lambda e, dc=dc: e.copy(Sbf[:, dc, :], S[:, dc, :]), [rS[dc]], [rSb[dc]])
                    for ci, cb in enumerate(order):
                        ts = slice(cb * 128, (cb + 1) * 128)
                        i = idx[ci]
                        j = i % 2
                        bank, br = self.ps_full()
                        P.op("pe", lambda e, i=i, bank=bank: e.matmul(bank[:, :], sm[i][:], vv[i][:], start=True, stop=False), [rsm[i], rvv[i]], br)
                        for dc in range(2):
                            P.op("pe", lambda e, dc=dc, bank=bank, ts=ts: e.matmul(bank[:, :], qd[:, dc, ts], Sbf[:, dc, :], start=False, stop=(dc == 1)), [rqdx, rSb[dc]], br)
                        if not (sample and last and ci == 1):
                            for dc in range(2):
                                P.op("dve", lambda e, i=i, dc=dc: e.scalar_tensor_tensor(S[:, dc, :], S[:, dc, :], cdec[:, dh:dh + 1], Usb[i][:, dc, :], ALU.mult, ALU.add),
                                     [rdec, rSb[dc], rU[i]], [rS[dc]])
                                P.op("act", lambda e, dc=dc: e.copy(Sbf[:, dc, :], S[:, dc, :]), [rS[dc]], [rSb[dc]])
                        P.op("dve", lambda e, j=j, bank=bank: e.bn_stats(stats[j][:], bank[:, :]), [], [rmv[j]] + br)
                        P.op("dve", lambda e, j=j: e.bn_aggr(mv[j][:], stats[j][:]), [rmv[j]], [rmv[j]])
                        P.op("dve", lambda e, j=j: e.tensor_scalar(mv[j][:, 1:2], mv[j][:, 1:2], LN_EPS, None, ALU.add), [rmv[j]], [rmv[j]])
                        P.op("dve", lambda e, j=j: e.reciprocal(mv[j][:, 1:2], mv[j][:, 1:2]), [rmv[j]], [rmv[j]])
                        P.op("act", lambda e, j=j: e.activation(mv[j][:, 1:2], mv[j][:, 1:2], AF.Sqrt), [rmv[j]], [rmv[j]])
                        P.op("dve", lambda e, j=j, bank=bank: e.tensor_scalar(yy[j][:], bank[:, :], mv[j][:, 0:1], mv[j][:, 1:2], ALU.subtract, ALU.mult), [rmv[j]], [ryy[j]] + br)
                        P.op("pool", lambda e, j=j: e.tensor_tensor(yy[j][:], yy[j][:], w["gn"][:], ALU.mult), [ryy[j], w["r"]], [ryy[j]])
                        P.op("pool", lambda e, j=j, i=i: e.tensor_tensor(yy[j][:], yy[j][:], sg[i][:], ALU.mult), [ryy[j], rsg[i]], [ryy[j]])
                        tk0 = g * G + cb * 128
                        P.dma("sp", lambda e, j=j, tk0=tk0: e.dma_start(out=D["Yd"][d, tk0:tk0 + 128, h * 512:(h + 1) * 512], in_=yy[j][:]), [ryy[j]], [self.yreg[d][g]])
                    if (not sample) and last:
                        row0 = ((g * 2 + d) * 4 + h) * 256
                        self.fin.append(P.dma("sp", lambda e: e.dma_start(out=D["o_ret"][row0:row0 + 256, :].rearrange("(c p) n -> p c n", p=128), in_=S[:]), rS, []))
                return scan

            self.yreg = [[Reg() for _ in range(NPG + NSG)] for _ in range(2)]
            wl(0)
            prev = None
            allitems = []
            for pi in range(8):
                d = passes[pi][0]
                gl = list(range(NPG, NPG + NSG))
                if d == 1:
                    gl = gl[::-1]
                for n_, (g, first, last) in enumerate([(g, True, True) for g in range(NPG)] + [(g, k == 0, k == NSG - 1) for k, g in enumerate(gl)]):
                    allitems.append((pi, g, first, last, n_))
            prefetch(allitems[0][1], 0)
            prefetch(allitems[1][1], 1)
            self.modulate(allitems[0][1], l, 0, xt[0], rxt[0], hTs[0], rhs_[0])
            for n2, (pi, g, first, last, n_) in enumerate(allitems):
                sc = group(pi, g, first, last, n2)
                if prev is not None:
                    prev()
                prev = sc
                if n_ == 0 and pi + 1 < 8:
                    wl(pi + 1)
            prev()
        with self.phase():
            ya = [self.sb([128, 2048], F32, "ya") for _ in range(2)]
            yb = [self.sb([128, 2048], F32, "yb") for _ in range(2)]
            ys = [self.sb([128, 2048], BF16, "ys") for _ in range(2)]
            rya, ryb, rys = [Reg(), Reg()], [Reg(), Reg()], [Reg(), Reg()]
            yT = [self.sb([128, 16, G], BF16, "yT") for _ in range(2)]
            ryT = [Reg(), Reg()]
            it = 0
            for g in range(NPG + NSG):
                sT = g % 2
                for b in range(2):
                    i = it % 2
                    it += 1
                    tk0 = g * G + b * 128
                    P.dma("sp", lambda e, i=i, tk0=tk0: e.dma_start(out=ya[i][:], in_=D["Yd"][0, tk0:tk0 + 128, :]), [self.yreg[0][g]], [rya[i]])
                    P.dma("pool", lambda e, i=i, tk0=tk0: e.dma_start(out=yb[i][:], in_=D["Yd"][1, tk0:tk0 + 128, :]), [self.yreg[1][g]], [ryb[i]])
                    P.op("dve", lambda e, i=i: e.tensor_tensor(ys[i][:], ya[i][:], yb[i][:], ALU.add), [rya[i], ryb[i]], [rys[i]])
                    for k2 in range(2):
                        bank, br = self.ps_full()
                        bbf = bank[:, :].bitcast(BF16)
                        for kk in range(8):
                            c = k2 * 8 + kk
                            P.op("pe", lambda e, i=i, c=c, kk=kk, bbf=bbf: e.transpose(bbf[:, kk * 128:(kk + 1) * 128], ys[i][:, c * 128:(c + 1) * 128], self.identb[:]), [rys[i], rc], br)
                        if k2 == 0:
                            P.op("act", lambda e, sT=sT, b=b, k2=k2, bbf=bbf: e.copy(yT[sT][:, k2 * 8:(k2 + 1) * 8, b * 128:(b + 1) * 128], bbf.rearrange("p (c t) -> p c t", t=128)), [], [ryT[sT]] + br)
                        else:
                            P.op("dve", lambda e, sT=sT, b=b, k2=k2, bbf=bbf: e.tensor_copy(yT[sT][:, k2 * 8:(k2 + 1) * 8, b * 128:(b + 1) * 128], bbf.rearrange("p (c t) -> p c t", t=128)), [], [ryT[sT]] + br)
                t0 = g * G
                P.dma("sp", lambda e, sT=sT, t0=t0: e.dma_start(out=D["mixR_d"][:, :, t0:t0 + G], in_=yT[sT][:]), [ryT[sT]], [self.mixreg[g]])


def build(seq=None):
    kb = KB()
    kb.declare()
    kb.prep()
    kb.na_table()
    if seq is None:
        seq = []
        for l in range(4):
            seq += [("mix", l), ("mlp", l)]
    for kind, l in seq:
        if kind == "mlp":
            kb.mlp(l)
        else:
            kb.mixer(l)
    kb.final()
    kb.P.op("pool", lambda e: e.memset(kb.ident[0:1, 0:1], 1.0), [], [])
    kb.P.flush(final_waits=kb.fin)
    kb.top.close()
    return kb


def _per_core_inputs(inp, consts, c):
    f = lambda a: np.ascontiguousarray(a, dtype=np.float32)
    m = {}
    m["xp"] = f(inp["x_prompt"][4 * c:4 * c + 4].reshape(1024, 1024))
    m["xs"] = f(inp["x_sample"][c])
    m["cswak"] = f(inp["cache_swa_k"][c, 0].reshape(512, 256)); m["cswav"] = f(inp["cache_swa_v"][c, 0].reshape(512, 256))
    m["cnak"] = f(inp["cache_na_k"][c, 0].reshape(512, 1024)); m["cnav"] = f(inp["cache_na_v"][c, 0].reshape(512, 1024))
    m["sret"] = f(inp["state_ret"][c, 0].reshape(2048, 512))
    m["cgqak"] = f(inp["cache_gqa_k"][c, 0].reshape(512, 256)); m["cgqav"] = f(inp["cache_gqa_v"][c, 0].reshape(512, 256))
    m["cT_h"] = f(np.concatenate([inp["c_ctx"].reshape(8, 128).T, inp["c"][c].reshape(8, 128).T], axis=1))
    return m


def _shared_inputs(inp, consts):
    f = lambda a: np.ascontiguousarray(a, dtype=np.float32)
    m = {}
    m["lngT_h"] = f(inp["ln_g"].reshape(64, 128).T); m["lnbT_h"] = f(inp["ln_b"].reshape(64, 128).T)
    m["modbT_h"] = f(inp["mod_b"].reshape(192, 128).T)
    p = np.arange(128) % 64
    partner = np.where((p % 32) < 16, p + 16, p - 16)
    qn, kn = inp["gqa_q_norm"][0], inp["gqa_k_norm"][0]
    m["gqn"] = f(np.stack([qn[p], qn[partner], kn[p], kn[partner]], axis=1))
    m["mod_w"] = f(inp["mod_w"]); m["mlp_w1"] = f(inp["mlp_w1"]); m["mlp_w2"] = f(inp["mlp_w2"])
    order = []
    for cch in range(8):
        for b in range(2):
            order.append(4 * (2 * (cch // 4) + b) + cch % 4)
    cols = np.concatenate([np.arange(h * 64, (h + 1) * 64) for h in order] + [np.arange(1024, 1536)])
    m["swa_wqkv_p"] = f(inp["swa_wqkv"][0][:, cols]); m["gqa_wqkv_p"] = f(inp["gqa_wqkv"][0][:, cols])
    m["swa_wo"] = f(inp["swa_wo"][0]); m["gqa_wo"] = f(inp["gqa_wo"][0]); m["swa_sink"] = f(inp["swa_sink"].reshape(1, 16))
    m["na_wqkv"] = f(inp["na_wqkv"][0]); m["na_wo"] = f(inp["na_wo"][0])
    m["na_rpbT"] = f(inp["na_rpb"][0].reshape(240, 31).T)
    m["ret_w"] = f(inp["ret_wqkvg"][0]); m["ret_wo"] = f(inp["ret_wo"][0])
    m["ret_decay"] = f(inp["ret_decay"].reshape(1, 8)); m["ret_gn"] = f(inp["ret_gn_g"][0])
    for k, v in consts.items():
        m[k] = f(v)
    return m


_CACHE = {}


def kernel(**inp):
    inp = {k: np.asarray(v) for k, v in inp.items()}
    seq = None
    env = os.environ.get("KSEQ")
    if env is not None:
        seq = [(s[:3], int(s[3:])) for s in env.split(",") if s]
    key = str(seq)
    if key not in _CACHE:
        _CACHE[key] = build(seq)
    kb = _CACHE[key]
    consts = _host_consts()
    shared = _shared_inputs(inp, consts)
    in_maps = []
    ncores = int(os.environ.get("KCORES", "8"))
    for c in range(ncores):
        m = dict(shared)
        m.update(_per_core_inputs(inp, consts, c))
        in_maps.append({k: m[k] for k in kb.in_names})
    res = run_bass_kernel_spmd(kb.nc, in_maps, core_ids=list(range(ncores)))
    R = list(res.results)
    while len(R) < 8:
        R.append(R[0])
    cat = lambda n: np.concatenate([np.asarray(R[c][n]) for c in range(8)], axis=0)
    y_p = cat("y_p").reshape(32, 256, 1024)
    y_s = cat("y_s").reshape(8, 4096, 1024)
    swak = cat("o_swak").reshape(32, 1, 256, 4, 64); swav = cat("o_swav").reshape(32, 1, 256, 4, 64)
    nak = cat("o_nak").reshape(32, 1, 256, 16, 64); nav = cat("o_nav").reshape(32, 1, 256, 16, 64)
    ret = cat("o_ret").reshape(32, 1, 2, 4, 256, 512)
    gqak = cat("o_gqak").reshape(32, 1, 256, 4, 64); gqav = cat("o_gqav").reshape(32, 1, 256, 4, 64)
    return tuple(np.ascontiguousarray(a, dtype=np.float32) for a in (y_p, y_s, swak, swav, nak, nav, ret, gqak, gqav))
```
